# Optimizing a Trainium2 kernel written in Bass

```python
import jax, jax.numpy as jnp
from jax import lax
import numpy as np

D_MODEL = 1024
BATCH = 8
SEQ = 2048
DEPTH = 1
DEC_BATCH = 128
DEC_SEQ = 4
PAST_LEN = 16384
PAGE_SIZE = 128

D_CONV = D_MODEL
CONV_W = 31
D_POOL = D_MODEL
POOL_WINDOWS = (2, 4, 8, 16)
N_POOL_GROUPS = len(POOL_WINDOWS)
POOL_GROUP = D_POOL // N_POOL_GROUPS
MAX_POOL = max(POOL_WINDOWS)
D_FF = 4 * D_MODEL
N_MOD = 6
D_IN = 2 * D_CONV + D_POOL + 2 * D_MODEL
EPS = 1e-6

kernel_name = "conv_pool_gated_hybrid_step"


def _rms(x):
    xf = x.astype(jnp.float32)
    return xf * lax.rsqrt(jnp.mean(xf * xf, axis=-1, keepdims=True) + EPS)


def _layer(x, c, conv_hist, pool_hist, pos0, w_ada, b_ada, w_in, conv_w, conv_b, ln_g, ln_b,
           w_conv_out, pool_mix, pool_scale, w_pool_out, w_out, w_ff1, w_ff2):
    dt = x.dtype
    n, t, _ = x.shape
    mod = (jax.nn.silu(c) @ w_ada + b_ada)[:, None, :]
    sh1, sc1, g1, sh2, sc2, g2 = jnp.split(mod, N_MOD, axis=-1)

    h = (_rms(x) * (1.0 + sc1) + sh1).astype(dt)
    z = h @ w_in
    a_val, a_gate, u, ga, gb = jnp.split(
        z, np.cumsum([D_CONV, D_CONV, D_POOL, D_MODEL]).tolist(), axis=-1)

    a = a_val * jax.nn.sigmoid(a_gate)
    a_ext = jnp.concatenate([conv_hist.astype(dt), a], axis=1)
    conv = lax.conv_general_dilated(
        a_ext, conv_w[:, None, :].astype(dt), window_strides=(1,), padding='VALID',
        dimension_numbers=('NWC', 'WIO', 'NWC'), feature_group_count=D_CONV) + conv_b
    cf = conv.astype(jnp.float32)
    mu = jnp.mean(cf, axis=-1, keepdims=True)
    var = jnp.mean(jnp.square(cf - mu), axis=-1, keepdims=True)
    cn = ((cf - mu) * lax.rsqrt(var + EPS) * ln_g + ln_b).astype(dt)
    o_a = jax.nn.silu(cn) @ w_conv_out

    p_hist = MAX_POOL - 1
    u_ext = jnp.concatenate([pool_hist.astype(dt), u], axis=1)
    s = jnp.cumsum(u_ext.astype(jnp.float32), axis=1)
    s = jnp.concatenate([jnp.zeros((n, 1, D_POOL), jnp.float32), s], axis=1)
    pos = pos0 + jnp.arange(t)
    outs = []
    for gi, w in enumerate(POOL_WINDOWS):
        sl = slice(gi * POOL_GROUP, (gi + 1) * POOL_GROUP)
        win_sum = s[:, p_hist + 1:p_hist + 1 + t, sl] - s[:, p_hist + 1 - w:p_hist + 1 - w + t, sl]
        cnt = jnp.minimum(pos + 1, w).astype(jnp.float32)[None, :, None]
        outs.append(win_sum / cnt - u[:, :, sl].astype(jnp.float32))
    pooled = jnp.concatenate(outs, axis=-1).astype(dt)
    pg = pooled.reshape(n, t, N_POOL_GROUPS, POOL_GROUP)
    pm = jnp.einsum('ntgc,gcd->ntgd', pg, pool_mix).reshape(n, t, D_POOL) * pool_scale
    o_b = pm @ w_pool_out

    m = jax.nn.sigmoid(ga) * o_a + jax.nn.sigmoid(gb) * o_b
    x = x + (g1 * (m @ w_out)).astype(dt)

    h2 = (_rms(x) * (1.0 + sc2) + sh2).astype(dt)
    f = jnp.square(jax.nn.relu(h2 @ w_ff1)) @ w_ff2
    x = x + (g2 * f).astype(dt)
    return x, a_ext[:, -(CONV_W - 1):], u_ext[:, -(MAX_POOL - 1):]


def setup_inputs(seed: int = 0) -> dict:
    key = jax.random.key(seed)
    ks = jax.random.split(key, 24)
    f32 = jnp.float32
    nrm = lambda k, shape, s: jax.random.normal(k, shape, f32) * s
    return {
        "x_prompt": nrm(ks[0], (BATCH, SEQ, D_MODEL), 1.0),
        "x_sample": nrm(ks[1], (DEC_BATCH, DEC_SEQ, D_MODEL), 1.0),
        "state_conv": nrm(ks[2], (DEPTH, DEC_BATCH, CONV_W - 1, D_CONV), 1.0),
        "state_pool": nrm(ks[3], (DEPTH, DEC_BATCH, MAX_POOL - 1, D_POOL), 1.0),
        "c_prompt": nrm(ks[4], (BATCH, D_MODEL), 1.0),
        "c_sample": nrm(ks[5], (DEC_BATCH, D_MODEL), 1.0),
        "w_ada": nrm(ks[6], (DEPTH, D_MODEL, N_MOD * D_MODEL), 0.5 * D_MODEL ** -0.5),
        "b_ada": nrm(ks[7], (DEPTH, N_MOD * D_MODEL), 0.01),
        "w_in": nrm(ks[8], (DEPTH, D_MODEL, D_IN), D_MODEL ** -0.5),
        "conv_w": nrm(ks[9], (DEPTH, CONV_W, D_CONV), CONV_W ** -0.5),
        "conv_b": nrm(ks[10], (DEPTH, D_CONV), 0.01),
        "ln_g": 1.0 + nrm(ks[11], (DEPTH, D_CONV), 0.05),
        "ln_b": nrm(ks[12], (DEPTH, D_CONV), 0.01),
        "w_conv_out": nrm(ks[13], (DEPTH, D_CONV, D_MODEL), D_CONV ** -0.5),
        "pool_mix": nrm(ks[14], (DEPTH, N_POOL_GROUPS, POOL_GROUP, POOL_GROUP), POOL_GROUP ** -0.5),
        "pool_scale": 1.0 + nrm(ks[15], (DEPTH, D_POOL), 0.05),
        "w_pool_out": nrm(ks[16], (DEPTH, D_POOL, D_MODEL), D_POOL ** -0.5),
        "w_out": nrm(ks[17], (DEPTH, D_MODEL, D_MODEL), D_MODEL ** -0.5),
        "w_ff1": nrm(ks[18], (DEPTH, D_MODEL, D_FF), D_MODEL ** -0.5),
        "w_ff2": nrm(ks[19], (DEPTH, D_FF, D_MODEL), D_FF ** -0.5),
        "final_g": 1.0 + nrm(ks[20], (D_MODEL,), 0.05),
    }


def reference(x_prompt, x_sample, state_conv, state_pool, c_prompt, c_sample, w_ada, b_ada, w_in,
              conv_w, conv_b, ln_g, ln_b, w_conv_out, pool_mix, pool_scale, w_pool_out, w_out,
              w_ff1, w_ff2, final_g):
    dt = x_prompt.dtype
    xp, xs = x_prompt, x_sample
    zc = jnp.zeros((BATCH, CONV_W - 1, D_CONV), dt)
    zp = jnp.zeros((BATCH, MAX_POOL - 1, D_POOL), dt)
    cp_l, pp_l, cs_l, ps_l = [], [], [], []
    for l in range(DEPTH):
        w = (w_ada[l], b_ada[l], w_in[l], conv_w[l], conv_b[l], ln_g[l], ln_b[l], w_conv_out[l],
             pool_mix[l], pool_scale[l], w_pool_out[l], w_out[l], w_ff1[l], w_ff2[l])
        xp, cp, pp = _layer(xp, c_prompt, zc, zp, 0, *w)
        xs, cs, ps = _layer(xs, c_sample, state_conv[l], state_pool[l], PAST_LEN, *w)
        cp_l.append(cp); pp_l.append(pp); cs_l.append(cs); ps_l.append(ps)
    y_prompt = (_rms(xp) * final_g).astype(dt)
    y_sample = (_rms(xs) * final_g).astype(dt)
    new_conv_prompt = jnp.stack(cp_l)
    new_pool_prompt = jnp.stack(pp_l)
    new_conv_sample = jnp.stack(cs_l)
    new_pool_sample = jnp.stack(ps_l)
    return (y_prompt, y_sample, new_conv_prompt, new_pool_prompt, new_conv_sample, new_pool_sample)
```

```python
import numpy as np
from contextlib import ExitStack
import concourse.bass as bass
import concourse.mybir as mybir
from concourse.bass_utils import run_bass_kernel_spmd

F32 = mybir.dt.float32
BF16 = mybir.dt.bfloat16
AF = mybir.ActivationFunctionType
ALU = mybir.AluOpType

D = 1024
NCH = 8
SEQ = 2048
NS = 16
LS = 4
HC = 30
HP = 15
CW = 31
EPS = 1e-6
NCORES = 8

T_PE = 12
RING = 5

ENGS = ("pe", "act", "dve", "pool", "sp")
EPOCH = 16000
SAFE_DIST = 4
STRICT_SAME_ENGINE = True


class Buf:
    __slots__ = ("name", "last_w", "readers")

    def __init__(self, name):
        self.name = name
        self.last_w = None
        self.readers = []


class Ins:
    __slots__ = ("eng", "fn", "deps", "sig", "cnt", "dma_key", "dma_val", "is_dma", "idx")


class Sched:
    def __init__(self, nc):
        self.nc = nc
        self.q = {e: [] for e in ENGS}
        self.dma_cnt = {}
        self.all_dma_out = []

    def op(self, eng, fn, reads=(), writes=(), dma_key=None, out_dma=False):
        ins = Ins()
        ins.eng = eng
        ins.fn = fn
        ins.sig = False
        ins.cnt = 0
        ins.is_dma = dma_key is not None
        ins.dma_key = dma_key
        ins.dma_val = 0
        ins.idx = len(self.q[eng])
        if ins.is_dma:
            v = self.dma_cnt.get(dma_key, 0) + 16
            self.dma_cnt[dma_key] = v
            ins.dma_val = v
        deps = {}
        for b in reads:
            if b.last_w is not None:
                deps[id(b.last_w)] = (b.last_w, True)
        for b in writes:
            if b.last_w is not None and id(b.last_w) not in deps:
                deps[id(b.last_w)] = (b.last_w, False)
            for r in b.readers:
                if id(r) not in deps:
                    deps[id(r)] = (r, False)
        final = []
        for d, raw in deps.values():
            if (not d.is_dma) and (not ins.is_dma) and d.eng == eng:
                if eng == "pe":
                    continue
                if not STRICT_SAME_ENGINE:
                    if not raw or ins.idx - d.idx >= SAFE_DIST:
                        continue
            final.append(d)
        ins.deps = final
        for b in reads:
            b.readers.append(ins)
        for b in writes:
            b.last_w = ins
            b.readers = []
        self.q[eng].append(ins)
        if out_dma:
            self.all_dma_out.append(ins)
        return ins

    def emit(self, stack):
        nc = self.nc
        fin = Ins()
        fin.eng = "sp"; fin.fn = None; fin.sig = False; fin.cnt = 0
        fin.is_dma = False; fin.dma_key = None; fin.dma_val = 0; fin.idx = len(self.q["sp"])
        fin.deps = list(self.all_dma_out)
        self.q["sp"].append(fin)
        for e in ENGS:
            for ins in self.q[e]:
                for d in ins.deps:
                    d.sig = True
        nsig = {}
        for e in ENGS:
            c = 0
            for ins in self.q[e]:
                if ins.sig and not ins.is_dma:
                    c += 1
                    ins.cnt = c
            nsig[e] = c
        esems = {}
        for e in ENGS:
            n = (nsig[e] + EPOCH - 1) // EPOCH
            esems[e] = [stack.enter_context(nc.semaphore(f"s_{e}_{i}")) for i in range(max(n, 1))]
        dsems = {}
        for k in self.dma_cnt:
            dsems[k] = stack.enter_context(nc.semaphore("d_" + "_".join(str(x) for x in k)))

        def signal_of(d):
            if d.is_dma:
                return ("d", d.dma_key), dsems[d.dma_key], d.dma_val
            ep = (d.cnt - 1) // EPOCH
            return ("e", d.eng, ep), esems[d.eng][ep], d.cnt - ep * EPOCH

        block = stack.enter_context(nc.Block())
        reg = {"pe": block.tensor, "act": block.scalar, "dve": block.vector,
               "pool": block.gpsimd, "sp": block.sync}
        for e in ENGS:
            qe = self.q[e]

            def body(eng, qe=qe, e=e):
                waited = {}
                maxep = {}
                for ins in qe:
                    for d in ins.deps:
                        key, sem, val = signal_of(d)
                        if key[0] == "e":
                            if maxep.get(key[1], -1) > key[2]:
                                continue
                        if waited.get(key, 0) < val:
                            eng.wait_ge(sem, val)
                            waited[key] = val
                            if key[0] == "e":
                                maxep[key[1]] = max(maxep.get(key[1], -1), key[2])
                    if ins.fn is None:
                        continue
                    bi = ins.fn(eng)
                    if ins.is_dma:
                        bi.then_inc(dsems[ins.dma_key], 16)
                    elif ins.sig:
                        ep = (ins.cnt - 1) // EPOCH
                        bi.then_inc(esems[e][ep], 1)

            reg[e](body)


def build_nc():
    nc = bass.Bass("TRN2", target_bir_lowering=False)

    def din(name, shape):
        return nc.dram_tensor(name, list(shape), F32, kind="ExternalInput").ap()

    def dout(name, shape):
        return nc.dram_tensor(name, list(shape), F32, kind="ExternalOutput").ap()

    xp = din("xp", [SEQ, D])
    xs = din("xs", [NS * LS, D])
    sconv = din("sconv", [NS * HC, D])
    spool = din("spool", [NS * HP, D])
    cvec = din("cvec", [NS + 1, D])
    w_ada = din("w_ada", [D, 6 * D])
    b_ada = din("b_ada", [1, 6 * D])
    b_adaT = din("b_adaT", [128, 48])
    w_in = din("w_in", [D, 5 * D])
    cwT = din("cwT", [128, NCH * CW])
    vecT = din("vecT", [128, 4 * NCH])
    w_co = din("w_co", [D, D])
    pmix = din("pmix", [4, 256, 256])
    w_po = din("w_po", [D, D])
    w_o = din("w_o", [D, D])
    w_f1 = din("w_f1", [D, 4 * D])
    w_f2 = din("w_f2", [4 * D, D])
    fg = din("fg", [1, D])

    yp = dout("yp", [SEQ, D])
    ys = dout("ys", [NS * LS, D])
    ncp = dout("ncp", [HC, D])
    npp = dout("npp", [HP, D])
    ncs = dout("ncs", [NS * HC, D])
    nps = dout("nps", [NS * HP, D])

    with ExitStack() as st:
        S = Sched(nc)

        def sb(name, shape, dt=F32):
            return st.enter_context(nc.sbuf_tensor(name, list(shape), dt))

        ring = [sb(f"ring{i}", [128, 4096], BF16) for i in range(RING)]
        ringB = [Buf(f"ring{i}") for i in range(RING)]
        pmx = sb("pmx", [128, 4, 2, 256], BF16); pmxB = Buf("pmx")
        xres = sb("xres", [128, 4, D]); xresB = [Buf(f"xres{q}") for q in range(4)]
        NSTG = 3
        stg = [sb(f"stg{i}", [128, D]) for i in range(NSTG)]; stgB = [Buf(f"stg{i}") for i in range(NSTG)]
        h = sb("h", [128, NCH, 512], BF16); hB = [Buf(f"h{c}") for c in range(NCH)]
        aext = sb("aext", [128, NCH, 544], BF16); aB = [Buf(f"a{c}") for c in range(NCH)]
        NTMP = 4
        tmp = [sb(f"tmp{i}", [128, 512]) for i in range(NTMP)]; tmpB = [Buf(f"tmp{i}") for i in range(NTMP)]
        NTMPP = 4
        tmpp = [sb(f"tmpp{i}", [128, 544]) for i in range(NTMPP)]; tmppB = [Buf(f"tmpp{i}") for i in range(NTMPP)]
        NTB = 0
        tmb = [sb(f"tmb{i}", [128, 512], BF16) for i in range(NTB)]; tmbB = [Buf(f"tmb{i}") for i in range(NTB)]
        utail = sb("utail", [128, NCH, HP]); utB = [Buf(f"ut{c}") for c in range(NCH)]
        pooled = sb("pooled", [128, NCH, 512], BF16); plB = [Buf(f"pl{c}") for c in range(NCH)]
        r2f = sb("r2f", [128, 32 * 512], BF16)
        sg32 = r2f[:].bitcast(F32)
        sgB = [Buf(f"sg{c}") for c in range(16)]
        cf = sb("cf", [128, NCH, 512]); cfB = [Buf(f"cf{c}") for c in range(NCH)]
        badd = cf[:].rearrange("p c n -> p (c n)")[:, 0:2 * D].rearrange("p (g d) -> p g d", d=D)
        lnst = sb("lnst", [128, 2, 512]); lnB = [Buf("ln1"), Buf("ln2")]
        sqj = sb("sqj", [128, D], BF16); sqjB = Buf("sqj")
        uhist = cf[:].rearrange("p c n -> p (c n)")[:, 2 * D:2 * D + NCH * NS * HP].rearrange("p (c n j) -> p c n j", n=NS, j=HP)
        uhB = [None] * NCH
        s_t = sb("s_t", [128, NCH, 512], BF16); sB = [Buf(f"s{c}") for c in range(NCH)]
        m_t = sb("m_t", [128, NCH, 512], BF16); mB = [Buf(f"m{c}") for c in range(NCH)]
        gbc = sb("gbc", [128, 2, D]); gbcB = [Buf("g1bc"), Buf("g2bc")]
        gsr = gbc; gsrB = gbcB
        fgbc = sb("fgbc", [128, D]); fgB = Buf("fgbc")
        a32 = sb("a32", [128, NCH, 64]); a32B = [Buf(f"a32_{c}") for c in range(NCH)]
        u32 = sb("u32", [128, NCH, 64]); u32B = [Buf(f"u32_{c}") for c in range(NCH)]
        ident = sb("ident", [128, 128]); identB = Buf("ident")
        identb = sb("identb", [128, 128], BF16); identbB = Buf("identb")
        onesD = sb("onesD", [128, 128], BF16); onesB = Buf("onesD")
        epst = sb("epst", [128, 1]); epsB = Buf("eps")
        cw = sb("cw", [128, NCH, CW]); vecs = sb("vecs", [128, 4, NCH]); cwB = Buf("cw"); vecB = Buf("vecs")
        bT = sb("bT", [128, 48]); bTB = Buf("bT")
        modT = sb("modT", [128, 48, NS + 1]); modB = Buf("modT")
        scT32 = sb("scT32", [128, NCH, NS + 1]); scT = sb("scT", [128, NCH, NS + 1], BF16); scB = Buf("scT")
        rep_p = a32[:].rearrange("p c n -> p (c n)").bitcast(BF16).rearrange("p (c n) -> p c n", n=128)
        rep_s = u32[:].rearrange("p c n -> p (c n)").bitcast(BF16)[:, 0:NCH * 64].rearrange("p (c n) -> p c n", n=64)
        repB = Buf("rep")
        invc = sb("invc", [128, 4, 16]); invB = Buf("invc")
        stat = sb("stat", [128, 16]); statB = Buf("stat")
        NDG = 7
        dgp = sb("dgp", [128, NDG, 128], BF16); dgB = [Buf(f"dg{i}") for i in range(NDG)]
        psum = [st.enter_context(nc.psum_tensor(f"ps{i}", [128, 512], F32)) for i in range(8)]
        psB = [Buf(f"ps{i}") for i in range(8)]

        rr = {"ps": 0, "tmp": 0, "tmb": 0, "stg": 0, "stat": 0, "dg": 0, "tmpp": 0}

        def get_tmpp():
            i = rr["tmpp"]; rr["tmpp"] = (i + 1) % NTMPP
            return tmpp[i], tmppB[i]

        def get_dg():
            i = rr["dg"]; rr["dg"] = (i + 1) % NDG
            return dgp[:, i, :], dgB[i]

        def get_ps():
            i = rr["ps"]; rr["ps"] = (i + 1) % 8
            return psum[i], psB[i]

        def get_tmp():
            i = rr["tmp"]; rr["tmp"] = (i + 1) % NTMP
            return tmp[i], tmpB[i]

        def get_tmb():
            i = rr["tmb"]; rr["tmb"] = (i + 1) % NTB
            return tmb[i], tmbB[i]

        def get_stg():
            i = rr["stg"]; rr["stg"] = (i + 1) % NSTG
            return stg[i], stgB[i]

        def get_stat():
            i = rr["stat"]; rr["stat"] = (i + 1) % 8
            return i * 2

        NBT = 32
        wscr = nc.dram_tensor("wscr", [NBT, 128, 4096], BF16).ap()
        scrB = [Buf(f"scr{i}") for i in range(NBT)]
        gscr = nc.dram_tensor("gscr", [2, 64, D], F32).ap()
        gscrB = [Buf("gscr0"), Buf("gscr1")]
        scr_written = set()
        scr_uses = {}
        pending_wo = {}

        def wblock(W, nk, r0, c0, ncols, sid=None):
            return (W, nk, r0, c0, ncols, sid)

        blk_state = {"n": 0, "issued": 0, "plan": []}

        def issue_block(bi):
            W, nk, r0, c0, ncols, sid = blk_state["plan"][bi]
            slot = bi % RING
            if sid is not None and sid in scr_written:
                S.op("sp", lambda e: e.dma_start(out=ring[slot][:, :], in_=wscr[sid]),
                     reads=[scrB[sid]], writes=[ringB[slot]], dma_key=("wh", slot))
                return
            src = W[r0:r0 + nk * 128, c0:c0 + ncols].rearrange("(k p) c -> p k c", p=128)
            dst = ring[slot][:, 0:nk * ncols].rearrange("p (k c) -> p k c", c=ncols)
            S.op("pool", lambda e: e.dma_start(out=dst, in_=src), writes=[ringB[slot]], dma_key=("w", slot))
            if sid is not None:
                if scr_uses.get(sid, 0) == sid % 3:
                    pending_wo[bi] = (sid, slot)
                scr_uses[sid] = scr_uses.get(sid, 0) + 1

        def use_block(hold_prev=False):
            bi = blk_state["n"]
            blk_state["n"] += 1
            if bi in pending_wo:
                sid_, slot_ = pending_wo.pop(bi)
                S.op("sp", lambda e: e.dma_start(out=wscr[sid_], in_=ring[slot_][:, :]),
                     reads=[ringB[slot_]], writes=[scrB[sid_]], dma_key=("wo", slot_))
                scr_written.add(sid_)
            depth = RING - 1 if hold_prev else RING
            while blk_state["issued"] < min(bi + depth, len(blk_state["plan"])):
                issue_block(blk_state["issued"])
                blk_state["issued"] += 1
            W, nk, r0, c0, ncols, sid = blk_state["plan"][bi]
            slot = bi % RING
            return ring[slot], ringB[slot], nk, ncols

        def halves(W, c0):
            return [wblock(W, 8, 0, c0, 512), wblock(W, 8, 0, c0 + 512, 512)]

        def tile_plan():
            p = []
            for hf in range(2):
                p.append(wblock(w_in, 8, 0, hf * 512, 512))
                p.append(wblock(w_in, 8, 0, 1024 + hf * 512, 512))
            p += halves(w_in, 2048)
            p += halves(w_in, 3072)
            p += halves(w_in, 4096)
            for hf in range(2):
                p.append(wblock(w_po, 8, 0, hf * 512, 512))
                p.append(wblock(w_co, 8, 0, hf * 512, 512))
            p += halves(w_o, 0)
            for b in range(8):
                p.append(wblock(w_f1, 8, 0, b * 512, 512))
            for b in range(4):
                p.append(wblock(w_f2, 16, 0, b * 256, 256))
                p.append(wblock(w_f2, 16, 2048, b * 256, 256))
            return p

        def gs_plan():
            return halves(w_ada, 2 * D) + halves(w_ada, 5 * D)

        def with_sid(blocks, sid0):
            return [wblock(b[0], b[1], b[2], b[3], b[4], sid0 + i) for i, b in enumerate(blocks)]

        def plan_A(sample_=False):
            p = []
            for hf in range(2):
                p.append(wblock(w_in, 8, 0, hf * 512, 512))
                p.append(wblock(w_in, 8, 0, 1024 + hf * 512, 512))
            if sample_:
                return with_sid(p, 0) + with_sid(halves(w_in, 2048), 4)
            return with_sid(p, 0)

        def plan_B(sample_):
            p = halves(w_in, 2048) + halves(w_in, 3072) + halves(w_in, 4096)
            nskip = 2 if sample_ else 0
            for hf in range(2):
                p.append(wblock(w_po, 8, 0, hf * 512, 512))
                p.append(wblock(w_co, 8, 0, hf * 512, 512))
            p = with_sid(p + halves(w_o, 0), 4)
            return p[nskip:]

        def plan_F():
            p = [wblock(w_f1, 8, 0, b * 512, 512) for b in range(8)]
            for b in range(4):
                p.append(wblock(w_f2, 16, 0, b * 256, 256))
                p.append(wblock(w_f2, 16, 2048, b * 256, 256))
            return with_sid(p, 16)

        plan = []
        for v in range(2):
            plan += halves(w_ada, v * 1024)
        NTILES = 5
        pa_, pb_ = plan_A(), plan_B(False)
        plan += pa_[0:4] + halves(w_ada, 2 * 1024) + halves(w_ada, 3 * 1024)
        plan += pb_[0:2] + halves(w_ada, 4 * 1024) + pb_[2:6] + halves(w_ada, 5 * 1024) + pb_[6:]
        for ti_ in range(1, NTILES):
            plan += plan_A(ti_ == 4) + plan_F() + plan_B(ti_ == 4)
        plan += plan_F()
        blk_state["plan"] = plan

        S.op("pool", lambda e: e.memset(ident[:], 0.0), writes=[identB])
        S.op("pool", lambda e: e.affine_select(out=ident[:], in_=ident[:], pattern=[[-1, 128]],
                                               compare_op=ALU.not_equal, fill=1.0, base=0, channel_multiplier=1),
             reads=[identB], writes=[identB])
        S.op("dve", lambda e: e.tensor_copy(out=identb[:], in_=ident[:]), reads=[identB], writes=[identbB])
        S.op("dve", lambda e: e.memset(onesD[:], 1.0 / D), writes=[onesB])
        S.op("dve", lambda e: e.memset(epst[:], EPS), writes=[epsB])
        S.op("dve", lambda e: e.memset(utail[:], 0.0), writes=utB)
        for c in range(NCH):
            S.op("dve", lambda e, c=c: e.memset(aext[:, c, 0:HC], 0.0), writes=[aB[c]])
        for g in range(4):
            w = 2 << g
            S.op("dve", lambda e, g=g, w=w: e.memset(invc[:, g, :], 1.0 / w), writes=[invB])
            for t in range(w - 1):
                S.op("dve", lambda e, g=g, t=t: e.memset(invc[:, g, t:t + 1], 1.0 / (t + 1)), writes=[invB])
        S.op("sp", lambda e: e.dma_start(out=cw[:].rearrange("p c k -> p (c k)"), in_=cwT), writes=[cwB], dma_key=("c", 0))
        S.op("sp", lambda e: e.dma_start(out=vecs[:].rearrange("p v c -> p (v c)"), in_=vecT), writes=[vecB], dma_key=("c", 1))
        S.op("sp", lambda e: e.dma_start(out=bT[:], in_=b_adaT), writes=[bTB], dma_key=("c", 2))
        c_full, cB = get_stg()
        c_sb = c_full[0:NS + 1, :]
        S.op("sp", lambda e: e.dma_start(out=c_sb, in_=cvec), writes=[cB], dma_key=("c", 3))
        S.op("sp", lambda e: e.dma_start(out=fgbc[:], in_=fg.partition_broadcast(128)), writes=[fgB], dma_key=("c", 4))
        CONV_B, LN_G, LN_B, PSC = 0, 1, 2, 3
        S.op("act", lambda e: e.activation(out=c_sb, in_=c_sb, func=AF.Silu), reads=[cB], writes=[cB])
        pt, ptB = get_ps()
        for k in range(NCH):
            S.op("pe", lambda e, k=k: e.transpose(out=pt[:, k * 17:(k + 1) * 17], in_=c_full[0:17, k * 128:(k + 1) * 128],
                                                  identity=ident[0:17, 0:17]),
                 reads=[cB, identB], writes=[ptB])
        S.op("dve", lambda e: e.tensor_copy(out=scT32[:].rearrange("p k n -> p (k n)"), in_=pt[:, 0:136]),
             reads=[ptB], writes=[scB])
        S.op("dve", lambda e: e.tensor_copy(out=scT[:].rearrange("p k n -> p (k n)"), in_=scT32[:].rearrange("p k n -> p (k n)")),
             reads=[scB], writes=[scB])
        for k in range(NCH):
            S.op("dve", lambda e, k=k: e.tensor_copy(out=rep_p[:, k, :], in_=scT32[:, k, 0:1].to_broadcast([128, 128])),
                 reads=[scB], writes=[repB])
            S.op("dve", lambda e, k=k: e.tensor_copy(out=rep_s[:, k, :].rearrange("p (n t) -> p n t", t=LS),
                                                     in_=scT32[:, k, 1:17].unsqueeze(2).to_broadcast([128, NS, LS])),
                 reads=[scB], writes=[repB])
        def g_rows(gi, f, slot, slotB, sample_rows, badd_ap=None, baddBufs=None, dst=None, dstBufs=None):
            pg, pgB = get_ps()
            R_ = 64 if sample_rows else 128
            lhs = rep_s if sample_rows else rep_p
            for k in range(NCH):
                S.op("pe", lambda e, k=k, pg=pg: e.matmul(
                    pg[0:R_, :], lhsT=lhs[:, k, :], rhs=slot[:, k * 512:(k + 1) * 512],
                    start=(k == 0), stop=(k == NCH - 1)), reads=[slotB, repB], writes=[pgB])
            S.op("dve", lambda e, pg=pg: e.tensor_tensor(
                out=(gbc[0:R_, gi, f * 512:(f + 1) * 512] if dst is None else dst), in0=pg[0:R_, :],
                in1=(badd[0:R_, gi, f * 512:(f + 1) * 512] if badd_ap is None else badd_ap), op=ALU.add),
                reads=[pgB] + (cfB[0:4] if baddBufs is None else baddBufs),
                writes=([gbcB[gi]] if dstBufs is None else dstBufs))

        def load_badd():
            S.op("sp", lambda e: e.dma_start(out=badd[:, 0, :], in_=b_ada[:, 2 * D:3 * D].partition_broadcast(128)),
                 writes=cfB[0:4], dma_key=("c", 5))
            S.op("sp", lambda e: e.dma_start(out=badd[:, 1, :], in_=b_ada[:, 5 * D:6 * D].partition_broadcast(128)),
                 writes=cfB[0:4], dma_key=("c", 6))

        def mod_v(v):
                pm_, pmB_ = get_ps()
                for hf in range(2):
                    slot, slotB, nk, ncols = use_block()
                    for cc in range(4):
                        c = hf * 4 + cc
                        for k in range(NCH):
                            S.op("pe", lambda e, c=c, cc=cc, k=k, slot=slot, pm_=pm_: e.matmul(
                                pm_[:, c * 17:(c + 1) * 17], lhsT=slot[:, k * 512 + cc * 128:k * 512 + (cc + 1) * 128],
                                rhs=scT[:, k, :], start=(k == 0), stop=(k == NCH - 1)),
                                reads=[slotB, scB], writes=[pmB_])
                    if v in (2, 5):
                        if hf == 0:
                            bs_, bsB_ = get_stg()
                            S.op("sp", lambda e, bs_=bs_, v=v: e.dma_start(
                                out=bs_[:, :], in_=b_ada[:, v * D:(v + 1) * D].partition_broadcast(128)),
                                writes=[bsB_], dma_key=("bs", 0 if v == 2 else 1))
                        gi_ = 0 if v == 2 else 1
                        g_rows(gi_, hf, slot, slotB, False, bs_[:, hf * 512:(hf + 1) * 512], [bsB_])
                        if hf == 0:
                            gs_, gsB_ = get_stg()
                        g_rows(gi_, hf, slot, slotB, True, bs_[0:64, hf * 512:(hf + 1) * 512], [bsB_],
                               dst=gs_[0:64, hf * 512:(hf + 1) * 512], dstBufs=[gsB_])
                        if hf == 1:
                            S.op("sp", lambda e, gs_=gs_, gi_=gi_: e.dma_start(out=gscr[gi_], in_=gs_[0:64, :]),
                                 reads=[gsB_], writes=[gscrB[gi_]], dma_key=("gso", gi_))
                S.op("dve", lambda e, v=v, pm_=pm_: e.tensor_tensor(
                    out=modT[:, v * 8:(v + 1) * 8, :], in0=pm_[:, 0:136].rearrange("p (c n) -> p c n", n=17),
                    in1=bT[:, v * 8:(v + 1) * 8].unsqueeze(2).to_broadcast([128, 8, 17]), op=ALU.add),
                    reads=[pmB_, bTB], writes=[modB])
                if v in (1, 4):
                    S.op("dve", lambda e, v=v: e.tensor_scalar(out=modT[:, v * 8:(v + 1) * 8, :], in0=modT[:, v * 8:(v + 1) * 8, :],
                                                              scalar1=1.0, scalar2=None, op0=ALU.add),
                         reads=[modB], writes=[modB])

        mod_v(0)
        mod_v(1)
        for g in range(4):
            S.op("pool", lambda e, g=g: e.dma_start(out=pmx[:, g, :, :],
                                                    in_=pmix[g].rearrange("(k p) c -> p k c", p=128)),
                 writes=[pmxB], dma_key=("pm", g))

        def mod_rest():
            for v in range(2, 6):
                mod_v(v)

        SH1, SC1, G1, SH2, SC2, G2 = range(6)

        def rms_rstd(src_ap, srcB, R, junk, junkB):
            c0 = get_stat()
            S.op("act", lambda e: e.activation(out=junk[0:R, :], in_=src_ap, func=AF.Square, scale=1.0 / 32.0,
                                               accum_out=stat[0:R, c0:c0 + 1]),
                 reads=[srcB], writes=[junkB, statB])
            S.op("act", lambda e: e.activation(out=stat[0:R, c0 + 1:c0 + 2], in_=stat[0:R, c0:c0 + 1], func=AF.Sqrt,
                                               bias=epst[0:R, 0:1], scale=1.0),
                 reads=[statB, epsB], writes=[statB])
            S.op("dve", lambda e: e.reciprocal(out=stat[0:R, c0 + 1:c0 + 2], in_=stat[0:R, c0 + 1:c0 + 2]),
                 reads=[statB], writes=[statB])
            return stat[0:R, c0 + 1:c0 + 2]

        def norm_stats(src_ap, srcB, R, in_place=None):
            if in_place is not None:
                xn, xnB = in_place
                rstd = rms_rstd(src_ap, srcB, R, sqj, sqjB)
            else:
                xn, xnB = get_stg()
                rstd = rms_rstd(src_ap, srcB, R, xn, xnB)
            S.op("act", lambda e: e.activation(out=xn[0:R, :], in_=src_ap, func=AF.Copy, scale=rstd),
                 reads=[srcB, statB], writes=[xnB])
            return xn, xnB

        def norm_gen(srcs, R, dst, dstB, vsc, vsh, sample, prefetch=None, n_early=2):
            nsub_ = len(srcs)
            staged = {}
            xns = {}
            if prefetch is not None:
                for q in range(min(NSTG - 1, nsub_)):
                    staged[q] = prefetch(q)

            def do_stats(q):
                if prefetch is not None:
                    xbuf, xB_ = staged.pop(q)
                    return norm_stats(xbuf[0:R, :], xB_, R, in_place=(xbuf, xB_))
                src_ap, srcB = srcs[q]()
                return norm_stats(src_ap, srcB, R)

            for q in range(min(n_early, nsub_)):
                xns[q] = do_stats(q)
            mark = rr["stg"]
            yield
            assert rr["stg"] == mark, "staging ring used between the two halves of a norm"
            banks = [get_ps() for _ in range(NCH)]
            for q in range(nsub_):
                if q not in xns:
                    xns[q] = do_stats(q)
                xn, xnB = xns.pop(q)
                for c in range(NCH):
                    pt_, ptB_ = banks[c]
                    S.op("pe", lambda e, c=c, q=q, pt_=pt_, xn=xn: e.transpose(
                        out=pt_[:, q * 128:q * 128 + R], in_=xn[0:R, c * 128:(c + 1) * 128], identity=ident[0:R, 0:R]),
                        reads=[xnB, identB], writes=[ptB_])
                if prefetch is not None and q + NSTG - 1 < nsub_:
                    staged[q + NSTG - 1] = prefetch(q + NSTG - 1)
            W_ = (nsub_ - 1) * 128 + R
            for c in range(NCH):
                pt_, ptB_ = banks[c]
                if not sample:
                    if c % 2 == 0:
                        S.op("act", lambda e, c=c, pt_=pt_: e.activation(
                            out=dst[:, c, 0:W_], in_=pt_[:, 0:W_], func=AF.Identity,
                            bias=modT[:, vsh * 8 + c, 0:1], scale=modT[:, vsc * 8 + c, 0:1]),
                            reads=[ptB_, modB], writes=[dstB[c]])
                    else:
                        S.op("dve", lambda e, c=c, pt_=pt_: e.tensor_scalar(
                            out=dst[:, c, 0:W_], in0=pt_[:, 0:W_], scalar1=modT[:, vsc * 8 + c, 0:1],
                            scalar2=modT[:, vsh * 8 + c, 0:1], op0=ALU.mult, op1=ALU.add),
                            reads=[ptB_, modB], writes=[dstB[c]])
                else:
                    t_, tB_ = get_tmp()
                    S.op("dve", lambda e, c=c, pt_=pt_, t_=t_: e.tensor_tensor(
                        out=t_[:, 0:R].rearrange("p (n t) -> p n t", t=LS),
                        in0=pt_[:, 0:R].rearrange("p (n t) -> p n t", t=LS),
                        in1=modT[:, vsc * 8 + c, 1:17].unsqueeze(2).to_broadcast([128, NS, LS]), op=ALU.mult),
                        reads=[ptB_, modB], writes=[tB_])
                    S.op("dve", lambda e, c=c, t_=t_: e.tensor_tensor(
                        out=dst[:, c, 0:R].rearrange("p (n t) -> p n t", t=LS),
                        in0=t_[:, 0:R].rearrange("p (n t) -> p n t", t=LS),
                        in1=modT[:, vsh * 8 + c, 1:17].unsqueeze(2).to_broadcast([128, NS, LS]), op=ALU.add),
                        reads=[tB_, modB], writes=[dstB[c]])

        def fm_rows_out(src, srcBs, ncols, dst_dram_ap, key, col0=0):
            so, soB = get_stg()
            for hb in range(2):
                pt_, ptB_ = get_ps()
                for cc in range(4):
                    c = hb * 4 + cc
                    S.op("pe", lambda e, c=c, cc=cc, pt_=pt_: e.transpose(
                        out=pt_[0:ncols, cc * 128:(cc + 1) * 128], in_=src[:, c, col0:col0 + ncols], identity=ident[:, :]),
                        reads=[srcBs[c], identB], writes=[ptB_])
                S.op("act", lambda e, hb=hb, pt_=pt_: e.activation(out=so[0:ncols, hb * 512:(hb + 1) * 512],
                                                                  in_=pt_[0:ncols, :], func=AF.Copy),
                     reads=[ptB_], writes=[soB])
            S.op("sp", lambda e: e.dma_start(out=dst_dram_ap, in_=so[0:ncols, :]), reads=[soB], dma_key=key, out_dma=True)

        def make_tile(ti):
            sample = (ti == 4)
            if not sample:
                NT, nseq, L, nsub, R = 512, 1, 512, 4, 128
            else:
                NT, nseq, L, nsub, R = 64, NS, LS, 1, 64
            first = (ti == 0)
            last_prompt = (ti == 3)
            AW = HC + L
            UW = HP + L
            mcol = 0 if not sample else 1

            def aview(c):
                return aext[:, c, 0:nseq * AW].rearrange("p (n w) -> p n w", w=AW)

            def v3(ap2d):
                return ap2d.rearrange("p (n l) -> p n l", l=L)

            tpe = (CW if first else T_PE) if not sample else 0
            n32 = HC if last_prompt else (NT if sample else 0)
            grow = (lambda gi, lo, hi: gbc[0:R, gi, lo:hi])
            growB = gbcB

            def phase_A1():
                if sample:
                    for qq in range(4):
                        hs, hsB = get_stg()
                        S.op("sp", lambda e, qq=qq, hs=hs: e.dma_start(out=hs[0:120, :], in_=sconv[qq * 120:(qq + 1) * 120, :]),
                             writes=[hsB], dma_key=("hs", qq))
                        for hb in range(2):
                            pt_, ptB_ = get_ps()
                            for cc in range(4):
                                c = hb * 4 + cc
                                S.op("pe", lambda e, c=c, cc=cc, pt_=pt_, hs=hs: e.transpose(
                                    out=pt_[:, cc * 128:cc * 128 + 120], in_=hs[0:120, c * 128:(c + 1) * 128],
                                    identity=ident[0:120, 0:120]), reads=[hsB, identB], writes=[ptB_])
                            for cc in range(4):
                                c = hb * 4 + cc
                                S.op("act", lambda e, c=c, cc=cc, pt_=pt_, qq=qq: e.activation(
                                    out=aview(c)[:, qq * 4:(qq + 1) * 4, 0:HC],
                                    in_=pt_[:, cc * 128:cc * 128 + 120].rearrange("p (n j) -> p n j", j=HC), func=AF.Copy),
                                    reads=[ptB_], writes=[aB[c]])
                    for qq in range(2):
                        hs, hsB = get_stg()
                        S.op("sp", lambda e, qq=qq, hs=hs: e.dma_start(out=hs[0:120, :], in_=spool[qq * 120:(qq + 1) * 120, :]),
                             writes=[hsB], dma_key=("hp", qq))
                        for hb in range(2):
                            pt_, ptB_ = get_ps()
                            for cc in range(4):
                                c = hb * 4 + cc
                                S.op("pe", lambda e, c=c, cc=cc, pt_=pt_, hs=hs: e.transpose(
                                    out=pt_[:, cc * 128:cc * 128 + 120], in_=hs[0:120, c * 128:(c + 1) * 128],
                                    identity=ident[0:120, 0:120]), reads=[hsB, identB], writes=[ptB_])
                            for cc in range(4):
                                c = hb * 4 + cc
                                S.op("act", lambda e, c=c, cc=cc, pt_=pt_, qq=qq: e.activation(
                                    out=uhist[:, c, qq * 8:(qq + 1) * 8, :],
                                    in_=pt_[:, cc * 128:cc * 128 + 120].rearrange("p (n j) -> p n j", j=HP), func=AF.Copy),
                                    reads=[ptB_], writes=cfB[4:8])
                    S.op("sp", lambda e: e.dma_start(
                        out=ncs.rearrange("(n j) d -> n j d", j=HC)[:, 0:HC - LS, :],
                        in_=sconv.rearrange("(n j) d -> n j d", j=HC)[:, LS:HC, :]), dma_key=("o", "ncs0"), out_dma=True)
                    S.op("sp", lambda e: e.dma_start(
                        out=nps.rearrange("(n j) d -> n j d", j=HP)[:, 0:HP - LS, :],
                        in_=spool.rearrange("(n j) d -> n j d", j=HP)[:, LS:HP, :]), dma_key=("o", "nps0"), out_dma=True)


                def load_x(q):
                    src = xp[ti * 512 + q * 128: ti * 512 + (q + 1) * 128, :] if not sample else xs
                    xs_, xsB_ = get_stg()
                    S.op("act", lambda e: e.dma_start(out=xs_[0:R, :], in_=src), writes=[xsB_], dma_key=("xa", q))
                    return xs_, xsB_
                g_ = norm_gen([None] * nsub, R, h, hB, SC1, SH1, sample, prefetch=load_x)
                next(g_)
                yield
                for _ in g_:
                    pass

            def stage_S3():
                slotU = slotUB = None
                for j in range(NCH):
                    if j % 4 == 0:
                        slotU, slotUB, _, _ = use_block()
                    jj = j % 4
                    g = j // 2
                    w = 2 << g
                    pu, puB = get_ps()
                    for k in range(NCH):
                        S.op("pe", lambda e, jj=jj, k=k, pu=pu, slotU=slotU: e.matmul(
                            pu[:, 0:NT], lhsT=slotU[:, k * 512 + jj * 128:k * 512 + (jj + 1) * 128], rhs=h[:, k, 0:NT],
                            start=(k == 0), stop=(k == NCH - 1)), reads=[slotUB, hB[k]], writes=[puB])
                    ue, ueB = get_tmpp()
                    uev = ue[:, 0:nseq * UW].rearrange("p (n w) -> p n w", w=UW)
                    S.op("act", lambda e, pu=pu, uev=uev: e.activation(out=uev[:, :, HP:HP + L], in_=v3(pu[:, 0:NT]), func=AF.Copy),
                         reads=[puB], writes=[ueB])
                    if not sample:
                        S.op("pool", lambda e, j=j, uev=uev: e.tensor_copy(out=uev[:, 0, 0:HP], in_=utail[:, j, :]),
                             reads=[utB[j]], writes=[ueB])
                        S.op("pool", lambda e, j=j, uev=uev: e.tensor_copy(out=utail[:, j, :], in_=uev[:, 0, L:L + HP]),
                             reads=[ueB], writes=[utB[j]])
                        if last_prompt:
                            S.op("pool", lambda e, j=j, uev=uev: e.tensor_copy(out=u32[:, j, 0:HP], in_=uev[:, 0, L:L + HP]),
                                 reads=[ueB], writes=[u32B[j], repB])
                    else:
                        S.op("pool", lambda e, j=j, uev=uev: e.tensor_copy(out=uev[:, :, 0:HP], in_=uhist[:, j, :, :]),
                             reads=cfB[4:8], writes=[ueB])
                        S.op("pool", lambda e, j=j, uev=uev: e.tensor_copy(
                            out=u32[:, j, 0:NT].rearrange("p (t n) -> p n t", n=NS), in_=uev[:, :, HP:HP + L]),
                            reads=[ueB], writes=[u32B[j], repB])
                    prev, prevB = uev, ueB
                    d = 1
                    pp_bufs = []
                    lvl = 0
                    while d < w:
                        lo = 2 * d - 1
                        if lvl < 2:
                            pp_bufs.append(get_tmpp())
                        nx, nxB = pp_bufs[lvl % 2]
                        lvl += 1
                        nxv = nx[:, 0:nseq * UW].rearrange("p (n w) -> p n w", w=UW)
                        S.op("pool" if sample else "dve", lambda e, prev=prev, nxv=nxv, lo=lo, d=d: e.tensor_tensor(
                            out=nxv[:, :, lo:UW], in0=prev[:, :, lo:UW], in1=prev[:, :, lo - d:UW - d], op=ALU.add),
                            reads=[prevB], writes=[nxB])
                        prev, prevB = nxv, nxB
                        d *= 2
                    S.op("dve", lambda e, j=j, prev=prev, uev=uev, w=w: e.scalar_tensor_tensor(
                        out=v3(pooled[:, j, 0:NT]), in0=prev[:, :, HP:HP + L], scalar=1.0 / w, in1=uev[:, :, HP:HP + L],
                        op0=ALU.mult, op1=ALU.subtract), reads=[prevB, ueB], writes=[plB[j]])
                    if first:
                        fx, fxB = get_tmp()
                        S.op("dve", lambda e, prev=prev, fx=fx, g=g: e.tensor_tensor(
                            out=fx[:, 0:HP], in0=prev[:, 0, HP:2 * HP], in1=invc[:, g, 0:HP], op=ALU.mult),
                            reads=[prevB, invB], writes=[fxB])
                        S.op("dve", lambda e, j=j, fx=fx, uev=uev: e.tensor_tensor(
                            out=pooled[:, j, 0:HP], in0=fx[:, 0:HP], in1=uev[:, 0, HP:2 * HP], op=ALU.subtract),
                            reads=[fxB, ueB], writes=[plB[j]])


            def phase_A2():
                slotV = slotVB = slotG = slotGB = None
                for j in range(NCH):
                    if j % 4 == 0:
                        slotV, slotVB, _, _ = use_block()
                        slotG, slotGB, _, _ = use_block(hold_prev=True)
                    jj = j % 4
                    pv, pvB = get_ps()
                    pg, pgB = get_ps()
                    for k in range(NCH):
                        S.op("pe", lambda e, jj=jj, k=k, pv=pv, slotV=slotV: e.matmul(
                            pv[:, 0:NT], lhsT=slotV[:, k * 512 + jj * 128:k * 512 + (jj + 1) * 128], rhs=h[:, k, 0:NT],
                            start=(k == 0), stop=(k == NCH - 1)), reads=[slotVB, hB[k]], writes=[pvB])
                    for k in range(NCH):
                        S.op("pe", lambda e, jj=jj, k=k, pg=pg, slotG=slotG: e.matmul(
                            pg[:, 0:NT], lhsT=slotG[:, k * 512 + jj * 128:k * 512 + (jj + 1) * 128], rhs=h[:, k, 0:NT],
                            start=(k == 0), stop=(k == NCH - 1)), reads=[slotGB, hB[k]], writes=[pgB])
                    t_, tB_ = get_tmp()
                    S.op("act", lambda e, pg=pg, t_=t_: e.activation(out=t_[:, 0:NT], in_=pg[:, 0:NT], func=AF.Sigmoid),
                         reads=[pgB], writes=[tB_])
                    S.op("dve", lambda e, j=j, pv=pv, t_=t_: e.tensor_tensor(
                        out=aview(j)[:, :, HC:HC + L], in0=v3(pv[:, 0:NT]), in1=v3(t_[:, 0:NT]), op=ALU.mult),
                        reads=[pvB, tB_], writes=[aB[j]])
                    if n32 and not sample:
                        S.op("dve", lambda e, j=j, pv=pv, t_=t_: e.tensor_tensor(
                            out=a32[:, j, 0:n32], in0=pv[:, NT - n32:NT], in1=t_[:, NT - n32:NT], op=ALU.mult),
                            reads=[pvB, tB_], writes=[a32B[j], repB])
                    if sample:
                        S.op("dve", lambda e, j=j, pv=pv, t_=t_: e.tensor_tensor(
                            out=a32[:, j, 0:NT].rearrange("p (t n) -> p n t", n=NS), in0=v3(pv[:, 0:NT]), in1=v3(t_[:, 0:NT]),
                            op=ALU.mult), reads=[pvB, tB_], writes=[a32B[j], repB])

                if "after_S2" in hooks:
                    hooks["after_S2"]()
                if prev_norm2[0] is not None:
                    for _ in prev_norm2[0]():
                        pass

                if tpe > 0:
                    for j in range(NCH):
                        pc, pcB = get_ps()
                        for k in range(tpe):
                            dg, dgB_ = get_dg()
                            if k % 2 == 0:
                                S.op("dve", lambda e, j=j, k=k, dg=dg: e.tensor_scalar(
                                    out=dg, in0=identb[:, :], scalar1=cw[:, j, k:k + 1], scalar2=None, op0=ALU.mult),
                                    reads=[identbB, cwB], writes=[dgB_])
                            else:
                                S.op("act", lambda e, j=j, k=k, dg=dg: e.activation(
                                    out=dg, in_=identb[:, :], func=AF.Copy, scale=cw[:, j, k:k + 1]),
                                    reads=[identbB, cwB], writes=[dgB_])
                            S.op("pe", lambda e, j=j, k=k, pc=pc, dg=dg: e.matmul(
                                pc[:, 0:NT], lhsT=dg, rhs=aext[:, j, k:k + L],
                                start=(k == 0), stop=(k == tpe - 1)), reads=[dgB_, aB[j]], writes=[pcB])
                        S.op("act", lambda e, j=j, pc=pc: e.activation(
                            out=cf[:, j, 0:NT], in_=pc[:, 0:NT], func=AF.Identity, bias=vecs[:, CONV_B, j:j + 1], scale=1.0),
                            reads=[pcB, vecB], writes=[cfB[j]])


                if "after_convPE" in hooks:
                    hooks["after_convPE"]()
                if sample:
                    stage_S3()

            def conv_gen():
                for k in range(tpe, CW):
                    for j in range(NCH):
                        if k == 0:
                            S.op("dve", lambda e, j=j: e.tensor_scalar(
                                out=v3(cf[:, j, 0:NT]), in0=aview(j)[:, :, 0:L], scalar1=cw[:, j, 0:1],
                                scalar2=vecs[:, CONV_B, j:j + 1], op0=ALU.mult, op1=ALU.add),
                                reads=[aB[j], cwB, vecB], writes=[cfB[j]])
                        else:
                            S.op("dve", lambda e, j=j, k=k: e.scalar_tensor_tensor(
                                out=v3(cf[:, j, 0:NT]), in0=aview(j)[:, :, k:k + L], scalar=cw[:, j, k:k + 1],
                                in1=v3(cf[:, j, 0:NT]), op0=ALU.mult, op1=ALU.add),
                                reads=[aB[j], cwB, cfB[j]], writes=[cfB[j]])
                    yield
                if not sample and not last_prompt:
                    for j in range(NCH):
                        S.op("pool", lambda e, j=j: e.tensor_copy(out=aext[:, j, 0:HC], in_=aext[:, j, L:L + HC]),
                             reads=[aB[j]], writes=[aB[j]])

            def phase_B():
                if sample:
                    for gi in range(2):
                        S.op("sp", lambda e, gi=gi: e.dma_start(out=gbc[0:64, gi, :], in_=gscr[gi]),
                             reads=[gscrB[gi]], writes=[gbcB[gi]], dma_key=("gsi", gi))
                for q in range(nsub):
                    src = xp[ti * 512 + q * 128: ti * 512 + (q + 1) * 128, :] if not sample else xs
                    S.op("act", lambda e, q=q, src=src: e.dma_start(out=xres[0:R, q, :], in_=src),
                         writes=[xresB[q]], dma_key=("x", q))
                if last_prompt:
                    fm_rows_out(a32, a32B, HC, ncp, ("o", "ncp"))
                def ln_copy(j):
                    S.op("pool", lambda e, j=j: e.tensor_copy(out=s_t[:, j, 0:NT], in_=cf[:, j, 0:NT]),
                         reads=[cfB[j]], writes=[sB[j]])
                    S.op("pool", lambda e, j=j: e.tensor_tensor(out=m_t[:, j, 0:NT], in0=cf[:, j, 0:NT], in1=cf[:, j, 0:NT],
                                                                op=ALU.mult),
                         reads=[cfB[j]], writes=[mB[j]])
                for j in range(NCH):
                    ln_copy(j)
                a1_ = None
                if next_A1[0] is not None:
                    a1_ = next_A1[0]()
                    next(a1_)
                if not sample:
                    stage_S3()
                if last_prompt:
                    fm_rows_out(u32, u32B, HP, npp, ("o", "npp"))
                if "after_S3" in hooks:
                    hooks["after_S3"]()

                for gi in range(2):
                    slotX = slotXB = None
                    for j in range(NCH):
                        if j % 4 == 0:
                            slotX, slotXB, _, _ = use_block()
                        jj = j % 4
                        pq, pqB = get_ps()
                        for k in range(NCH):
                            S.op("pe", lambda e, jj=jj, k=k, pq=pq, slotX=slotX: e.matmul(
                                pq[:, 0:NT], lhsT=slotX[:, k * 512 + jj * 128:k * 512 + (jj + 1) * 128], rhs=h[:, k, 0:NT],
                                start=(k == 0), stop=(k == NCH - 1)), reads=[slotXB, hB[k]], writes=[pqB])
                        ci = gi * 8 + j
                        S.op("act", lambda e, ci=ci, pq=pq: e.activation(
                            out=sg32[:, ci * 512:ci * 512 + NT], in_=pq[:, 0:NT], func=AF.Sigmoid),
                            reads=[pqB], writes=[sgB[ci]])

                if a1_ is not None:
                    for _ in a1_:
                        pass
                if "after_S5" in hooks:
                    hooks["after_S5"]()

                pmean, pmeanB = get_ps()
                pe2, pe2B = get_ps()
                for j in range(NCH):
                    S.op("pe", lambda e, j=j: e.matmul(pmean[:, 0:NT], lhsT=onesD[:, :], rhs=s_t[:, j, 0:NT],
                                                       start=(j == 0), stop=(j == NCH - 1)),
                         reads=[onesB, sB[j]], writes=[pmeanB])
                    S.op("pe", lambda e, j=j: e.matmul(pe2[:, 0:NT], lhsT=onesD[:, :], rhs=m_t[:, j, 0:NT],
                                                       start=(j == 0), stop=(j == NCH - 1)),
                         reads=[onesB, mB[j]], writes=[pe2B])

                slots7 = {}

                def emit_ob(j):
                    slotPO, slotPOB = slots7["po"]
                    jj = j % 4
                    pb_, pbB_ = get_ps()
                    for k in range(NCH):
                        S.op("pe", lambda e, jj=jj, k=k, pb_=pb_: e.matmul(
                            pb_[:, 0:NT], lhsT=slotPO[:, k * 512 + jj * 128:k * 512 + (jj + 1) * 128], rhs=pooled[:, k, 0:NT],
                            start=(k == 0), stop=(k == NCH - 1)), reads=[slotPOB, plB[k]], writes=[pbB_])
                    t2, t2B = get_tmp()
                    S.op("dve", lambda e, j=j, pb_=pb_, t2=t2: e.tensor_tensor(
                        out=t2[:, 0:NT], in0=pb_[:, 0:NT], in1=sg32[:, (8 + j) * 512:(8 + j) * 512 + NT], op=ALU.mult),
                        reads=[pbB_, sgB[8 + j]], writes=[t2B])
                    return t2, t2B

                msq, msqB = get_tmp()
                rstd, rstdB = lnst[:, 0, :], lnB[0]
                nmr, nmrB = lnst[:, 1, :], lnB[1]
                S.op("act", lambda e: e.activation(out=msq[:, 0:NT], in_=pmean[:, 0:NT], func=AF.Square),
                     reads=[pmeanB], writes=[msqB])
                S.op("dve", lambda e: e.tensor_tensor(out=msq[:, 0:NT], in0=pe2[:, 0:NT], in1=msq[:, 0:NT], op=ALU.subtract),
                     reads=[pe2B, msqB], writes=[msqB])
                S.op("act", lambda e: e.activation(out=rstd[:, 0:NT], in_=msq[:, 0:NT], func=AF.Sqrt, bias=epst[:, 0:1], scale=1.0),
                     reads=[msqB, epsB], writes=[rstdB])
                S.op("dve", lambda e: e.reciprocal(out=rstd[:, 0:NT], in_=rstd[:, 0:NT]), reads=[rstdB], writes=[rstdB])
                S.op("dve", lambda e: e.scalar_tensor_tensor(out=nmr[:, 0:NT], in0=pmean[:, 0:NT], scalar=-1.0, in1=rstd[:, 0:NT],
                                                             op0=ALU.mult, op1=ALU.mult),
                     reads=[pmeanB, rstdB], writes=[nmrB])
                for j in range(NCH):
                    z, zB = get_tmp()
                    S.op("dve", lambda e, j=j, z=z: e.tensor_tensor(out=z[:, 0:NT], in0=cf[:, j, 0:NT], in1=rstd[:, 0:NT], op=ALU.mult),
                         reads=[cfB[j], rstdB], writes=[zB])
                    S.op("dve", lambda e, z=z: e.tensor_tensor(out=z[:, 0:NT], in0=z[:, 0:NT], in1=nmr[:, 0:NT], op=ALU.add),
                         reads=[zB, nmrB], writes=[zB])
                    S.op("act", lambda e, j=j, z=z: e.activation(
                        out=s_t[:, j, 0:NT], in_=z[:, 0:NT], func=AF.Silu, bias=vecs[:, LN_B, j:j + 1], scale=vecs[:, LN_G, j:j + 1]),
                        reads=[zB, vecB], writes=[sB[j]])

                for g in range(4):
                    pps = []
                    for jj in range(2):
                        pp, ppB = get_ps()
                        pps.append((pp, ppB))
                        for kk in range(2):
                            S.op("pe", lambda e, g=g, jj=jj, kk=kk, pp=pp: e.matmul(
                                pp[:, 0:NT], lhsT=pmx[:, g, kk, jj * 128:(jj + 1) * 128], rhs=pooled[:, 2 * g + kk, 0:NT],
                                start=(kk == 0), stop=(kk == 1)), reads=[pmxB, plB[2 * g + kk]], writes=[ppB])
                    for jj in range(2):
                        pp, ppB = pps[jj]
                        j = 2 * g + jj
                        S.op("act", lambda e, j=j, pp=pp: e.activation(
                            out=pooled[:, j, 0:NT], in_=pp[:, 0:NT], func=AF.Copy, scale=vecs[:, PSC, j:j + 1]),
                            reads=[ppB, vecB], writes=[plB[j]])

                for j in range(NCH):
                    if j % 4 == 0:
                        a_, b_, _, _ = use_block(); slots7["po"] = (a_, b_)
                        a_, b_, _, _ = use_block(hold_prev=True); slots7["co"] = (a_, b_)
                    slotCO, slotCOB = slots7["co"]
                    jj = j % 4
                    t2, t2B = emit_ob(j)
                    pa_, paB_ = get_ps()
                    for k in range(NCH):
                        S.op("pe", lambda e, jj=jj, k=k, pa_=pa_, slotCO=slotCO: e.matmul(
                            pa_[:, 0:NT], lhsT=slotCO[:, k * 512 + jj * 128:k * 512 + (jj + 1) * 128], rhs=s_t[:, k, 0:NT],
                            start=(k == 0), stop=(k == NCH - 1)), reads=[slotCOB, sB[k]], writes=[paB_])
                    t1, t1B = get_tmp()
                    S.op("dve", lambda e, j=j, pa_=pa_, t1=t1: e.tensor_tensor(
                        out=t1[:, 0:NT], in0=pa_[:, 0:NT], in1=sg32[:, j * 512:j * 512 + NT], op=ALU.mult),
                        reads=[paB_, sgB[j]], writes=[t1B])
                    S.op("dve", lambda e, j=j, t1=t1, t2=t2: e.tensor_tensor(
                        out=m_t[:, j, 0:NT], in0=t1[:, 0:NT], in1=t2[:, 0:NT], op=ALU.add),
                        reads=[t1B, t2B], writes=[mB[j]])

                for f in range(2):
                    slotO, slotOB, _, _ = use_block()
                    for q in range(nsub):
                        po, poB = get_ps()
                        for k in range(NCH):
                            S.op("pe", lambda e, q=q, k=k, po=po, slotO=slotO: e.matmul(
                                po[0:R, :], lhsT=m_t[:, k, q * 128:q * 128 + R], rhs=slotO[:, k * 512:(k + 1) * 512],
                                start=(k == 0), stop=(k == NCH - 1)), reads=[slotOB, mB[k]], writes=[poB])
                        t_, tB_ = get_tmp()
                        S.op("dve", lambda e, f=f, po=po, t_=t_: e.tensor_tensor(
                            out=t_[0:R, 0:512], in0=po[0:R, :], in1=grow(0, f * 512, (f + 1) * 512), op=ALU.mult),
                            reads=[poB, growB[0]], writes=[tB_])
                        S.op("pool", lambda e, q=q, f=f, t_=t_: e.tensor_tensor(
                            out=xres[0:R, q, f * 512:(f + 1) * 512], in0=xres[0:R, q, f * 512:(f + 1) * 512], in1=t_[0:R, 0:512], op=ALU.add),
                            reads=[tB_, xresB[q]], writes=[xresB[q]])
                n2_ = norm2()
                if not last_prompt:
                    next(n2_)
                if defer_norm2[0] is None:
                    for _ in n2_:
                        pass
                else:
                    defer_norm2[0] = n2_


            next_A1 = [None]
            hooks = {}
            defer_norm2 = [None]
            prev_norm2 = [None]

            def norm2():
                return norm_gen([(lambda q=q: (xres[0:R, q, :], xresB[q])) for q in range(nsub)], R, s_t, sB, SC2, SH2, sample)

            def r2deps(j):
                return [sgB[j // 2]]

            def F_gen():
                for jb in range(8):
                    slotF, slotFB, _, _ = use_block()
                    for jj in range(4):
                        j = jb * 4 + jj
                        pf, pfB = get_ps()
                        for k in range(NCH):
                            S.op("pe", lambda e, jj=jj, k=k, pf=pf, slotF=slotF: e.matmul(
                                pf[:, 0:NT], lhsT=slotF[:, k * 512 + jj * 128:k * 512 + (jj + 1) * 128], rhs=s_t[:, k, 0:NT],
                                start=(k == 0), stop=(k == NCH - 1)), reads=[slotFB, sB[k]], writes=[pfB])
                        S.op("act", lambda e, pf=pf: e.activation(out=pf[:, 0:NT], in_=pf[:, 0:NT], func=AF.Relu),
                             reads=[pfB], writes=[pfB])
                        S.op("act", lambda e, j=j, pf=pf: e.activation(
                            out=r2f[:, j * 512:j * 512 + NT], in_=pf[:, 0:NT], func=AF.Square),
                            reads=[pfB], writes=r2deps(j))
                    yield

                for fq in range(4):
                    pws = [get_ps() for _ in range(nsub)]
                    for kh in range(2):
                        slotW, slotWB, _, _ = use_block()
                        for q in range(nsub):
                            pw, pwB = pws[q]
                            for kk in range(16):
                                k = kh * 16 + kk
                                S.op("pe", lambda e, q=q, k=k, kk=kk, pw=pw, slotW=slotW: e.matmul(
                                    pw[0:R, 0:256], lhsT=r2f[:, k * 512 + q * 128:k * 512 + q * 128 + R],
                                    rhs=slotW[:, kk * 256:(kk + 1) * 256],
                                    start=(k == 0), stop=(k == 31)), reads=[slotWB] + r2deps(k), writes=[pwB])
                    for q in range(nsub):
                        pw, pwB = pws[q]
                        t_, tB_ = get_tmp()
                        S.op("dve", lambda e, fq=fq, pw=pw, t_=t_: e.tensor_tensor(
                            out=t_[0:R, 0:256], in0=pw[0:R, 0:256], in1=grow(1, fq * 256, (fq + 1) * 256), op=ALU.mult),
                            reads=[pwB, growB[1]], writes=[tB_])
                        S.op("pool", lambda e, q=q, fq=fq, t_=t_: e.tensor_tensor(
                            out=xres[0:R, q, fq * 256:(fq + 1) * 256], in0=xres[0:R, q, fq * 256:(fq + 1) * 256],
                            in1=t_[0:R, 0:256], op=ALU.add), reads=[tB_, xresB[q]], writes=[xresB[q]])
                    yield

                for q in range(nsub):
                    yo, yoB = get_stg()
                    rstd_ = rms_rstd(xres[0:R, q, :], xresB[q], R, yo, yoB)
                    S.op("dve", lambda e, q=q, yo=yo, rstd_=rstd_: e.scalar_tensor_tensor(
                        out=yo[0:R, :], in0=xres[0:R, q, :], scalar=rstd_, in1=fgbc[0:R, :], op0=ALU.mult, op1=ALU.mult),
                        reads=[xresB[q], statB, fgB], writes=[yoB])
                    if not sample:
                        dst = yp[ti * 512 + q * 128: ti * 512 + (q + 1) * 128, :]
                    else:
                        dst = ys
                    S.op("sp", lambda e, yo=yo, dst=dst: e.dma_start(out=dst, in_=yo[0:R, :]), reads=[yoB],
                         dma_key=("y", ti, q), out_dma=True)


                if sample:
                    for t in range(LS):
                        fm_rows_out(a32, a32B, NS, ncs.rearrange("(n j) d -> n j d", j=HC)[:, HC - LS + t, :],
                                    ("o", "ncs1", t), col0=t * NS)
                        fm_rows_out(u32, u32B, NS, nps.rearrange("(n j) d -> n j d", j=HP)[:, HP - LS + t, :],
                                    ("o", "nps1", t), col0=t * NS)
                yield

            return phase_A1, phase_A2, conv_gen, phase_B, F_gen, next_A1, hooks, defer_norm2, prev_norm2, norm2

        tiles = [make_tile(ti) for ti in range(NTILES)]

        def drain(g):
            for _ in g:
                pass

        for ti in range(NTILES - 2):
            tiles[ti][5][0] = tiles[ti + 1][0]
        for ti in range(NTILES - 1):
            tiles[ti][7][0] = True
            tiles[ti + 1][8][0] = (lambda ti=ti: tiles[ti][7][0])
        tiles[0][6]["after_S2"] = lambda: mod_v(2)
        tiles[0][6]["after_convPE"] = lambda: mod_v(3)
        tiles[0][6]["after_S3"] = lambda: mod_v(4)
        tiles[0][6]["after_S5"] = lambda: mod_v(5)
        drain(tiles[0][0]())
        tiles[0][1]()
        drain(tiles[0][2]())
        tiles[0][3]()
        for ti in range(1, NTILES):
            if ti == NTILES - 1:
                drain(tiles[ti][0]())
            tiles[ti][1]()
            cg = tiles[ti][2]()
            fg_ = tiles[ti - 1][4]()
            step = 0
            n_rounds = CW - (T_PE if ti < 4 else 0)
            done_r = 0
            for _ in fg_:
                step += 1
                tgt = min(n_rounds, (n_rounds * step + 9) // 10)
                while done_r < tgt:
                    next(cg, None)
                    done_r += 1
            drain(cg)
            tiles[ti][3]()
        drain(tiles[NTILES - 1][4]())

        S.emit(st)
    return nc


_NC_CACHE = {}


def kernel(x_prompt, x_sample, state_conv, state_pool, c_prompt, c_sample, w_ada, b_ada, w_in,
           conv_w, conv_b, ln_g, ln_b, w_conv_out, pool_mix, pool_scale, w_pool_out, w_out,
           w_ff1, w_ff2, final_g):
    f = lambda a: np.ascontiguousarray(np.asarray(a, dtype=np.float32))
    x_prompt, x_sample, state_conv, state_pool = f(x_prompt), f(x_sample), f(state_conv), f(state_pool)
    c_prompt, c_sample = f(c_prompt), f(c_sample)
    if "nc" not in _NC_CACHE:
        _NC_CACHE["nc"] = build_nc()
    nc = _NC_CACHE["nc"]

    def fm(vec):
        return f(np.asarray(vec).reshape(NCH, 128).T)

    shared = {
        "w_ada": f(w_ada[0]), "b_ada": f(np.asarray(b_ada[0]).reshape(1, 6 * D)),
        "b_adaT": f(np.asarray(b_ada[0]).reshape(48, 128).T),
        "w_in": f(w_in[0]),
        "cwT": f(np.asarray(conv_w[0]).T.reshape(NCH, 128, CW).transpose(1, 0, 2).reshape(128, NCH * CW)),
        "vecT": f(np.concatenate([fm(conv_b[0]), fm(ln_g[0]), fm(ln_b[0]), fm(pool_scale[0])], axis=1)),
        "w_co": f(w_conv_out[0]), "pmix": f(pool_mix[0]), "w_po": f(w_pool_out[0]), "w_o": f(w_out[0]),
        "w_f1": f(w_ff1[0]), "w_f2": f(w_ff2[0]), "fg": f(np.asarray(final_g).reshape(1, D)),
    }
    in_maps = []
    for i in range(NCORES):
        m = dict(shared)
        m["xp"] = x_prompt[i]
        m["xs"] = x_sample[i * NS:(i + 1) * NS].reshape(NS * LS, D)
        m["sconv"] = state_conv[0, i * NS:(i + 1) * NS].reshape(NS * HC, D)
        m["spool"] = state_pool[0, i * NS:(i + 1) * NS].reshape(NS * HP, D)
        m["cvec"] = f(np.concatenate([c_prompt[i:i + 1], c_sample[i * NS:(i + 1) * NS]], axis=0))
        in_maps.append(m)
    res = run_bass_kernel_spmd(nc, in_maps, core_ids=list(range(NCORES)))
    rs = res.results
    y_prompt = np.stack([rs[i]["yp"] for i in range(NCORES)], axis=0)
    y_sample = np.concatenate([rs[i]["ys"].reshape(NS, LS, D) for i in range(NCORES)], axis=0)
    ncp_ = np.stack([rs[i]["ncp"] for i in range(NCORES)], axis=0)[None]
    npp_ = np.stack([rs[i]["npp"] for i in range(NCORES)], axis=0)[None]
    ncs_ = np.concatenate([rs[i]["ncs"].reshape(NS, HC, D) for i in range(NCORES)], axis=0)[None]
    nps_ = np.concatenate([rs[i]["nps"].reshape(NS, HP, D) for i in range(NCORES)], axis=0)[None]
    return (y_prompt.astype(np.float32), y_sample.astype(np.float32), ncp_.astype(np.float32),
            npp_.astype(np.float32), ncs_.astype(np.float32), nps_.astype(np.float32))
```

```python
import numpy as np
from contextlib import ExitStack
import concourse.bass as bass
import concourse.mybir as mybir
from concourse.bass_utils import run_bass_kernel_spmd

F32 = mybir.dt.float32
BF16 = mybir.dt.bfloat16
AF = mybir.ActivationFunctionType
ALU = mybir.AluOpType

D = 1024
NCH = 8
SEQ = 2048
NS = 16
LS = 4
HC = 30
HP = 15
CW = 31
EPS = 1e-6
NCORES = 8

T_PE = 12
RING = 5

ENGS = ("pe", "act", "dve", "pool", "sp")
EPOCH = 16000
SAFE_DIST = 4
STRICT_SAME_ENGINE = True


class Buf:
    __slots__ = ("name", "last_w", "readers")

    def __init__(self, name):
        self.name = name
        self.last_w = None
        self.readers = []


class Ins:
    __slots__ = ("eng", "fn", "deps", "sig", "cnt", "dma_key", "dma_val", "is_dma", "idx")


class Sched:
    def __init__(self, nc):
        self.nc = nc
        self.q = {e: [] for e in ENGS}
        self.dma_cnt = {}
        self.all_dma_out = []

    def op(self, eng, fn, reads=(), writes=(), dma_key=None, out_dma=False):
        ins = Ins()
        ins.eng = eng
        ins.fn = fn
        ins.sig = False
        ins.cnt = 0
        ins.is_dma = dma_key is not None
        ins.dma_key = dma_key
        ins.dma_val = 0
        ins.idx = len(self.q[eng])
        if ins.is_dma:
            v = self.dma_cnt.get(dma_key, 0) + 16
            self.dma_cnt[dma_key] = v
            ins.dma_val = v
        deps = {}
        for b in reads:
            if b.last_w is not None:
                deps[id(b.last_w)] = (b.last_w, True)
        for b in writes:
            if b.last_w is not None and id(b.last_w) not in deps:
                deps[id(b.last_w)] = (b.last_w, False)
            for r in b.readers:
                if id(r) not in deps:
                    deps[id(r)] = (r, False)
        final = []
        for d, raw in deps.values():
            if (not d.is_dma) and (not ins.is_dma) and d.eng == eng:
                if eng == "pe":
                    continue
                if not STRICT_SAME_ENGINE:
                    if not raw or ins.idx - d.idx >= SAFE_DIST:
                        continue
            final.append(d)
        ins.deps = final
        for b in reads:
            b.readers.append(ins)
        for b in writes:
            b.last_w = ins
            b.readers = []
        self.q[eng].append(ins)
        if out_dma:
            self.all_dma_out.append(ins)
        return ins

    def emit(self, stack):
        nc = self.nc
        fin = Ins()
        fin.eng = "sp"; fin.fn = None; fin.sig = False; fin.cnt = 0
        fin.is_dma = False; fin.dma_key = None; fin.dma_val = 0; fin.idx = len(self.q["sp"])
        fin.deps = list(self.all_dma_out)
        self.q["sp"].append(fin)
        for e in ENGS:
            for ins in self.q[e]:
                for d in ins.deps:
                    d.sig = True
        nsig = {}
        for e in ENGS:
            c = 0
            for ins in self.q[e]:
                if ins.sig and not ins.is_dma:
                    c += 1
                    ins.cnt = c
            nsig[e] = c
        esems = {}
        for e in ENGS:
            n = (nsig[e] + EPOCH - 1) // EPOCH
            esems[e] = [stack.enter_context(nc.semaphore(f"s_{e}_{i}")) for i in range(max(n, 1))]
        dsems = {}
        for k in self.dma_cnt:
            dsems[k] = stack.enter_context(nc.semaphore("d_" + "_".join(str(x) for x in k)))

        def signal_of(d):
            if d.is_dma:
                return ("d", d.dma_key), dsems[d.dma_key], d.dma_val
            ep = (d.cnt - 1) // EPOCH
            return ("e", d.eng, ep), esems[d.eng][ep], d.cnt - ep * EPOCH

        block = stack.enter_context(nc.Block())
        reg = {"pe": block.tensor, "act": block.scalar, "dve": block.vector,
               "pool": block.gpsimd, "sp": block.sync}
        for e in ENGS:
            qe = self.q[e]

            def body(eng, qe=qe, e=e):
                waited = {}
                maxep = {}
                for ins in qe:
                    for d in ins.deps:
                        key, sem, val = signal_of(d)
                        if key[0] == "e":
                            if maxep.get(key[1], -1) > key[2]:
                                continue
                        if waited.get(key, 0) < val:
                            eng.wait_ge(sem, val)
                            waited[key] = val
                            if key[0] == "e":
                                maxep[key[1]] = max(maxep.get(key[1], -1), key[2])
                    if ins.fn is None:
                        continue
                    bi = ins.fn(eng)
                    if ins.is_dma:
                        bi.then_inc(dsems[ins.dma_key], 16)
                    elif ins.sig:
                        ep = (ins.cnt - 1) // EPOCH
                        bi.then_inc(esems[e][ep], 1)

            reg[e](body)


def build_nc():
    nc = bass.Bass("TRN2", target_bir_lowering=False)

    def din(name, shape):
        return nc.dram_tensor(name, list(shape), F32, kind="ExternalInput").ap()

    def dout(name, shape):
        return nc.dram_tensor(name, list(shape), F32, kind="ExternalOutput").ap()

    xp = din("xp", [SEQ, D])
    xs = din("xs", [NS * LS, D])
    sconv = din("sconv", [NS * HC, D])
    spool = din("spool", [NS * HP, D])
    cvec = din("cvec", [NS + 1, D])
    w_ada = din("w_ada", [D, 6 * D])
    b_ada = din("b_ada", [1, 6 * D])
    b_adaT = din("b_adaT", [128, 48])
    w_in = din("w_in", [D, 5 * D])
    cwT = din("cwT", [128, NCH * CW])
    vecT = din("vecT", [128, 4 * NCH])
    w_co = din("w_co", [D, D])
    pmix = din("pmix", [4, 256, 256])
    w_po = din("w_po", [D, D])
    w_o = din("w_o", [D, D])
    w_f1 = din("w_f1", [D, 4 * D])
    w_f2 = din("w_f2", [4 * D, D])
    fg = din("fg", [1, D])

    yp = dout("yp", [SEQ, D])
    ys = dout("ys", [NS * LS, D])
    ncp = dout("ncp", [HC, D])
    npp = dout("npp", [HP, D])
    ncs = dout("ncs", [NS * HC, D])
    nps = dout("nps", [NS * HP, D])

    with ExitStack() as st:
        S = Sched(nc)

        def sb(name, shape, dt=F32):
            return st.enter_context(nc.sbuf_tensor(name, list(shape), dt))

        ring = [sb(f"ring{i}", [128, 4096], BF16) for i in range(RING)]
        ringB = [Buf(f"ring{i}") for i in range(RING)]
        pmx = sb("pmx", [128, 4, 2, 256], BF16); pmxB = Buf("pmx")
        xres = sb("xres", [128, 4, D]); xresB = [Buf(f"xres{q}") for q in range(4)]
        NSTG = 3
        stg = [sb(f"stg{i}", [128, D]) for i in range(NSTG)]; stgB = [Buf(f"stg{i}") for i in range(NSTG)]
        h = sb("h", [128, NCH, 512], BF16); hB = [Buf(f"h{c}") for c in range(NCH)]
        aext = sb("aext", [128, NCH, 544], BF16); aB = [Buf(f"a{c}") for c in range(NCH)]
        NTMP = 4
        tmp = [sb(f"tmp{i}", [128, 512]) for i in range(NTMP)]; tmpB = [Buf(f"tmp{i}") for i in range(NTMP)]
        NTMPP = 4
        tmpp = [sb(f"tmpp{i}", [128, 544]) for i in range(NTMPP)]; tmppB = [Buf(f"tmpp{i}") for i in range(NTMPP)]
        NTB = 0
        tmb = [sb(f"tmb{i}", [128, 512], BF16) for i in range(NTB)]; tmbB = [Buf(f"tmb{i}") for i in range(NTB)]
        utail = sb("utail", [128, NCH, HP]); utB = [Buf(f"ut{c}") for c in range(NCH)]
        pooled = sb("pooled", [128, NCH, 512], BF16); plB = [Buf(f"pl{c}") for c in range(NCH)]
        r2f = sb("r2f", [128, 32 * 512], BF16)
        sg32 = r2f[:].bitcast(F32)
        sgB = [Buf(f"sg{c}") for c in range(16)]
        cf = sb("cf", [128, NCH, 512]); cfB = [Buf(f"cf{c}") for c in range(NCH)]
        badd = cf[:].rearrange("p c n -> p (c n)")[:, 0:2 * D].rearrange("p (g d) -> p g d", d=D)
        lnst = sb("lnst", [128, 2, 512]); lnB = [Buf("ln1"), Buf("ln2")]
        sqj = sb("sqj", [128, D], BF16); sqjB = Buf("sqj")
        uhist = cf[:].rearrange("p c n -> p (c n)")[:, 2 * D:2 * D + NCH * NS * HP].rearrange("p (c n j) -> p c n j", n=NS, j=HP)
        uhB = [None] * NCH
        s_t = sb("s_t", [128, NCH, 512], BF16); sB = [Buf(f"s{c}") for c in range(NCH)]
        m_t = sb("m_t", [128, NCH, 512], BF16); mB = [Buf(f"m{c}") for c in range(NCH)]
        gbc = sb("gbc", [128, 2, D]); gbcB = [Buf("g1bc"), Buf("g2bc")]
        gsr = gbc; gsrB = gbcB
        fgbc = sb("fgbc", [128, D]); fgB = Buf("fgbc")
        a32 = sb("a32", [128, NCH, 64]); a32B = [Buf(f"a32_{c}") for c in range(NCH)]
        u32 = sb("u32", [128, NCH, 64]); u32B = [Buf(f"u32_{c}") for c in range(NCH)]
        ident = sb("ident", [128, 128]); identB = Buf("ident")
        identb = sb("identb", [128, 128], BF16); identbB = Buf("identb")
        onesD = sb("onesD", [128, 128], BF16); onesB = Buf("onesD")
        epst = sb("epst", [128, 1]); epsB = Buf("eps")
        cw = sb("cw", [128, NCH, CW]); vecs = sb("vecs", [128, 4, NCH]); cwB = Buf("cw"); vecB = Buf("vecs")
        bT = sb("bT", [128, 48]); bTB = Buf("bT")
        modT = sb("modT", [128, 48, NS + 1]); modB = Buf("modT")
        scT32 = sb("scT32", [128, NCH, NS + 1]); scT = sb("scT", [128, NCH, NS + 1], BF16); scB = Buf("scT")
        rep_p = a32[:].rearrange("p c n -> p (c n)").bitcast(BF16).rearrange("p (c n) -> p c n", n=128)
        rep_s = u32[:].rearrange("p c n -> p (c n)").bitcast(BF16)[:, 0:NCH * 64].rearrange("p (c n) -> p c n", n=64)
        repB = Buf("rep")
        invc = sb("invc", [128, 4, 16]); invB = Buf("invc")
        stat = sb("stat", [128, 16]); statB = Buf("stat")
        NDG = 7
        dgp = sb("dgp", [128, NDG, 128], BF16); dgB = [Buf(f"dg{i}") for i in range(NDG)]
        psum = [st.enter_context(nc.psum_tensor(f"ps{i}", [128, 512], F32)) for i in range(8)]
        psB = [Buf(f"ps{i}") for i in range(8)]

        rr = {"ps": 0, "tmp": 0, "tmb": 0, "stg": 0, "stat": 0, "dg": 0, "tmpp": 0}

        def get_tmpp():
            i = rr["tmpp"]; rr["tmpp"] = (i + 1) % NTMPP
            return tmpp[i], tmppB[i]

        def get_dg():
            i = rr["dg"]; rr["dg"] = (i + 1) % NDG
            return dgp[:, i, :], dgB[i]

        def get_ps():
            i = rr["ps"]; rr["ps"] = (i + 1) % 8
            return psum[i], psB[i]

        def get_tmp():
            i = rr["tmp"]; rr["tmp"] = (i + 1) % NTMP
            return tmp[i], tmpB[i]

        def get_tmb():
            i = rr["tmb"]; rr["tmb"] = (i + 1) % NTB
            return tmb[i], tmbB[i]

        def get_stg():
            i = rr["stg"]; rr["stg"] = (i + 1) % NSTG
            return stg[i], stgB[i]

        def get_stat():
            i = rr["stat"]; rr["stat"] = (i + 1) % 8
            return i * 2

        NBT = 32
        wscr = nc.dram_tensor("wscr", [NBT, 128, 4096], BF16).ap()
        scrB = [Buf(f"scr{i}") for i in range(NBT)]
        gscr = nc.dram_tensor("gscr", [2, 64, D], F32).ap()
        gscrB = [Buf("gscr0"), Buf("gscr1")]
        scr_written = set()
        scr_uses = {}
        pending_wo = {}

        def wblock(W, nk, r0, c0, ncols, sid=None):
            return (W, nk, r0, c0, ncols, sid)

        blk_state = {"n": 0, "issued": 0, "plan": []}

        def issue_block(bi):
            W, nk, r0, c0, ncols, sid = blk_state["plan"][bi]
            slot = bi % RING
            if sid is not None and sid in scr_written:
                S.op("sp", lambda e: e.dma_start(out=ring[slot][:, :], in_=wscr[sid]),
                     reads=[scrB[sid]], writes=[ringB[slot]], dma_key=("wh", slot))
                return
            src = W[r0:r0 + nk * 128, c0:c0 + ncols].rearrange("(k p) c -> p k c", p=128)
            dst = ring[slot][:, 0:nk * ncols].rearrange("p (k c) -> p k c", c=ncols)
            S.op("pool", lambda e: e.dma_start(out=dst, in_=src), writes=[ringB[slot]], dma_key=("w", slot))
            if sid is not None:
                if scr_uses.get(sid, 0) == (sid // 2) % 3:
                    pending_wo[bi] = (sid, slot)
                scr_uses[sid] = scr_uses.get(sid, 0) + 1

        def use_block(hold_prev=False):
            bi = blk_state["n"]
            blk_state["n"] += 1
            if bi in pending_wo:
                sid_, slot_ = pending_wo.pop(bi)
                S.op("sp", lambda e: e.dma_start(out=wscr[sid_], in_=ring[slot_][:, :]),
                     reads=[ringB[slot_]], writes=[scrB[sid_]], dma_key=("wo", slot_))
                scr_written.add(sid_)
            depth = RING - 1 if hold_prev else RING
            while blk_state["issued"] < min(bi + depth, len(blk_state["plan"])):
                issue_block(blk_state["issued"])
                blk_state["issued"] += 1
            W, nk, r0, c0, ncols, sid = blk_state["plan"][bi]
            slot = bi % RING
            return ring[slot], ringB[slot], nk, ncols

        def halves(W, c0):
            return [wblock(W, 8, 0, c0, 512), wblock(W, 8, 0, c0 + 512, 512)]

        def tile_plan():
            p = []
            for hf in range(2):
                p.append(wblock(w_in, 8, 0, hf * 512, 512))
                p.append(wblock(w_in, 8, 0, 1024 + hf * 512, 512))
            p += halves(w_in, 2048)
            p += halves(w_in, 3072)
            p += halves(w_in, 4096)
            for hf in range(2):
                p.append(wblock(w_po, 8, 0, hf * 512, 512))
                p.append(wblock(w_co, 8, 0, hf * 512, 512))
            p += halves(w_o, 0)
            for b in range(8):
                p.append(wblock(w_f1, 8, 0, b * 512, 512))
            for b in range(4):
                p.append(wblock(w_f2, 16, 0, b * 256, 256))
                p.append(wblock(w_f2, 16, 2048, b * 256, 256))
            return p

        def gs_plan():
            return halves(w_ada, 2 * D) + halves(w_ada, 5 * D)

        def with_sid(blocks, sid0):
            return [wblock(b[0], b[1], b[2], b[3], b[4], sid0 + i) for i, b in enumerate(blocks)]

        def plan_A(sample_=False):
            p = []
            for hf in range(2):
                p.append(wblock(w_in, 8, 0, hf * 512, 512))
                p.append(wblock(w_in, 8, 0, 1024 + hf * 512, 512))
            if sample_:
                return with_sid(p, 0) + with_sid(halves(w_in, 2048), 4)
            return with_sid(p, 0)

        def plan_B(sample_):
            p = halves(w_in, 2048) + halves(w_in, 3072) + halves(w_in, 4096)
            nskip = 2 if sample_ else 0
            for hf in range(2):
                p.append(wblock(w_po, 8, 0, hf * 512, 512))
                p.append(wblock(w_co, 8, 0, hf * 512, 512))
            p = with_sid(p + halves(w_o, 0), 4)
            return p[nskip:]

        def plan_F():
            p = [wblock(w_f1, 8, 0, b * 512, 512) for b in range(8)]
            for b in range(4):
                p.append(wblock(w_f2, 16, 0, b * 256, 256))
                p.append(wblock(w_f2, 16, 2048, b * 256, 256))
            return with_sid(p, 16)

        plan = []
        for v in range(2):
            plan += halves(w_ada, v * 1024)
        NTILES = 5
        pa_, pb_ = plan_A(), plan_B(False)
        plan += pa_[0:4] + halves(w_ada, 2 * 1024) + halves(w_ada, 3 * 1024)
        plan += pb_[0:2] + halves(w_ada, 4 * 1024) + pb_[2:6] + halves(w_ada, 5 * 1024) + pb_[6:]
        for ti_ in range(1, NTILES):
            plan += plan_A(ti_ == 4) + plan_F() + plan_B(ti_ == 4)
        plan += plan_F()
        blk_state["plan"] = plan

        S.op("pool", lambda e: e.memset(ident[:], 0.0), writes=[identB])
        S.op("pool", lambda e: e.affine_select(out=ident[:], in_=ident[:], pattern=[[-1, 128]],
                                               compare_op=ALU.not_equal, fill=1.0, base=0, channel_multiplier=1),
             reads=[identB], writes=[identB])
        S.op("dve", lambda e: e.tensor_copy(out=identb[:], in_=ident[:]), reads=[identB], writes=[identbB])
        S.op("dve", lambda e: e.memset(onesD[:], 1.0 / D), writes=[onesB])
        S.op("dve", lambda e: e.memset(epst[:], EPS), writes=[epsB])
        S.op("dve", lambda e: e.memset(utail[:], 0.0), writes=utB)
        for c in range(NCH):
            S.op("dve", lambda e, c=c: e.memset(aext[:, c, 0:HC], 0.0), writes=[aB[c]])
        for g in range(4):
            w = 2 << g
            S.op("dve", lambda e, g=g, w=w: e.memset(invc[:, g, :], 1.0 / w), writes=[invB])
            for t in range(w - 1):
                S.op("dve", lambda e, g=g, t=t: e.memset(invc[:, g, t:t + 1], 1.0 / (t + 1)), writes=[invB])
        S.op("sp", lambda e: e.dma_start(out=cw[:].rearrange("p c k -> p (c k)"), in_=cwT), writes=[cwB], dma_key=("c", 0))
        S.op("sp", lambda e: e.dma_start(out=vecs[:].rearrange("p v c -> p (v c)"), in_=vecT), writes=[vecB], dma_key=("c", 1))
        S.op("sp", lambda e: e.dma_start(out=bT[:], in_=b_adaT), writes=[bTB], dma_key=("c", 2))
        c_full, cB = get_stg()
        c_sb = c_full[0:NS + 1, :]
        S.op("sp", lambda e: e.dma_start(out=c_sb, in_=cvec), writes=[cB], dma_key=("c", 3))
        S.op("sp", lambda e: e.dma_start(out=fgbc[:], in_=fg.partition_broadcast(128)), writes=[fgB], dma_key=("c", 4))
        CONV_B, LN_G, LN_B, PSC = 0, 1, 2, 3
        S.op("act", lambda e: e.activation(out=c_sb, in_=c_sb, func=AF.Silu), reads=[cB], writes=[cB])
        pt, ptB = get_ps()
        for k in range(NCH):
            S.op("pe", lambda e, k=k: e.transpose(out=pt[:, k * 17:(k + 1) * 17], in_=c_full[0:17, k * 128:(k + 1) * 128],
                                                  identity=ident[0:17, 0:17]),
                 reads=[cB, identB], writes=[ptB])
        S.op("dve", lambda e: e.tensor_copy(out=scT32[:].rearrange("p k n -> p (k n)"), in_=pt[:, 0:136]),
             reads=[ptB], writes=[scB])
        S.op("dve", lambda e: e.tensor_copy(out=scT[:].rearrange("p k n -> p (k n)"), in_=scT32[:].rearrange("p k n -> p (k n)")),
             reads=[scB], writes=[scB])
        for k in range(NCH):
            S.op("dve", lambda e, k=k: e.tensor_copy(out=rep_p[:, k, :], in_=scT32[:, k, 0:1].to_broadcast([128, 128])),
                 reads=[scB], writes=[repB])
            S.op("dve", lambda e, k=k: e.tensor_copy(out=rep_s[:, k, :].rearrange("p (n t) -> p n t", t=LS),
                                                     in_=scT32[:, k, 1:17].unsqueeze(2).to_broadcast([128, NS, LS])),
                 reads=[scB], writes=[repB])
        def g_rows(gi, f, slot, slotB, sample_rows, badd_ap=None, baddBufs=None, dst=None, dstBufs=None):
            pg, pgB = get_ps()
            R_ = 64 if sample_rows else 128
            lhs = rep_s if sample_rows else rep_p
            for k in range(NCH):
                S.op("pe", lambda e, k=k, pg=pg: e.matmul(
                    pg[0:R_, :], lhsT=lhs[:, k, :], rhs=slot[:, k * 512:(k + 1) * 512],
                    start=(k == 0), stop=(k == NCH - 1)), reads=[slotB, repB], writes=[pgB])
            S.op("dve", lambda e, pg=pg: e.tensor_tensor(
                out=(gbc[0:R_, gi, f * 512:(f + 1) * 512] if dst is None else dst), in0=pg[0:R_, :],
                in1=(badd[0:R_, gi, f * 512:(f + 1) * 512] if badd_ap is None else badd_ap), op=ALU.add),
                reads=[pgB] + (cfB[0:4] if baddBufs is None else baddBufs),
                writes=([gbcB[gi]] if dstBufs is None else dstBufs))

        def load_badd():
            S.op("sp", lambda e: e.dma_start(out=badd[:, 0, :], in_=b_ada[:, 2 * D:3 * D].partition_broadcast(128)),
                 writes=cfB[0:4], dma_key=("c", 5))
            S.op("sp", lambda e: e.dma_start(out=badd[:, 1, :], in_=b_ada[:, 5 * D:6 * D].partition_broadcast(128)),
                 writes=cfB[0:4], dma_key=("c", 6))

        def mod_v(v):
                pm_, pmB_ = get_ps()
                for hf in range(2):
                    slot, slotB, nk, ncols = use_block()
                    for cc in range(4):
                        c = hf * 4 + cc
                        for k in range(NCH):
                            S.op("pe", lambda e, c=c, cc=cc, k=k, slot=slot, pm_=pm_: e.matmul(
                                pm_[:, c * 17:(c + 1) * 17], lhsT=slot[:, k * 512 + cc * 128:k * 512 + (cc + 1) * 128],
                                rhs=scT[:, k, :], start=(k == 0), stop=(k == NCH - 1)),
                                reads=[slotB, scB], writes=[pmB_])
                    if v in (2, 5):
                        if hf == 0:
                            bs_, bsB_ = get_stg()
                            S.op("sp", lambda e, bs_=bs_, v=v: e.dma_start(
                                out=bs_[:, :], in_=b_ada[:, v * D:(v + 1) * D].partition_broadcast(128)),
                                writes=[bsB_], dma_key=("bs", 0 if v == 2 else 1))
                        gi_ = 0 if v == 2 else 1
                        g_rows(gi_, hf, slot, slotB, False, bs_[:, hf * 512:(hf + 1) * 512], [bsB_])
                        if hf == 0:
                            gs_, gsB_ = get_stg()
                        g_rows(gi_, hf, slot, slotB, True, bs_[0:64, hf * 512:(hf + 1) * 512], [bsB_],
                               dst=gs_[0:64, hf * 512:(hf + 1) * 512], dstBufs=[gsB_])
                        if hf == 1:
                            S.op("sp", lambda e, gs_=gs_, gi_=gi_: e.dma_start(out=gscr[gi_], in_=gs_[0:64, :]),
                                 reads=[gsB_], writes=[gscrB[gi_]], dma_key=("gso", gi_))
                S.op("dve", lambda e, v=v, pm_=pm_: e.tensor_tensor(
                    out=modT[:, v * 8:(v + 1) * 8, :], in0=pm_[:, 0:136].rearrange("p (c n) -> p c n", n=17),
                    in1=bT[:, v * 8:(v + 1) * 8].unsqueeze(2).to_broadcast([128, 8, 17]), op=ALU.add),
                    reads=[pmB_, bTB], writes=[modB])
                if v in (1, 4):
                    S.op("dve", lambda e, v=v: e.tensor_scalar(out=modT[:, v * 8:(v + 1) * 8, :], in0=modT[:, v * 8:(v + 1) * 8, :],
                                                              scalar1=1.0, scalar2=None, op0=ALU.add),
                         reads=[modB], writes=[modB])

        mod_v(0)
        mod_v(1)
        for g in range(4):
            S.op("pool", lambda e, g=g: e.dma_start(out=pmx[:, g, :, :],
                                                    in_=pmix[g].rearrange("(k p) c -> p k c", p=128)),
                 writes=[pmxB], dma_key=("pm", g))

        def mod_rest():
            for v in range(2, 6):
                mod_v(v)

        SH1, SC1, G1, SH2, SC2, G2 = range(6)

        def rms_rstd(src_ap, srcB, R, junk, junkB):
            c0 = get_stat()
            S.op("act", lambda e: e.activation(out=junk[0:R, :], in_=src_ap, func=AF.Square, scale=1.0 / 32.0,
                                               accum_out=stat[0:R, c0:c0 + 1]),
                 reads=[srcB], writes=[junkB, statB])
            S.op("act", lambda e: e.activation(out=stat[0:R, c0 + 1:c0 + 2], in_=stat[0:R, c0:c0 + 1], func=AF.Sqrt,
                                               bias=epst[0:R, 0:1], scale=1.0),
                 reads=[statB, epsB], writes=[statB])
            S.op("dve", lambda e: e.reciprocal(out=stat[0:R, c0 + 1:c0 + 2], in_=stat[0:R, c0 + 1:c0 + 2]),
                 reads=[statB], writes=[statB])
            return stat[0:R, c0 + 1:c0 + 2]

        def norm_stats(src_ap, srcB, R, in_place=None):
            if in_place is not None:
                xn, xnB = in_place
                rstd = rms_rstd(src_ap, srcB, R, sqj, sqjB)
            else:
                xn, xnB = get_stg()
                rstd = rms_rstd(src_ap, srcB, R, xn, xnB)
            S.op("act", lambda e: e.activation(out=xn[0:R, :], in_=src_ap, func=AF.Copy, scale=rstd),
                 reads=[srcB, statB], writes=[xnB])
            return xn, xnB

        def norm_gen(srcs, R, dst, dstB, vsc, vsh, sample, prefetch=None, n_early=2):
            nsub_ = len(srcs)
            staged = {}
            xns = {}
            if prefetch is not None:
                for q in range(min(NSTG - 1, nsub_)):
                    staged[q] = prefetch(q)

            def do_stats(q):
                if prefetch is not None:
                    xbuf, xB_ = staged.pop(q)
                    return norm_stats(xbuf[0:R, :], xB_, R, in_place=(xbuf, xB_))
                src_ap, srcB = srcs[q]()
                return norm_stats(src_ap, srcB, R)

            for q in range(min(n_early, nsub_)):
                xns[q] = do_stats(q)
            mark = rr["stg"]
            yield
            assert rr["stg"] == mark, "staging ring used between the two halves of a norm"
            banks = [get_ps() for _ in range(NCH)]
            for q in range(nsub_):
                if q not in xns:
                    xns[q] = do_stats(q)
                xn, xnB = xns.pop(q)
                for c in range(NCH):
                    pt_, ptB_ = banks[c]
                    S.op("pe", lambda e, c=c, q=q, pt_=pt_, xn=xn: e.transpose(
                        out=pt_[:, q * 128:q * 128 + R], in_=xn[0:R, c * 128:(c + 1) * 128], identity=ident[0:R, 0:R]),
                        reads=[xnB, identB], writes=[ptB_])
                if prefetch is not None and q + NSTG - 1 < nsub_:
                    staged[q + NSTG - 1] = prefetch(q + NSTG - 1)
            W_ = (nsub_ - 1) * 128 + R
            for c in range(NCH):
                pt_, ptB_ = banks[c]
                if not sample:
                    if c % 2 == 0:
                        S.op("act", lambda e, c=c, pt_=pt_: e.activation(
                            out=dst[:, c, 0:W_], in_=pt_[:, 0:W_], func=AF.Identity,
                            bias=modT[:, vsh * 8 + c, 0:1], scale=modT[:, vsc * 8 + c, 0:1]),
                            reads=[ptB_, modB], writes=[dstB[c]])
                    else:
                        S.op("dve", lambda e, c=c, pt_=pt_: e.tensor_scalar(
                            out=dst[:, c, 0:W_], in0=pt_[:, 0:W_], scalar1=modT[:, vsc * 8 + c, 0:1],
                            scalar2=modT[:, vsh * 8 + c, 0:1], op0=ALU.mult, op1=ALU.add),
                            reads=[ptB_, modB], writes=[dstB[c]])
                else:
                    t_, tB_ = get_tmp()
                    S.op("dve", lambda e, c=c, pt_=pt_, t_=t_: e.tensor_tensor(
                        out=t_[:, 0:R].rearrange("p (n t) -> p n t", t=LS),
                        in0=pt_[:, 0:R].rearrange("p (n t) -> p n t", t=LS),
                        in1=modT[:, vsc * 8 + c, 1:17].unsqueeze(2).to_broadcast([128, NS, LS]), op=ALU.mult),
                        reads=[ptB_, modB], writes=[tB_])
                    S.op("dve", lambda e, c=c, t_=t_: e.tensor_tensor(
                        out=dst[:, c, 0:R].rearrange("p (n t) -> p n t", t=LS),
                        in0=t_[:, 0:R].rearrange("p (n t) -> p n t", t=LS),
                        in1=modT[:, vsh * 8 + c, 1:17].unsqueeze(2).to_broadcast([128, NS, LS]), op=ALU.add),
                        reads=[tB_, modB], writes=[dstB[c]])

        def fm_rows_out(src, srcBs, ncols, dst_dram_ap, key, col0=0):
            so, soB = get_stg()
            for hb in range(2):
                pt_, ptB_ = get_ps()
                for cc in range(4):
                    c = hb * 4 + cc
                    S.op("pe", lambda e, c=c, cc=cc, pt_=pt_: e.transpose(
                        out=pt_[0:ncols, cc * 128:(cc + 1) * 128], in_=src[:, c, col0:col0 + ncols], identity=ident[:, :]),
                        reads=[srcBs[c], identB], writes=[ptB_])
                S.op("act", lambda e, hb=hb, pt_=pt_: e.activation(out=so[0:ncols, hb * 512:(hb + 1) * 512],
                                                                  in_=pt_[0:ncols, :], func=AF.Copy),
                     reads=[ptB_], writes=[soB])
            S.op("sp", lambda e: e.dma_start(out=dst_dram_ap, in_=so[0:ncols, :]), reads=[soB], dma_key=key, out_dma=True)

        def make_tile(ti):
            sample = (ti == 4)
            if not sample:
                NT, nseq, L, nsub, R = 512, 1, 512, 4, 128
            else:
                NT, nseq, L, nsub, R = 64, NS, LS, 1, 64
            first = (ti == 0)
            last_prompt = (ti == 3)
            AW = HC + L
            UW = HP + L
            mcol = 0 if not sample else 1

            def aview(c):
                return aext[:, c, 0:nseq * AW].rearrange("p (n w) -> p n w", w=AW)

            def v3(ap2d):
                return ap2d.rearrange("p (n l) -> p n l", l=L)

            tpe = (CW if first else T_PE) if not sample else 0
            n32 = HC if last_prompt else (NT if sample else 0)
            grow = (lambda gi, lo, hi: gbc[0:R, gi, lo:hi])
            growB = gbcB

            def phase_A1():
                if sample:
                    for qq in range(4):
                        hs, hsB = get_stg()
                        S.op("sp", lambda e, qq=qq, hs=hs: e.dma_start(out=hs[0:120, :], in_=sconv[qq * 120:(qq + 1) * 120, :]),
                             writes=[hsB], dma_key=("hs", qq))
                        for hb in range(2):
                            pt_, ptB_ = get_ps()
                            for cc in range(4):
                                c = hb * 4 + cc
                                S.op("pe", lambda e, c=c, cc=cc, pt_=pt_, hs=hs: e.transpose(
                                    out=pt_[:, cc * 128:cc * 128 + 120], in_=hs[0:120, c * 128:(c + 1) * 128],
                                    identity=ident[0:120, 0:120]), reads=[hsB, identB], writes=[ptB_])
                            for cc in range(4):
                                c = hb * 4 + cc
                                S.op("act", lambda e, c=c, cc=cc, pt_=pt_, qq=qq: e.activation(
                                    out=aview(c)[:, qq * 4:(qq + 1) * 4, 0:HC],
                                    in_=pt_[:, cc * 128:cc * 128 + 120].rearrange("p (n j) -> p n j", j=HC), func=AF.Copy),
                                    reads=[ptB_], writes=[aB[c]])
                    for qq in range(2):
                        hs, hsB = get_stg()
                        S.op("sp", lambda e, qq=qq, hs=hs: e.dma_start(out=hs[0:120, :], in_=spool[qq * 120:(qq + 1) * 120, :]),
                             writes=[hsB], dma_key=("hp", qq))
                        for hb in range(2):
                            pt_, ptB_ = get_ps()
                            for cc in range(4):
                                c = hb * 4 + cc
                                S.op("pe", lambda e, c=c, cc=cc, pt_=pt_, hs=hs: e.transpose(
                                    out=pt_[:, cc * 128:cc * 128 + 120], in_=hs[0:120, c * 128:(c + 1) * 128],
                                    identity=ident[0:120, 0:120]), reads=[hsB, identB], writes=[ptB_])
                            for cc in range(4):
                                c = hb * 4 + cc
                                S.op("act", lambda e, c=c, cc=cc, pt_=pt_, qq=qq: e.activation(
                                    out=uhist[:, c, qq * 8:(qq + 1) * 8, :],
                                    in_=pt_[:, cc * 128:cc * 128 + 120].rearrange("p (n j) -> p n j", j=HP), func=AF.Copy),
                                    reads=[ptB_], writes=cfB[4:8])
                    S.op("sp", lambda e: e.dma_start(
                        out=ncs.rearrange("(n j) d -> n j d", j=HC)[:, 0:HC - LS, :],
                        in_=sconv.rearrange("(n j) d -> n j d", j=HC)[:, LS:HC, :]), dma_key=("o", "ncs0"), out_dma=True)
                    S.op("sp", lambda e: e.dma_start(
                        out=nps.rearrange("(n j) d -> n j d", j=HP)[:, 0:HP - LS, :],
                        in_=spool.rearrange("(n j) d -> n j d", j=HP)[:, LS:HP, :]), dma_key=("o", "nps0"), out_dma=True)


                def load_x(q):
                    src = xp[ti * 512 + q * 128: ti * 512 + (q + 1) * 128, :] if not sample else xs
                    xs_, xsB_ = get_stg()
                    S.op("act", lambda e: e.dma_start(out=xs_[0:R, :], in_=src), writes=[xsB_], dma_key=("xa", q))
                    return xs_, xsB_
                g_ = norm_gen([None] * nsub, R, h, hB, SC1, SH1, sample, prefetch=load_x)
                next(g_)
                yield
                for _ in g_:
                    pass

            def stage_S3():
                slotU = slotUB = None
                for j in range(NCH):
                    if j % 4 == 0:
                        slotU, slotUB, _, _ = use_block()
                    jj = j % 4
                    g = j // 2
                    w = 2 << g
                    pu, puB = get_ps()
                    for k in range(NCH):
                        S.op("pe", lambda e, jj=jj, k=k, pu=pu, slotU=slotU: e.matmul(
                            pu[:, 0:NT], lhsT=slotU[:, k * 512 + jj * 128:k * 512 + (jj + 1) * 128], rhs=h[:, k, 0:NT],
                            start=(k == 0), stop=(k == NCH - 1)), reads=[slotUB, hB[k]], writes=[puB])
                    ue, ueB = get_tmpp()
                    uev = ue[:, 0:nseq * UW].rearrange("p (n w) -> p n w", w=UW)
                    S.op("act", lambda e, pu=pu, uev=uev: e.activation(out=uev[:, :, HP:HP + L], in_=v3(pu[:, 0:NT]), func=AF.Copy),
                         reads=[puB], writes=[ueB])
                    if not sample:
                        S.op("pool", lambda e, j=j, uev=uev: e.tensor_copy(out=uev[:, 0, 0:HP], in_=utail[:, j, :]),
                             reads=[utB[j]], writes=[ueB])
                        S.op("pool", lambda e, j=j, uev=uev: e.tensor_copy(out=utail[:, j, :], in_=uev[:, 0, L:L + HP]),
                             reads=[ueB], writes=[utB[j]])
                        if last_prompt:
                            S.op("pool", lambda e, j=j, uev=uev: e.tensor_copy(out=u32[:, j, 0:HP], in_=uev[:, 0, L:L + HP]),
                                 reads=[ueB], writes=[u32B[j], repB])
                    else:
                        S.op("pool", lambda e, j=j, uev=uev: e.tensor_copy(out=uev[:, :, 0:HP], in_=uhist[:, j, :, :]),
                             reads=cfB[4:8], writes=[ueB])
                        S.op("pool", lambda e, j=j, uev=uev: e.tensor_copy(
                            out=u32[:, j, 0:NT].rearrange("p (t n) -> p n t", n=NS), in_=uev[:, :, HP:HP + L]),
                            reads=[ueB], writes=[u32B[j], repB])
                    prev, prevB = uev, ueB
                    d = 1
                    pp_bufs = []
                    lvl = 0
                    while d < w:
                        lo = 2 * d - 1
                        if lvl < 2:
                            pp_bufs.append(get_tmpp())
                        nx, nxB = pp_bufs[lvl % 2]
                        lvl += 1
                        nxv = nx[:, 0:nseq * UW].rearrange("p (n w) -> p n w", w=UW)
                        S.op("pool" if sample else "dve", lambda e, prev=prev, nxv=nxv, lo=lo, d=d: e.tensor_tensor(
                            out=nxv[:, :, lo:UW], in0=prev[:, :, lo:UW], in1=prev[:, :, lo - d:UW - d], op=ALU.add),
                            reads=[prevB], writes=[nxB])
                        prev, prevB = nxv, nxB
                        d *= 2
                    S.op("dve", lambda e, j=j, prev=prev, uev=uev, w=w: e.scalar_tensor_tensor(
                        out=v3(pooled[:, j, 0:NT]), in0=prev[:, :, HP:HP + L], scalar=1.0 / w, in1=uev[:, :, HP:HP + L],
                        op0=ALU.mult, op1=ALU.subtract), reads=[prevB, ueB], writes=[plB[j]])
                    if first:
                        fx, fxB = get_tmp()
                        S.op("dve", lambda e, prev=prev, fx=fx, g=g: e.tensor_tensor(
                            out=fx[:, 0:HP], in0=prev[:, 0, HP:2 * HP], in1=invc[:, g, 0:HP], op=ALU.mult),
                            reads=[prevB, invB], writes=[fxB])
                        S.op("dve", lambda e, j=j, fx=fx, uev=uev: e.tensor_tensor(
                            out=pooled[:, j, 0:HP], in0=fx[:, 0:HP], in1=uev[:, 0, HP:2 * HP], op=ALU.subtract),
                            reads=[fxB, ueB], writes=[plB[j]])


            def phase_A2():
                slotV = slotVB = slotG = slotGB = None
                for j in range(NCH):
                    if j % 4 == 0:
                        slotV, slotVB, _, _ = use_block()
                        slotG, slotGB, _, _ = use_block(hold_prev=True)
                    jj = j % 4
                    pv, pvB = get_ps()
                    pg, pgB = get_ps()
                    for k in range(NCH):
                        S.op("pe", lambda e, jj=jj, k=k, pv=pv, slotV=slotV: e.matmul(
                            pv[:, 0:NT], lhsT=slotV[:, k * 512 + jj * 128:k * 512 + (jj + 1) * 128], rhs=h[:, k, 0:NT],
                            start=(k == 0), stop=(k == NCH - 1)), reads=[slotVB, hB[k]], writes=[pvB])
                    for k in range(NCH):
                        S.op("pe", lambda e, jj=jj, k=k, pg=pg, slotG=slotG: e.matmul(
                            pg[:, 0:NT], lhsT=slotG[:, k * 512 + jj * 128:k * 512 + (jj + 1) * 128], rhs=h[:, k, 0:NT],
                            start=(k == 0), stop=(k == NCH - 1)), reads=[slotGB, hB[k]], writes=[pgB])
                    t_, tB_ = get_tmp()
                    S.op("act", lambda e, pg=pg, t_=t_: e.activation(out=t_[:, 0:NT], in_=pg[:, 0:NT], func=AF.Sigmoid),
                         reads=[pgB], writes=[tB_])
                    S.op("dve", lambda e, j=j, pv=pv, t_=t_: e.tensor_tensor(
                        out=aview(j)[:, :, HC:HC + L], in0=v3(pv[:, 0:NT]), in1=v3(t_[:, 0:NT]), op=ALU.mult),
                        reads=[pvB, tB_], writes=[aB[j]])
                    if n32 and not sample:
                        S.op("dve", lambda e, j=j, pv=pv, t_=t_: e.tensor_tensor(
                            out=a32[:, j, 0:n32], in0=pv[:, NT - n32:NT], in1=t_[:, NT - n32:NT], op=ALU.mult),
                            reads=[pvB, tB_], writes=[a32B[j], repB])
                    if sample:
                        S.op("dve", lambda e, j=j, pv=pv, t_=t_: e.tensor_tensor(
                            out=a32[:, j, 0:NT].rearrange("p (t n) -> p n t", n=NS), in0=v3(pv[:, 0:NT]), in1=v3(t_[:, 0:NT]),
                            op=ALU.mult), reads=[pvB, tB_], writes=[a32B[j], repB])

                if "after_S2" in hooks:
                    hooks["after_S2"]()
                if prev_norm2[0] is not None:
                    for _ in prev_norm2[0]():
                        pass

                if tpe > 0:
                    for j in range(NCH):
                        pc, pcB = get_ps()
                        for k in range(tpe):
                            dg, dgB_ = get_dg()
                            S.op("dve", lambda e, j=j, k=k, dg=dg: e.tensor_scalar(
                                out=dg, in0=identb[:, :], scalar1=cw[:, j, k:k + 1], scalar2=None, op0=ALU.mult),
                                reads=[identbB, cwB], writes=[dgB_])
                            S.op("pe", lambda e, j=j, k=k, pc=pc, dg=dg: e.matmul(
                                pc[:, 0:NT], lhsT=dg, rhs=aext[:, j, k:k + L],
                                start=(k == 0), stop=(k == tpe - 1)), reads=[dgB_, aB[j]], writes=[pcB])
                        S.op("act", lambda e, j=j, pc=pc: e.activation(
                            out=cf[:, j, 0:NT], in_=pc[:, 0:NT], func=AF.Identity, bias=vecs[:, CONV_B, j:j + 1], scale=1.0),
                            reads=[pcB, vecB], writes=[cfB[j]])


                if "after_convPE" in hooks:
                    hooks["after_convPE"]()
                if sample:
                    stage_S3()

            def conv_gen():
                for k in range(tpe, CW):
                    for j in range(NCH):
                        if k == 0:
                            S.op("dve", lambda e, j=j: e.tensor_scalar(
                                out=v3(cf[:, j, 0:NT]), in0=aview(j)[:, :, 0:L], scalar1=cw[:, j, 0:1],
                                scalar2=vecs[:, CONV_B, j:j + 1], op0=ALU.mult, op1=ALU.add),
                                reads=[aB[j], cwB, vecB], writes=[cfB[j]])
                        else:
                            S.op("dve", lambda e, j=j, k=k: e.scalar_tensor_tensor(
                                out=v3(cf[:, j, 0:NT]), in0=aview(j)[:, :, k:k + L], scalar=cw[:, j, k:k + 1],
                                in1=v3(cf[:, j, 0:NT]), op0=ALU.mult, op1=ALU.add),
                                reads=[aB[j], cwB, cfB[j]], writes=[cfB[j]])
                    yield
                if not sample and not last_prompt:
                    for j in range(NCH):
                        S.op("pool", lambda e, j=j: e.tensor_copy(out=aext[:, j, 0:HC], in_=aext[:, j, L:L + HC]),
                             reads=[aB[j]], writes=[aB[j]])

            def phase_B():
                if sample:
                    for gi in range(2):
                        S.op("sp", lambda e, gi=gi: e.dma_start(out=gbc[0:64, gi, :], in_=gscr[gi]),
                             reads=[gscrB[gi]], writes=[gbcB[gi]], dma_key=("gsi", gi))
                for q in range(nsub):
                    src = xp[ti * 512 + q * 128: ti * 512 + (q + 1) * 128, :] if not sample else xs
                    S.op("act", lambda e, q=q, src=src: e.dma_start(out=xres[0:R, q, :], in_=src),
                         writes=[xresB[q]], dma_key=("x", q))
                if last_prompt:
                    fm_rows_out(a32, a32B, HC, ncp, ("o", "ncp"))
                def ln_copy(j):
                    S.op("pool", lambda e, j=j: e.tensor_copy(out=s_t[:, j, 0:NT], in_=cf[:, j, 0:NT]),
                         reads=[cfB[j]], writes=[sB[j]])
                    S.op("pool", lambda e, j=j: e.tensor_tensor(out=m_t[:, j, 0:NT], in0=cf[:, j, 0:NT], in1=cf[:, j, 0:NT],
                                                                op=ALU.mult),
                         reads=[cfB[j]], writes=[mB[j]])
                for j in range(NCH):
                    ln_copy(j)
                a1_ = None
                if next_A1[0] is not None:
                    a1_ = next_A1[0]()
                    next(a1_)
                if not sample:
                    stage_S3()
                if last_prompt:
                    fm_rows_out(u32, u32B, HP, npp, ("o", "npp"))
                if "after_S3" in hooks:
                    hooks["after_S3"]()

                for gi in range(2):
                    slotX = slotXB = None
                    for j in range(NCH):
                        if j % 4 == 0:
                            slotX, slotXB, _, _ = use_block()
                        jj = j % 4
                        pq, pqB = get_ps()
                        for k in range(NCH):
                            S.op("pe", lambda e, jj=jj, k=k, pq=pq, slotX=slotX: e.matmul(
                                pq[:, 0:NT], lhsT=slotX[:, k * 512 + jj * 128:k * 512 + (jj + 1) * 128], rhs=h[:, k, 0:NT],
                                start=(k == 0), stop=(k == NCH - 1)), reads=[slotXB, hB[k]], writes=[pqB])
                        ci = gi * 8 + j
                        S.op("act", lambda e, ci=ci, pq=pq: e.activation(
                            out=sg32[:, ci * 512:ci * 512 + NT], in_=pq[:, 0:NT], func=AF.Sigmoid),
                            reads=[pqB], writes=[sgB[ci]])

                if a1_ is not None:
                    for _ in a1_:
                        pass
                if "after_S5" in hooks:
                    hooks["after_S5"]()

                pmean, pmeanB = get_ps()
                pe2, pe2B = get_ps()
                for j in range(NCH):
                    S.op("pe", lambda e, j=j: e.matmul(pmean[:, 0:NT], lhsT=onesD[:, :], rhs=s_t[:, j, 0:NT],
                                                       start=(j == 0), stop=(j == NCH - 1)),
                         reads=[onesB, sB[j]], writes=[pmeanB])
                    S.op("pe", lambda e, j=j: e.matmul(pe2[:, 0:NT], lhsT=onesD[:, :], rhs=m_t[:, j, 0:NT],
                                                       start=(j == 0), stop=(j == NCH - 1)),
                         reads=[onesB, mB[j]], writes=[pe2B])

                slots7 = {}

                def emit_ob(j):
                    slotPO, slotPOB = slots7["po"]
                    jj = j % 4
                    pb_, pbB_ = get_ps()
                    for k in range(NCH):
                        S.op("pe", lambda e, jj=jj, k=k, pb_=pb_: e.matmul(
                            pb_[:, 0:NT], lhsT=slotPO[:, k * 512 + jj * 128:k * 512 + (jj + 1) * 128], rhs=pooled[:, k, 0:NT],
                            start=(k == 0), stop=(k == NCH - 1)), reads=[slotPOB, plB[k]], writes=[pbB_])
                    t2, t2B = get_tmp()
                    S.op("dve", lambda e, j=j, pb_=pb_, t2=t2: e.tensor_tensor(
                        out=t2[:, 0:NT], in0=pb_[:, 0:NT], in1=sg32[:, (8 + j) * 512:(8 + j) * 512 + NT], op=ALU.mult),
                        reads=[pbB_, sgB[8 + j]], writes=[t2B])
                    return t2, t2B

                msq, msqB = get_tmp()
                rstd, rstdB = lnst[:, 0, :], lnB[0]
                nmr, nmrB = lnst[:, 1, :], lnB[1]
                S.op("act", lambda e: e.activation(out=msq[:, 0:NT], in_=pmean[:, 0:NT], func=AF.Square),
                     reads=[pmeanB], writes=[msqB])
                S.op("dve", lambda e: e.tensor_tensor(out=msq[:, 0:NT], in0=pe2[:, 0:NT], in1=msq[:, 0:NT], op=ALU.subtract),
                     reads=[pe2B, msqB], writes=[msqB])
                S.op("act", lambda e: e.activation(out=rstd[:, 0:NT], in_=msq[:, 0:NT], func=AF.Sqrt, bias=epst[:, 0:1], scale=1.0),
                     reads=[msqB, epsB], writes=[rstdB])
                S.op("dve", lambda e: e.reciprocal(out=rstd[:, 0:NT], in_=rstd[:, 0:NT]), reads=[rstdB], writes=[rstdB])
                S.op("dve", lambda e: e.scalar_tensor_tensor(out=nmr[:, 0:NT], in0=pmean[:, 0:NT], scalar=-1.0, in1=rstd[:, 0:NT],
                                                             op0=ALU.mult, op1=ALU.mult),
                     reads=[pmeanB, rstdB], writes=[nmrB])
                for j in range(NCH):
                    z, zB = get_tmp()
                    S.op("dve", lambda e, j=j, z=z: e.tensor_tensor(out=z[:, 0:NT], in0=cf[:, j, 0:NT], in1=rstd[:, 0:NT], op=ALU.mult),
                         reads=[cfB[j], rstdB], writes=[zB])
                    S.op("dve", lambda e, z=z: e.tensor_tensor(out=z[:, 0:NT], in0=z[:, 0:NT], in1=nmr[:, 0:NT], op=ALU.add),
                         reads=[zB, nmrB], writes=[zB])
                    S.op("act", lambda e, j=j, z=z: e.activation(
                        out=s_t[:, j, 0:NT], in_=z[:, 0:NT], func=AF.Silu, bias=vecs[:, LN_B, j:j + 1], scale=vecs[:, LN_G, j:j + 1]),
                        reads=[zB, vecB], writes=[sB[j]])

                for g in range(4):
                    pps = []
                    for jj in range(2):
                        pp, ppB = get_ps()
                        pps.append((pp, ppB))
                        for kk in range(2):
                            S.op("pe", lambda e, g=g, jj=jj, kk=kk, pp=pp: e.matmul(
                                pp[:, 0:NT], lhsT=pmx[:, g, kk, jj * 128:(jj + 1) * 128], rhs=pooled[:, 2 * g + kk, 0:NT],
                                start=(kk == 0), stop=(kk == 1)), reads=[pmxB, plB[2 * g + kk]], writes=[ppB])
                    for jj in range(2):
                        pp, ppB = pps[jj]
                        j = 2 * g + jj
                        S.op("act", lambda e, j=j, pp=pp: e.activation(
                            out=pooled[:, j, 0:NT], in_=pp[:, 0:NT], func=AF.Copy, scale=vecs[:, PSC, j:j + 1]),
                            reads=[ppB, vecB], writes=[plB[j]])

                for j in range(NCH):
                    if j % 4 == 0:
                        a_, b_, _, _ = use_block(); slots7["po"] = (a_, b_)
                        a_, b_, _, _ = use_block(hold_prev=True); slots7["co"] = (a_, b_)
                    slotCO, slotCOB = slots7["co"]
                    jj = j % 4
                    t2, t2B = emit_ob(j)
                    pa_, paB_ = get_ps()
                    for k in range(NCH):
                        S.op("pe", lambda e, jj=jj, k=k, pa_=pa_, slotCO=slotCO: e.matmul(
                            pa_[:, 0:NT], lhsT=slotCO[:, k * 512 + jj * 128:k * 512 + (jj + 1) * 128], rhs=s_t[:, k, 0:NT],
                            start=(k == 0), stop=(k == NCH - 1)), reads=[slotCOB, sB[k]], writes=[paB_])
                    t1, t1B = get_tmp()
                    S.op("dve", lambda e, j=j, pa_=pa_, t1=t1: e.tensor_tensor(
                        out=t1[:, 0:NT], in0=pa_[:, 0:NT], in1=sg32[:, j * 512:j * 512 + NT], op=ALU.mult),
                        reads=[paB_, sgB[j]], writes=[t1B])
                    S.op("dve", lambda e, j=j, t1=t1, t2=t2: e.tensor_tensor(
                        out=m_t[:, j, 0:NT], in0=t1[:, 0:NT], in1=t2[:, 0:NT], op=ALU.add),
                        reads=[t1B, t2B], writes=[mB[j]])

                for f in range(2):
                    slotO, slotOB, _, _ = use_block()
                    for q in range(nsub):
                        po, poB = get_ps()
                        for k in range(NCH):
                            S.op("pe", lambda e, q=q, k=k, po=po, slotO=slotO: e.matmul(
                                po[0:R, :], lhsT=m_t[:, k, q * 128:q * 128 + R], rhs=slotO[:, k * 512:(k + 1) * 512],
                                start=(k == 0), stop=(k == NCH - 1)), reads=[slotOB, mB[k]], writes=[poB])
                        t_, tB_ = get_tmp()
                        S.op("dve", lambda e, f=f, po=po, t_=t_: e.tensor_tensor(
                            out=t_[0:R, 0:512], in0=po[0:R, :], in1=grow(0, f * 512, (f + 1) * 512), op=ALU.mult),
                            reads=[poB, growB[0]], writes=[tB_])
                        S.op("pool", lambda e, q=q, f=f, t_=t_: e.tensor_tensor(
                            out=xres[0:R, q, f * 512:(f + 1) * 512], in0=xres[0:R, q, f * 512:(f + 1) * 512], in1=t_[0:R, 0:512], op=ALU.add),
                            reads=[tB_, xresB[q]], writes=[xresB[q]])
                n2_ = norm2()
                if not last_prompt:
                    next(n2_)
                if defer_norm2[0] is None:
                    for _ in n2_:
                        pass
                else:
                    defer_norm2[0] = n2_


            next_A1 = [None]
            hooks = {}
            defer_norm2 = [None]
            prev_norm2 = [None]

            def norm2():
                return norm_gen([(lambda q=q: (xres[0:R, q, :], xresB[q])) for q in range(nsub)], R, s_t, sB, SC2, SH2, sample)

            def r2deps(j):
                return [sgB[j // 2]]

            def F_gen():
                for jb in range(8):
                    slotF, slotFB, _, _ = use_block()
                    for jj in range(4):
                        j = jb * 4 + jj
                        pf, pfB = get_ps()
                        for k in range(NCH):
                            S.op("pe", lambda e, jj=jj, k=k, pf=pf, slotF=slotF: e.matmul(
                                pf[:, 0:NT], lhsT=slotF[:, k * 512 + jj * 128:k * 512 + (jj + 1) * 128], rhs=s_t[:, k, 0:NT],
                                start=(k == 0), stop=(k == NCH - 1)), reads=[slotFB, sB[k]], writes=[pfB])
                        S.op("act", lambda e, pf=pf: e.activation(out=pf[:, 0:NT], in_=pf[:, 0:NT], func=AF.Relu),
                             reads=[pfB], writes=[pfB])
                        S.op("act", lambda e, j=j, pf=pf: e.activation(
                            out=r2f[:, j * 512:j * 512 + NT], in_=pf[:, 0:NT], func=AF.Square),
                            reads=[pfB], writes=r2deps(j))
                    yield

                for fq in range(4):
                    pws = [get_ps() for _ in range(nsub)]
                    for kh in range(2):
                        slotW, slotWB, _, _ = use_block()
                        for q in range(nsub):
                            pw, pwB = pws[q]
                            for kk in range(16):
                                k = kh * 16 + kk
                                S.op("pe", lambda e, q=q, k=k, kk=kk, pw=pw, slotW=slotW: e.matmul(
                                    pw[0:R, 0:256], lhsT=r2f[:, k * 512 + q * 128:k * 512 + q * 128 + R],
                                    rhs=slotW[:, kk * 256:(kk + 1) * 256],
                                    start=(k == 0), stop=(k == 31)), reads=[slotWB] + r2deps(k), writes=[pwB])
                    for q in range(nsub):
                        pw, pwB = pws[q]
                        t_, tB_ = get_tmp()
                        S.op("dve", lambda e, fq=fq, pw=pw, t_=t_: e.tensor_tensor(
                            out=t_[0:R, 0:256], in0=pw[0:R, 0:256], in1=grow(1, fq * 256, (fq + 1) * 256), op=ALU.mult),
                            reads=[pwB, growB[1]], writes=[tB_])
                        S.op("pool", lambda e, q=q, fq=fq, t_=t_: e.tensor_tensor(
                            out=xres[0:R, q, fq * 256:(fq + 1) * 256], in0=xres[0:R, q, fq * 256:(fq + 1) * 256],
                            in1=t_[0:R, 0:256], op=ALU.add), reads=[tB_, xresB[q]], writes=[xresB[q]])
                    yield

                for q in range(nsub):
                    yo, yoB = get_stg()
                    rstd_ = rms_rstd(xres[0:R, q, :], xresB[q], R, yo, yoB)
                    S.op("dve", lambda e, q=q, yo=yo, rstd_=rstd_: e.scalar_tensor_tensor(
                        out=yo[0:R, :], in0=xres[0:R, q, :], scalar=rstd_, in1=fgbc[0:R, :], op0=ALU.mult, op1=ALU.mult),
                        reads=[xresB[q], statB, fgB], writes=[yoB])
                    if not sample:
                        dst = yp[ti * 512 + q * 128: ti * 512 + (q + 1) * 128, :]
                    else:
                        dst = ys
                    S.op("sp", lambda e, yo=yo, dst=dst: e.dma_start(out=dst, in_=yo[0:R, :]), reads=[yoB],
                         dma_key=("y", ti, q), out_dma=True)


                if sample:
                    for t in range(LS):
                        fm_rows_out(a32, a32B, NS, ncs.rearrange("(n j) d -> n j d", j=HC)[:, HC - LS + t, :],
                                    ("o", "ncs1", t), col0=t * NS)
                        fm_rows_out(u32, u32B, NS, nps.rearrange("(n j) d -> n j d", j=HP)[:, HP - LS + t, :],
                                    ("o", "nps1", t), col0=t * NS)
                yield

            return phase_A1, phase_A2, conv_gen, phase_B, F_gen, next_A1, hooks, defer_norm2, prev_norm2, norm2

        tiles = [make_tile(ti) for ti in range(NTILES)]

        def drain(g):
            for _ in g:
                pass

        for ti in range(NTILES - 2):
            tiles[ti][5][0] = tiles[ti + 1][0]
        for ti in range(NTILES - 1):
            tiles[ti][7][0] = True
            tiles[ti + 1][8][0] = (lambda ti=ti: tiles[ti][7][0])
        tiles[0][6]["after_S2"] = lambda: mod_v(2)
        tiles[0][6]["after_convPE"] = lambda: mod_v(3)
        tiles[0][6]["after_S3"] = lambda: mod_v(4)
        tiles[0][6]["after_S5"] = lambda: mod_v(5)
        drain(tiles[0][0]())
        tiles[0][1]()
        drain(tiles[0][2]())
        tiles[0][3]()
        for ti in range(1, NTILES):
            if ti == NTILES - 1:
                drain(tiles[ti][0]())
            tiles[ti][1]()
            cg = tiles[ti][2]()
            fg_ = tiles[ti - 1][4]()
            step = 0
            n_rounds = CW - (T_PE if ti < 4 else 0)
            done_r = 0
            for _ in fg_:
                step += 1
                tgt = min(n_rounds, (n_rounds * step + 9) // 10)
                while done_r < tgt:
                    next(cg, None)
                    done_r += 1
            drain(cg)
            tiles[ti][3]()
        drain(tiles[NTILES - 1][4]())

        S.emit(st)
    return nc


_NC_CACHE = {}


def kernel(x_prompt, x_sample, state_conv, state_pool, c_prompt, c_sample, w_ada, b_ada, w_in,
           conv_w, conv_b, ln_g, ln_b, w_conv_out, pool_mix, pool_scale, w_pool_out, w_out,
           w_ff1, w_ff2, final_g):
    f = lambda a: np.ascontiguousarray(np.asarray(a, dtype=np.float32))
    x_prompt, x_sample, state_conv, state_pool = f(x_prompt), f(x_sample), f(state_conv), f(state_pool)
    c_prompt, c_sample = f(c_prompt), f(c_sample)
    if "nc" not in _NC_CACHE:
        _NC_CACHE["nc"] = build_nc()
    nc = _NC_CACHE["nc"]

    def fm(vec):
        return f(np.asarray(vec).reshape(NCH, 128).T)

    shared = {
        "w_ada": f(w_ada[0]), "b_ada": f(np.asarray(b_ada[0]).reshape(1, 6 * D)),
        "b_adaT": f(np.asarray(b_ada[0]).reshape(48, 128).T),
        "w_in": f(w_in[0]),
        "cwT": f(np.asarray(conv_w[0]).T.reshape(NCH, 128, CW).transpose(1, 0, 2).reshape(128, NCH * CW)),
        "vecT": f(np.concatenate([fm(conv_b[0]), fm(ln_g[0]), fm(ln_b[0]), fm(pool_scale[0])], axis=1)),
        "w_co": f(w_conv_out[0]), "pmix": f(pool_mix[0]), "w_po": f(w_pool_out[0]), "w_o": f(w_out[0]),
        "w_f1": f(w_ff1[0]), "w_f2": f(w_ff2[0]), "fg": f(np.asarray(final_g).reshape(1, D)),
    }
    in_maps = []
    for i in range(NCORES):
        m = dict(shared)
        m["xp"] = x_prompt[i]
        m["xs"] = x_sample[i * NS:(i + 1) * NS].reshape(NS * LS, D)
        m["sconv"] = state_conv[0, i * NS:(i + 1) * NS].reshape(NS * HC, D)
        m["spool"] = state_pool[0, i * NS:(i + 1) * NS].reshape(NS * HP, D)
        m["cvec"] = f(np.concatenate([c_prompt[i:i + 1], c_sample[i * NS:(i + 1) * NS]], axis=0))
        in_maps.append(m)
    res = run_bass_kernel_spmd(nc, in_maps, core_ids=list(range(NCORES)))
    rs = res.results
    y_prompt = np.stack([rs[i]["yp"] for i in range(NCORES)], axis=0)
    y_sample = np.concatenate([rs[i]["ys"].reshape(NS, LS, D) for i in range(NCORES)], axis=0)
    ncp_ = np.stack([rs[i]["ncp"] for i in range(NCORES)], axis=0)[None]
    npp_ = np.stack([rs[i]["npp"] for i in range(NCORES)], axis=0)[None]
    ncs_ = np.concatenate([rs[i]["ncs"].reshape(NS, HC, D) for i in range(NCORES)], axis=0)[None]
    nps_ = np.concatenate([rs[i]["nps"].reshape(NS, HP, D) for i in range(NCORES)], axis=0)[None]
    return (y_prompt.astype(np.float32), y_sample.astype(np.float32), ncp_.astype(np.float32),
            npp_.astype(np.float32), ncs_.astype(np.float32), nps_.astype(np.float32))
```

```python
import numpy as np
from contextlib import ExitStack
import concourse.bass as bass
import concourse.mybir as mybir
from concourse.bass_utils import run_bass_kernel_spmd

F32 = mybir.dt.float32
BF16 = mybir.dt.bfloat16
AF = mybir.ActivationFunctionType
ALU = mybir.AluOpType

D = 1024
NCH = 8
SEQ = 2048
NS = 16
LS = 4
HC = 30
HP = 15
CW = 31
EPS = 1e-6
NCORES = 8

T_PE = 12
RING = 5

ENGS = ("pe", "act", "dve", "pool", "sp")
EPOCH = 16000
SAFE_DIST = 4
STRICT_SAME_ENGINE = True


class Buf:
    __slots__ = ("name", "last_w", "readers")

    def __init__(self, name):
        self.name = name
        self.last_w = None
        self.readers = []


class Ins:
    __slots__ = ("eng", "fn", "deps", "sig", "cnt", "dma_key", "dma_val", "is_dma", "idx")


class Sched:
    def __init__(self, nc):
        self.nc = nc
        self.q = {e: [] for e in ENGS}
        self.dma_cnt = {}
        self.all_dma_out = []

    def op(self, eng, fn, reads=(), writes=(), dma_key=None, out_dma=False):
        ins = Ins()
        ins.eng = eng
        ins.fn = fn
        ins.sig = False
        ins.cnt = 0
        ins.is_dma = dma_key is not None
        ins.dma_key = dma_key
        ins.dma_val = 0
        ins.idx = len(self.q[eng])
        if ins.is_dma:
            v = self.dma_cnt.get(dma_key, 0) + 16
            self.dma_cnt[dma_key] = v
            ins.dma_val = v
        deps = {}
        for b in reads:
            if b.last_w is not None:
                deps[id(b.last_w)] = (b.last_w, True)
        for b in writes:
            if b.last_w is not None and id(b.last_w) not in deps:
                deps[id(b.last_w)] = (b.last_w, False)
            for r in b.readers:
                if id(r) not in deps:
                    deps[id(r)] = (r, False)
        final = []
        for d, raw in deps.values():
            if (not d.is_dma) and (not ins.is_dma) and d.eng == eng:
                if eng == "pe":
                    continue
                if not STRICT_SAME_ENGINE:
                    if not raw or ins.idx - d.idx >= SAFE_DIST:
                        continue
            final.append(d)
        ins.deps = final
        for b in reads:
            b.readers.append(ins)
        for b in writes:
            b.last_w = ins
            b.readers = []
        self.q[eng].append(ins)
        if out_dma:
            self.all_dma_out.append(ins)
        return ins

    def emit(self, stack):
        nc = self.nc
        fin = Ins()
        fin.eng = "sp"; fin.fn = None; fin.sig = False; fin.cnt = 0
        fin.is_dma = False; fin.dma_key = None; fin.dma_val = 0; fin.idx = len(self.q["sp"])
        fin.deps = list(self.all_dma_out)
        self.q["sp"].append(fin)
        for e in ENGS:
            for ins in self.q[e]:
                for d in ins.deps:
                    d.sig = True
        nsig = {}
        for e in ENGS:
            c = 0
            for ins in self.q[e]:
                if ins.sig and not ins.is_dma:
                    c += 1
                    ins.cnt = c
            nsig[e] = c
        esems = {}
        for e in ENGS:
            n = (nsig[e] + EPOCH - 1) // EPOCH
            esems[e] = [stack.enter_context(nc.semaphore(f"s_{e}_{i}")) for i in range(max(n, 1))]
        dsems = {}
        for k in self.dma_cnt:
            dsems[k] = stack.enter_context(nc.semaphore("d_" + "_".join(str(x) for x in k)))

        def signal_of(d):
            if d.is_dma:
                return ("d", d.dma_key), dsems[d.dma_key], d.dma_val
            ep = (d.cnt - 1) // EPOCH
            return ("e", d.eng, ep), esems[d.eng][ep], d.cnt - ep * EPOCH

        block = stack.enter_context(nc.Block())
        reg = {"pe": block.tensor, "act": block.scalar, "dve": block.vector,
               "pool": block.gpsimd, "sp": block.sync}
        for e in ENGS:
            qe = self.q[e]

            def body(eng, qe=qe, e=e):
                waited = {}
                maxep = {}
                for ins in qe:
                    for d in ins.deps:
                        key, sem, val = signal_of(d)
                        if key[0] == "e":
                            if maxep.get(key[1], -1) > key[2]:
                                continue
                        if waited.get(key, 0) < val:
                            eng.wait_ge(sem, val)
                            waited[key] = val
                            if key[0] == "e":
                                maxep[key[1]] = max(maxep.get(key[1], -1), key[2])
                    if ins.fn is None:
                        continue
                    bi = ins.fn(eng)
                    if ins.is_dma:
                        bi.then_inc(dsems[ins.dma_key], 16)
                    elif ins.sig:
                        ep = (ins.cnt - 1) // EPOCH
                        bi.then_inc(esems[e][ep], 1)

            reg[e](body)


def build_nc():
    nc = bass.Bass("TRN2", target_bir_lowering=False)

    def din(name, shape):
        return nc.dram_tensor(name, list(shape), F32, kind="ExternalInput").ap()

    def dout(name, shape):
        return nc.dram_tensor(name, list(shape), F32, kind="ExternalOutput").ap()

    xp = din("xp", [SEQ, D])
    xs = din("xs", [NS * LS, D])
    sconv = din("sconv", [NS * HC, D])
    spool = din("spool", [NS * HP, D])
    cvec = din("cvec", [NS + 1, D])
    w_ada = din("w_ada", [D, 6 * D])
    b_ada = din("b_ada", [1, 6 * D])
    b_adaT = din("b_adaT", [128, 48])
    w_in = din("w_in", [D, 5 * D])
    cwT = din("cwT", [128, NCH * CW])
    vecT = din("vecT", [128, 4 * NCH])
    w_co = din("w_co", [D, D])
    pmix = din("pmix", [4, 256, 256])
    w_po = din("w_po", [D, D])
    w_o = din("w_o", [D, D])
    w_f1 = din("w_f1", [D, 4 * D])
    w_f2 = din("w_f2", [4 * D, D])
    fg = din("fg", [1, D])

    yp = dout("yp", [SEQ, D])
    ys = dout("ys", [NS * LS, D])
    ncp = dout("ncp", [HC, D])
    npp = dout("npp", [HP, D])
    ncs = dout("ncs", [NS * HC, D])
    nps = dout("nps", [NS * HP, D])

    with ExitStack() as st:
        S = Sched(nc)

        def sb(name, shape, dt=F32):
            return st.enter_context(nc.sbuf_tensor(name, list(shape), dt))

        ring = [sb(f"ring{i}", [128, 4096], BF16) for i in range(RING)]
        ringB = [Buf(f"ring{i}") for i in range(RING)]
        pmx = sb("pmx", [128, 4, 2, 256], BF16); pmxB = Buf("pmx")
        xres = sb("xres", [128, 4, D]); xresB = [Buf(f"xres{q}") for q in range(4)]
        NSTG = 3
        stg = [sb(f"stg{i}", [128, D]) for i in range(NSTG)]; stgB = [Buf(f"stg{i}") for i in range(NSTG)]
        h = sb("h", [128, NCH, 512], BF16); hB = [Buf(f"h{c}") for c in range(NCH)]
        aext = sb("aext", [128, NCH, 544], BF16); aB = [Buf(f"a{c}") for c in range(NCH)]
        NTMP = 4
        tmp = [sb(f"tmp{i}", [128, 512]) for i in range(NTMP)]; tmpB = [Buf(f"tmp{i}") for i in range(NTMP)]
        NTMPP = 4
        tmpp = [sb(f"tmpp{i}", [128, 544]) for i in range(NTMPP)]; tmppB = [Buf(f"tmpp{i}") for i in range(NTMPP)]
        NTB = 0
        tmb = [sb(f"tmb{i}", [128, 512], BF16) for i in range(NTB)]; tmbB = [Buf(f"tmb{i}") for i in range(NTB)]
        utail = sb("utail", [128, NCH, HP]); utB = [Buf(f"ut{c}") for c in range(NCH)]
        pooled = sb("pooled", [128, NCH, 512], BF16); plB = [Buf(f"pl{c}") for c in range(NCH)]
        r2f = sb("r2f", [128, 32 * 512], BF16)
        sg32 = r2f[:].bitcast(F32)
        sgB = [Buf(f"sg{c}") for c in range(16)]
        cf = sb("cf", [128, NCH, 512]); cfB = [Buf(f"cf{c}") for c in range(NCH)]
        badd = cf[:].rearrange("p c n -> p (c n)")[:, 0:2 * D].rearrange("p (g d) -> p g d", d=D)
        lnst = sb("lnst", [128, 2, 512]); lnB = [Buf("ln1"), Buf("ln2")]
        sqj = sb("sqj", [128, D], BF16); sqjB = Buf("sqj")
        uhist = cf[:].rearrange("p c n -> p (c n)")[:, 2 * D:2 * D + NCH * NS * HP].rearrange("p (c n j) -> p c n j", n=NS, j=HP)
        uhB = [None] * NCH
        s_t = sb("s_t", [128, NCH, 512], BF16); sB = [Buf(f"s{c}") for c in range(NCH)]
        m_t = sb("m_t", [128, NCH, 512], BF16); mB = [Buf(f"m{c}") for c in range(NCH)]
        gbc = sb("gbc", [128, 2, D]); gbcB = [Buf("g1bc"), Buf("g2bc")]
        gsr = gbc; gsrB = gbcB
        fgbc = sb("fgbc", [128, D]); fgB = Buf("fgbc")
        a32 = sb("a32", [128, NCH, 64]); a32B = [Buf(f"a32_{c}") for c in range(NCH)]
        u32 = sb("u32", [128, NCH, 64]); u32B = [Buf(f"u32_{c}") for c in range(NCH)]
        ident = sb("ident", [128, 128]); identB = Buf("ident")
        identb = sb("identb", [128, 128], BF16); identbB = Buf("identb")
        onesD = sb("onesD", [128, 128], BF16); onesB = Buf("onesD")
        epst = sb("epst", [128, 1]); epsB = Buf("eps")
        cw = sb("cw", [128, NCH, CW]); vecs = sb("vecs", [128, 4, NCH]); cwB = Buf("cw"); vecB = Buf("vecs")
        bT = sb("bT", [128, 48]); bTB = Buf("bT")
        modT = sb("modT", [128, 48, NS + 1]); modB = Buf("modT")
        scT32 = sb("scT32", [128, NCH, NS + 1]); scT = sb("scT", [128, NCH, NS + 1], BF16); scB = Buf("scT")
        rep_p = a32[:].rearrange("p c n -> p (c n)").bitcast(BF16).rearrange("p (c n) -> p c n", n=128)
        rep_s = u32[:].rearrange("p c n -> p (c n)").bitcast(BF16)[:, 0:NCH * 64].rearrange("p (c n) -> p c n", n=64)
        repB = Buf("rep")
        invc = sb("invc", [128, 4, 16]); invB = Buf("invc")
        stat = sb("stat", [128, 16]); statB = Buf("stat")
        NDG = 7
        dgp = sb("dgp", [128, NDG, 128], BF16); dgB = [Buf(f"dg{i}") for i in range(NDG)]
        psum = [st.enter_context(nc.psum_tensor(f"ps{i}", [128, 512], F32)) for i in range(8)]
        psB = [Buf(f"ps{i}") for i in range(8)]

        rr = {"ps": 0, "tmp": 0, "tmb": 0, "stg": 0, "stat": 0, "dg": 0, "tmpp": 0}

        def get_tmpp():
            i = rr["tmpp"]; rr["tmpp"] = (i + 1) % NTMPP
            return tmpp[i], tmppB[i]

        def get_dg():
            i = rr["dg"]; rr["dg"] = (i + 1) % NDG
            return dgp[:, i, :], dgB[i]

        def get_ps():
            i = rr["ps"]; rr["ps"] = (i + 1) % 8
            return psum[i], psB[i]

        def get_tmp():
            i = rr["tmp"]; rr["tmp"] = (i + 1) % NTMP
            return tmp[i], tmpB[i]

        def get_tmb():
            i = rr["tmb"]; rr["tmb"] = (i + 1) % NTB
            return tmb[i], tmbB[i]

        def get_stg():
            i = rr["stg"]; rr["stg"] = (i + 1) % NSTG
            return stg[i], stgB[i]

        def get_stat():
            i = rr["stat"]; rr["stat"] = (i + 1) % 8
            return i * 2

        NBT = 32
        wscr = nc.dram_tensor("wscr", [NBT, 128, 4096], BF16).ap()
        scrB = [Buf(f"scr{i}") for i in range(NBT)]
        gscr = nc.dram_tensor("gscr", [2, 64, D], F32).ap()
        gscrB = [Buf("gscr0"), Buf("gscr1")]
        scr_written = set()
        scr_uses = {}
        pending_wo = {}

        def wblock(W, nk, r0, c0, ncols, sid=None):
            return (W, nk, r0, c0, ncols, sid)

        blk_state = {"n": 0, "issued": 0, "plan": []}

        def issue_block(bi):
            W, nk, r0, c0, ncols, sid = blk_state["plan"][bi]
            slot = bi % RING
            if sid is not None and sid in scr_written:
                S.op("sp", lambda e: e.dma_start(out=ring[slot][:, :], in_=wscr[sid]),
                     reads=[scrB[sid]], writes=[ringB[slot]], dma_key=("wh", slot))
                return
            src = W[r0:r0 + nk * 128, c0:c0 + ncols].rearrange("(k p) c -> p k c", p=128)
            dst = ring[slot][:, 0:nk * ncols].rearrange("p (k c) -> p k c", c=ncols)
            S.op("pool", lambda e: e.dma_start(out=dst, in_=src), writes=[ringB[slot]], dma_key=("w", slot))
            if sid is not None:
                if scr_uses.get(sid, 0) == sid % 3:
                    pending_wo[bi] = (sid, slot)
                scr_uses[sid] = scr_uses.get(sid, 0) + 1

        def use_block(hold_prev=False):
            bi = blk_state["n"]
            blk_state["n"] += 1
            if bi in pending_wo:
                sid_, slot_ = pending_wo.pop(bi)
                S.op("sp", lambda e: e.dma_start(out=wscr[sid_], in_=ring[slot_][:, :]),
                     reads=[ringB[slot_]], writes=[scrB[sid_]], dma_key=("wo", slot_))
                scr_written.add(sid_)
            depth = RING - 1 if hold_prev else RING
            while blk_state["issued"] < min(bi + depth, len(blk_state["plan"])):
                issue_block(blk_state["issued"])
                blk_state["issued"] += 1
            W, nk, r0, c0, ncols, sid = blk_state["plan"][bi]
            slot = bi % RING
            return ring[slot], ringB[slot], nk, ncols

        def halves(W, c0):
            return [wblock(W, 8, 0, c0, 512), wblock(W, 8, 0, c0 + 512, 512)]

        def tile_plan():
            p = []
            for hf in range(2):
                p.append(wblock(w_in, 8, 0, hf * 512, 512))
                p.append(wblock(w_in, 8, 0, 1024 + hf * 512, 512))
            p += halves(w_in, 2048)
            p += halves(w_in, 3072)
            p += halves(w_in, 4096)
            for hf in range(2):
                p.append(wblock(w_po, 8, 0, hf * 512, 512))
                p.append(wblock(w_co, 8, 0, hf * 512, 512))
            p += halves(w_o, 0)
            for b in range(8):
                p.append(wblock(w_f1, 8, 0, b * 512, 512))
            for b in range(4):
                p.append(wblock(w_f2, 16, 0, b * 256, 256))
                p.append(wblock(w_f2, 16, 2048, b * 256, 256))
            return p

        def gs_plan():
            return halves(w_ada, 2 * D) + halves(w_ada, 5 * D)

        def with_sid(blocks, sid0):
            return [wblock(b[0], b[1], b[2], b[3], b[4], sid0 + i) for i, b in enumerate(blocks)]

        def plan_A(sample_=False):
            p = []
            for hf in range(2):
                p.append(wblock(w_in, 8, 0, hf * 512, 512))
                p.append(wblock(w_in, 8, 0, 1024 + hf * 512, 512))
            if sample_:
                return with_sid(p, 0) + with_sid(halves(w_in, 2048), 4)
            return with_sid(p, 0)

        def plan_B(sample_):
            p = halves(w_in, 2048) + halves(w_in, 3072) + halves(w_in, 4096)
            nskip = 2 if sample_ else 0
            for hf in range(2):
                p.append(wblock(w_po, 8, 0, hf * 512, 512))
                p.append(wblock(w_co, 8, 0, hf * 512, 512))
            p = with_sid(p + halves(w_o, 0), 4)
            return p[nskip:]

        def plan_F():
            p = [wblock(w_f1, 8, 0, b * 512, 512) for b in range(8)]
            for b in range(4):
                p.append(wblock(w_f2, 16, 0, b * 256, 256))
                p.append(wblock(w_f2, 16, 2048, b * 256, 256))
            return with_sid(p, 16)

        plan = []
        for v in range(2):
            plan += halves(w_ada, v * 1024)
        NTILES = 5
        pa_, pb_ = plan_A(), plan_B(False)
        plan += pa_[0:4] + halves(w_ada, 2 * 1024) + halves(w_ada, 3 * 1024)
        plan += pb_[0:2] + halves(w_ada, 4 * 1024) + pb_[2:6] + halves(w_ada, 5 * 1024) + pb_[6:]
        for ti_ in range(1, NTILES):
            plan += plan_A(ti_ == 4) + plan_F() + plan_B(ti_ == 4)
        plan += plan_F()
        blk_state["plan"] = plan

        S.op("pool", lambda e: e.memset(ident[:], 0.0), writes=[identB])
        S.op("pool", lambda e: e.affine_select(out=ident[:], in_=ident[:], pattern=[[-1, 128]],
                                               compare_op=ALU.not_equal, fill=1.0, base=0, channel_multiplier=1),
             reads=[identB], writes=[identB])
        S.op("dve", lambda e: e.tensor_copy(out=identb[:], in_=ident[:]), reads=[identB], writes=[identbB])
        S.op("dve", lambda e: e.memset(onesD[:], 1.0 / D), writes=[onesB])
        S.op("dve", lambda e: e.memset(epst[:], EPS), writes=[epsB])
        S.op("dve", lambda e: e.memset(utail[:], 0.0), writes=utB)
        for c in range(NCH):
            S.op("dve", lambda e, c=c: e.memset(aext[:, c, 0:HC], 0.0), writes=[aB[c]])
        for g in range(4):
            w = 2 << g
            S.op("dve", lambda e, g=g, w=w: e.memset(invc[:, g, :], 1.0 / w), writes=[invB])
            for t in range(w - 1):
                S.op("dve", lambda e, g=g, t=t: e.memset(invc[:, g, t:t + 1], 1.0 / (t + 1)), writes=[invB])
        S.op("sp", lambda e: e.dma_start(out=cw[:].rearrange("p c k -> p (c k)"), in_=cwT), writes=[cwB], dma_key=("c", 0))
        S.op("sp", lambda e: e.dma_start(out=vecs[:].rearrange("p v c -> p (v c)"), in_=vecT), writes=[vecB], dma_key=("c", 1))
        S.op("sp", lambda e: e.dma_start(out=bT[:], in_=b_adaT), writes=[bTB], dma_key=("c", 2))
        c_full, cB = get_stg()
        c_sb = c_full[0:NS + 1, :]
        S.op("sp", lambda e: e.dma_start(out=c_sb, in_=cvec), writes=[cB], dma_key=("c", 3))
        S.op("sp", lambda e: e.dma_start(out=fgbc[:], in_=fg.partition_broadcast(128)), writes=[fgB], dma_key=("c", 4))
        CONV_B, LN_G, LN_B, PSC = 0, 1, 2, 3
        S.op("act", lambda e: e.activation(out=c_sb, in_=c_sb, func=AF.Silu), reads=[cB], writes=[cB])
        pt, ptB = get_ps()
        for k in range(NCH):
            S.op("pe", lambda e, k=k: e.transpose(out=pt[:, k * 17:(k + 1) * 17], in_=c_full[0:17, k * 128:(k + 1) * 128],
                                                  identity=ident[0:17, 0:17]),
                 reads=[cB, identB], writes=[ptB])
        S.op("dve", lambda e: e.tensor_copy(out=scT32[:].rearrange("p k n -> p (k n)"), in_=pt[:, 0:136]),
             reads=[ptB], writes=[scB])
        S.op("dve", lambda e: e.tensor_copy(out=scT[:].rearrange("p k n -> p (k n)"), in_=scT32[:].rearrange("p k n -> p (k n)")),
             reads=[scB], writes=[scB])
        for k in range(NCH):
            S.op("dve", lambda e, k=k: e.tensor_copy(out=rep_p[:, k, :], in_=scT32[:, k, 0:1].to_broadcast([128, 128])),
                 reads=[scB], writes=[repB])
            S.op("dve", lambda e, k=k: e.tensor_copy(out=rep_s[:, k, :].rearrange("p (n t) -> p n t", t=LS),
                                                     in_=scT32[:, k, 1:17].unsqueeze(2).to_broadcast([128, NS, LS])),
                 reads=[scB], writes=[repB])
        def g_rows(gi, f, slot, slotB, sample_rows, badd_ap=None, baddBufs=None, dst=None, dstBufs=None):
            pg, pgB = get_ps()
            R_ = 64 if sample_rows else 128
            lhs = rep_s if sample_rows else rep_p
            for k in range(NCH):
                S.op("pe", lambda e, k=k, pg=pg: e.matmul(
                    pg[0:R_, :], lhsT=lhs[:, k, :], rhs=slot[:, k * 512:(k + 1) * 512],
                    start=(k == 0), stop=(k == NCH - 1)), reads=[slotB, repB], writes=[pgB])
            S.op("dve", lambda e, pg=pg: e.tensor_tensor(
                out=(gbc[0:R_, gi, f * 512:(f + 1) * 512] if dst is None else dst), in0=pg[0:R_, :],
                in1=(badd[0:R_, gi, f * 512:(f + 1) * 512] if badd_ap is None else badd_ap), op=ALU.add),
                reads=[pgB] + (cfB[0:4] if baddBufs is None else baddBufs),
                writes=([gbcB[gi]] if dstBufs is None else dstBufs))

        def load_badd():
            S.op("sp", lambda e: e.dma_start(out=badd[:, 0, :], in_=b_ada[:, 2 * D:3 * D].partition_broadcast(128)),
                 writes=cfB[0:4], dma_key=("c", 5))
            S.op("sp", lambda e: e.dma_start(out=badd[:, 1, :], in_=b_ada[:, 5 * D:6 * D].partition_broadcast(128)),
                 writes=cfB[0:4], dma_key=("c", 6))

        def mod_v(v):
                pm_, pmB_ = get_ps()
                for hf in range(2):
                    slot, slotB, nk, ncols = use_block()
                    for cc in range(4):
                        c = hf * 4 + cc
                        for k in range(NCH):
                            S.op("pe", lambda e, c=c, cc=cc, k=k, slot=slot, pm_=pm_: e.matmul(
                                pm_[:, c * 17:(c + 1) * 17], lhsT=slot[:, k * 512 + cc * 128:k * 512 + (cc + 1) * 128],
                                rhs=scT[:, k, :], start=(k == 0), stop=(k == NCH - 1)),
                                reads=[slotB, scB], writes=[pmB_])
                    if v in (2, 5):
                        if hf == 0:
                            bs_, bsB_ = get_stg()
                            S.op("sp", lambda e, bs_=bs_, v=v: e.dma_start(
                                out=bs_[:, :], in_=b_ada[:, v * D:(v + 1) * D].partition_broadcast(128)),
                                writes=[bsB_], dma_key=("bs", 0 if v == 2 else 1))
                        gi_ = 0 if v == 2 else 1
                        g_rows(gi_, hf, slot, slotB, False, bs_[:, hf * 512:(hf + 1) * 512], [bsB_])
                        if hf == 0:
                            gs_, gsB_ = get_stg()
                        g_rows(gi_, hf, slot, slotB, True, bs_[0:64, hf * 512:(hf + 1) * 512], [bsB_],
                               dst=gs_[0:64, hf * 512:(hf + 1) * 512], dstBufs=[gsB_])
                        if hf == 1:
                            S.op("sp", lambda e, gs_=gs_, gi_=gi_: e.dma_start(out=gscr[gi_], in_=gs_[0:64, :]),
                                 reads=[gsB_], writes=[gscrB[gi_]], dma_key=("gso", gi_))
                S.op("dve", lambda e, v=v, pm_=pm_: e.tensor_tensor(
                    out=modT[:, v * 8:(v + 1) * 8, :], in0=pm_[:, 0:136].rearrange("p (c n) -> p c n", n=17),
                    in1=bT[:, v * 8:(v + 1) * 8].unsqueeze(2).to_broadcast([128, 8, 17]), op=ALU.add),
                    reads=[pmB_, bTB], writes=[modB])
                if v in (1, 4):
                    S.op("dve", lambda e, v=v: e.tensor_scalar(out=modT[:, v * 8:(v + 1) * 8, :], in0=modT[:, v * 8:(v + 1) * 8, :],
                                                              scalar1=1.0, scalar2=None, op0=ALU.add),
                         reads=[modB], writes=[modB])

        mod_v(0)
        mod_v(1)
        for g in range(4):
            S.op("pool", lambda e, g=g: e.dma_start(out=pmx[:, g, :, :],
                                                    in_=pmix[g].rearrange("(k p) c -> p k c", p=128)),
                 writes=[pmxB], dma_key=("pm", g))

        def mod_rest():
            for v in range(2, 6):
                mod_v(v)

        SH1, SC1, G1, SH2, SC2, G2 = range(6)

        def rms_rstd(src_ap, srcB, R, junk, junkB):
            c0 = get_stat()
            S.op("act", lambda e: e.activation(out=junk[0:R, :], in_=src_ap, func=AF.Square, scale=1.0 / 32.0,
                                               accum_out=stat[0:R, c0:c0 + 1]),
                 reads=[srcB], writes=[junkB, statB])
            S.op("act", lambda e: e.activation(out=stat[0:R, c0 + 1:c0 + 2], in_=stat[0:R, c0:c0 + 1], func=AF.Sqrt,
                                               bias=epst[0:R, 0:1], scale=1.0),
                 reads=[statB, epsB], writes=[statB])
            S.op("dve", lambda e: e.reciprocal(out=stat[0:R, c0 + 1:c0 + 2], in_=stat[0:R, c0 + 1:c0 + 2]),
                 reads=[statB], writes=[statB])
            return stat[0:R, c0 + 1:c0 + 2]

        def norm_stats(src_ap, srcB, R, in_place=None):
            if in_place is not None:
                xn, xnB = in_place
                rstd = rms_rstd(src_ap, srcB, R, sqj, sqjB)
            else:
                xn, xnB = get_stg()
                rstd = rms_rstd(src_ap, srcB, R, xn, xnB)
            S.op("act", lambda e: e.activation(out=xn[0:R, :], in_=src_ap, func=AF.Copy, scale=rstd),
                 reads=[srcB, statB], writes=[xnB])
            return xn, xnB

        def norm_gen(srcs, R, dst, dstB, vsc, vsh, sample, prefetch=None, n_early=2):
            nsub_ = len(srcs)
            staged = {}
            xns = {}
            if prefetch is not None:
                for q in range(min(NSTG - 1, nsub_)):
                    staged[q] = prefetch(q)

            def do_stats(q):
                if prefetch is not None:
                    xbuf, xB_ = staged.pop(q)
                    return norm_stats(xbuf[0:R, :], xB_, R, in_place=(xbuf, xB_))
                src_ap, srcB = srcs[q]()
                return norm_stats(src_ap, srcB, R)

            for q in range(min(n_early, nsub_)):
                xns[q] = do_stats(q)
            mark = rr["stg"]
            yield
            assert rr["stg"] == mark, "staging ring used between the two halves of a norm"
            banks = [get_ps() for _ in range(NCH)]
            for q in range(nsub_):
                if q not in xns:
                    xns[q] = do_stats(q)
                xn, xnB = xns.pop(q)
                for c in range(NCH):
                    pt_, ptB_ = banks[c]
                    S.op("pe", lambda e, c=c, q=q, pt_=pt_, xn=xn: e.transpose(
                        out=pt_[:, q * 128:q * 128 + R], in_=xn[0:R, c * 128:(c + 1) * 128], identity=ident[0:R, 0:R]),
                        reads=[xnB, identB], writes=[ptB_])
                if prefetch is not None and q + NSTG - 1 < nsub_:
                    staged[q + NSTG - 1] = prefetch(q + NSTG - 1)
            W_ = (nsub_ - 1) * 128 + R
            for c in range(NCH):
                pt_, ptB_ = banks[c]
                if not sample:
                    if c % 2 == 0:
                        S.op("act", lambda e, c=c, pt_=pt_: e.activation(
                            out=dst[:, c, 0:W_], in_=pt_[:, 0:W_], func=AF.Identity,
                            bias=modT[:, vsh * 8 + c, 0:1], scale=modT[:, vsc * 8 + c, 0:1]),
                            reads=[ptB_, modB], writes=[dstB[c]])
                    else:
                        S.op("dve", lambda e, c=c, pt_=pt_: e.tensor_scalar(
                            out=dst[:, c, 0:W_], in0=pt_[:, 0:W_], scalar1=modT[:, vsc * 8 + c, 0:1],
                            scalar2=modT[:, vsh * 8 + c, 0:1], op0=ALU.mult, op1=ALU.add),
                            reads=[ptB_, modB], writes=[dstB[c]])
                else:
                    t_, tB_ = get_tmp()
                    S.op("dve", lambda e, c=c, pt_=pt_, t_=t_: e.tensor_tensor(
                        out=t_[:, 0:R].rearrange("p (n t) -> p n t", t=LS),
                        in0=pt_[:, 0:R].rearrange("p (n t) -> p n t", t=LS),
                        in1=modT[:, vsc * 8 + c, 1:17].unsqueeze(2).to_broadcast([128, NS, LS]), op=ALU.mult),
                        reads=[ptB_, modB], writes=[tB_])
                    S.op("dve", lambda e, c=c, t_=t_: e.tensor_tensor(
                        out=dst[:, c, 0:R].rearrange("p (n t) -> p n t", t=LS),
                        in0=t_[:, 0:R].rearrange("p (n t) -> p n t", t=LS),
                        in1=modT[:, vsh * 8 + c, 1:17].unsqueeze(2).to_broadcast([128, NS, LS]), op=ALU.add),
                        reads=[tB_, modB], writes=[dstB[c]])

        def fm_rows_out(src, srcBs, ncols, dst_dram_ap, key, col0=0):
            so, soB = get_stg()
            for hb in range(2):
                pt_, ptB_ = get_ps()
                for cc in range(4):
                    c = hb * 4 + cc
                    S.op("pe", lambda e, c=c, cc=cc, pt_=pt_: e.transpose(
                        out=pt_[0:ncols, cc * 128:(cc + 1) * 128], in_=src[:, c, col0:col0 + ncols], identity=ident[:, :]),
                        reads=[srcBs[c], identB], writes=[ptB_])
                S.op("act", lambda e, hb=hb, pt_=pt_: e.activation(out=so[0:ncols, hb * 512:(hb + 1) * 512],
                                                                  in_=pt_[0:ncols, :], func=AF.Copy),
                     reads=[ptB_], writes=[soB])
            S.op("sp", lambda e: e.dma_start(out=dst_dram_ap, in_=so[0:ncols, :]), reads=[soB], dma_key=key, out_dma=True)

        def make_tile(ti):
            sample = (ti == 4)
            if not sample:
                NT, nseq, L, nsub, R = 512, 1, 512, 4, 128
            else:
                NT, nseq, L, nsub, R = 64, NS, LS, 1, 64
            first = (ti == 0)
            last_prompt = (ti == 3)
            AW = HC + L
            UW = HP + L
            mcol = 0 if not sample else 1

            def aview(c):
                return aext[:, c, 0:nseq * AW].rearrange("p (n w) -> p n w", w=AW)

            def v3(ap2d):
                return ap2d.rearrange("p (n l) -> p n l", l=L)

            tpe = (CW if first else T_PE) if not sample else 0
            n32 = HC if last_prompt else (NT if sample else 0)
            grow = (lambda gi, lo, hi: gbc[0:R, gi, lo:hi])
            growB = gbcB

            def phase_A1():
                if sample:
                    for qq in range(4):
                        hs, hsB = get_stg()
                        S.op("sp", lambda e, qq=qq, hs=hs: e.dma_start(out=hs[0:120, :], in_=sconv[qq * 120:(qq + 1) * 120, :]),
                             writes=[hsB], dma_key=("hs", qq))
                        for hb in range(2):
                            pt_, ptB_ = get_ps()
                            for cc in range(4):
                                c = hb * 4 + cc
                                S.op("pe", lambda e, c=c, cc=cc, pt_=pt_, hs=hs: e.transpose(
                                    out=pt_[:, cc * 128:cc * 128 + 120], in_=hs[0:120, c * 128:(c + 1) * 128],
                                    identity=ident[0:120, 0:120]), reads=[hsB, identB], writes=[ptB_])
                            for cc in range(4):
                                c = hb * 4 + cc
                                S.op("act", lambda e, c=c, cc=cc, pt_=pt_, qq=qq: e.activation(
                                    out=aview(c)[:, qq * 4:(qq + 1) * 4, 0:HC],
                                    in_=pt_[:, cc * 128:cc * 128 + 120].rearrange("p (n j) -> p n j", j=HC), func=AF.Copy),
                                    reads=[ptB_], writes=[aB[c]])
                    for qq in range(2):
                        hs, hsB = get_stg()
                        S.op("sp", lambda e, qq=qq, hs=hs: e.dma_start(out=hs[0:120, :], in_=spool[qq * 120:(qq + 1) * 120, :]),
                             writes=[hsB], dma_key=("hp", qq))
                        for hb in range(2):
                            pt_, ptB_ = get_ps()
                            for cc in range(4):
                                c = hb * 4 + cc
                                S.op("pe", lambda e, c=c, cc=cc, pt_=pt_, hs=hs: e.transpose(
                                    out=pt_[:, cc * 128:cc * 128 + 120], in_=hs[0:120, c * 128:(c + 1) * 128],
                                    identity=ident[0:120, 0:120]), reads=[hsB, identB], writes=[ptB_])
                            for cc in range(4):
                                c = hb * 4 + cc
                                S.op("act", lambda e, c=c, cc=cc, pt_=pt_, qq=qq: e.activation(
                                    out=uhist[:, c, qq * 8:(qq + 1) * 8, :],
                                    in_=pt_[:, cc * 128:cc * 128 + 120].rearrange("p (n j) -> p n j", j=HP), func=AF.Copy),
                                    reads=[ptB_], writes=cfB[4:8])
                    S.op("sp", lambda e: e.dma_start(
                        out=ncs.rearrange("(n j) d -> n j d", j=HC)[:, 0:HC - LS, :],
                        in_=sconv.rearrange("(n j) d -> n j d", j=HC)[:, LS:HC, :]), dma_key=("o", "ncs0"), out_dma=True)
                    S.op("sp", lambda e: e.dma_start(
                        out=nps.rearrange("(n j) d -> n j d", j=HP)[:, 0:HP - LS, :],
                        in_=spool.rearrange("(n j) d -> n j d", j=HP)[:, LS:HP, :]), dma_key=("o", "nps0"), out_dma=True)


                def load_x(q):
                    src = xp[ti * 512 + q * 128: ti * 512 + (q + 1) * 128, :] if not sample else xs
                    xs_, xsB_ = get_stg()
                    S.op("act", lambda e: e.dma_start(out=xs_[0:R, :], in_=src), writes=[xsB_], dma_key=("xa", q))
                    return xs_, xsB_
                g_ = norm_gen([None] * nsub, R, h, hB, SC1, SH1, sample, prefetch=load_x)
                next(g_)
                yield
                for _ in g_:
                    pass

            def stage_S3():
                slotU = slotUB = None
                for j in range(NCH):
                    if j % 4 == 0:
                        slotU, slotUB, _, _ = use_block()
                    jj = j % 4
                    g = j // 2
                    w = 2 << g
                    pu, puB = get_ps()
                    for k in range(NCH):
                        S.op("pe", lambda e, jj=jj, k=k, pu=pu, slotU=slotU: e.matmul(
                            pu[:, 0:NT], lhsT=slotU[:, k * 512 + jj * 128:k * 512 + (jj + 1) * 128], rhs=h[:, k, 0:NT],
                            start=(k == 0), stop=(k == NCH - 1)), reads=[slotUB, hB[k]], writes=[puB])
                    ue, ueB = get_tmpp()
                    uev = ue[:, 0:nseq * UW].rearrange("p (n w) -> p n w", w=UW)
                    S.op("act", lambda e, pu=pu, uev=uev: e.activation(out=uev[:, :, HP:HP + L], in_=v3(pu[:, 0:NT]), func=AF.Copy),
                         reads=[puB], writes=[ueB])
                    if not sample:
                        S.op("pool", lambda e, j=j, uev=uev: e.tensor_copy(out=uev[:, 0, 0:HP], in_=utail[:, j, :]),
                             reads=[utB[j]], writes=[ueB])
                        S.op("pool", lambda e, j=j, uev=uev: e.tensor_copy(out=utail[:, j, :], in_=uev[:, 0, L:L + HP]),
                             reads=[ueB], writes=[utB[j]])
                        if last_prompt:
                            S.op("pool", lambda e, j=j, uev=uev: e.tensor_copy(out=u32[:, j, 0:HP], in_=uev[:, 0, L:L + HP]),
                                 reads=[ueB], writes=[u32B[j], repB])
                    else:
                        S.op("pool", lambda e, j=j, uev=uev: e.tensor_copy(out=uev[:, :, 0:HP], in_=uhist[:, j, :, :]),
                             reads=cfB[4:8], writes=[ueB])
                        S.op("pool", lambda e, j=j, uev=uev: e.tensor_copy(
                            out=u32[:, j, 0:NT].rearrange("p (t n) -> p n t", n=NS), in_=uev[:, :, HP:HP + L]),
                            reads=[ueB], writes=[u32B[j], repB])
                    prev, prevB = uev, ueB
                    d = 1
                    pp_bufs = []
                    lvl = 0
                    while d < w:
                        lo = 2 * d - 1
                        if lvl < 2:
                            pp_bufs.append(get_tmpp())
                        nx, nxB = pp_bufs[lvl % 2]
                        lvl += 1
                        nxv = nx[:, 0:nseq * UW].rearrange("p (n w) -> p n w", w=UW)
                        S.op("pool" if sample else "dve", lambda e, prev=prev, nxv=nxv, lo=lo, d=d: e.tensor_tensor(
                            out=nxv[:, :, lo:UW], in0=prev[:, :, lo:UW], in1=prev[:, :, lo - d:UW - d], op=ALU.add),
                            reads=[prevB], writes=[nxB])
                        prev, prevB = nxv, nxB
                        d *= 2
                    S.op("dve", lambda e, j=j, prev=prev, uev=uev, w=w: e.scalar_tensor_tensor(
                        out=v3(pooled[:, j, 0:NT]), in0=prev[:, :, HP:HP + L], scalar=1.0 / w, in1=uev[:, :, HP:HP + L],
                        op0=ALU.mult, op1=ALU.subtract), reads=[prevB, ueB], writes=[plB[j]])
                    if first:
                        fx, fxB = get_tmp()
                        S.op("dve", lambda e, prev=prev, fx=fx, g=g: e.tensor_tensor(
                            out=fx[:, 0:HP], in0=prev[:, 0, HP:2 * HP], in1=invc[:, g, 0:HP], op=ALU.mult),
                            reads=[prevB, invB], writes=[fxB])
                        S.op("dve", lambda e, j=j, fx=fx, uev=uev: e.tensor_tensor(
                            out=pooled[:, j, 0:HP], in0=fx[:, 0:HP], in1=uev[:, 0, HP:2 * HP], op=ALU.subtract),
                            reads=[fxB, ueB], writes=[plB[j]])


            def phase_A2():
                slotV = slotVB = slotG = slotGB = None
                for j in range(NCH):
                    if j % 4 == 0:
                        slotV, slotVB, _, _ = use_block()
                        slotG, slotGB, _, _ = use_block(hold_prev=True)
                    jj = j % 4
                    pv, pvB = get_ps()
                    pg, pgB = get_ps()
                    for k in range(NCH):
                        S.op("pe", lambda e, jj=jj, k=k, pv=pv, slotV=slotV: e.matmul(
                            pv[:, 0:NT], lhsT=slotV[:, k * 512 + jj * 128:k * 512 + (jj + 1) * 128], rhs=h[:, k, 0:NT],
                            start=(k == 0), stop=(k == NCH - 1)), reads=[slotVB, hB[k]], writes=[pvB])
                    for k in range(NCH):
                        S.op("pe", lambda e, jj=jj, k=k, pg=pg, slotG=slotG: e.matmul(
                            pg[:, 0:NT], lhsT=slotG[:, k * 512 + jj * 128:k * 512 + (jj + 1) * 128], rhs=h[:, k, 0:NT],
                            start=(k == 0), stop=(k == NCH - 1)), reads=[slotGB, hB[k]], writes=[pgB])
                    t_, tB_ = get_tmp()
                    S.op("act", lambda e, pg=pg, t_=t_: e.activation(out=t_[:, 0:NT], in_=pg[:, 0:NT], func=AF.Sigmoid),
                         reads=[pgB], writes=[tB_])
                    S.op("dve", lambda e, j=j, pv=pv, t_=t_: e.tensor_tensor(
                        out=aview(j)[:, :, HC:HC + L], in0=v3(pv[:, 0:NT]), in1=v3(t_[:, 0:NT]), op=ALU.mult),
                        reads=[pvB, tB_], writes=[aB[j]])
                    if n32 and not sample:
                        S.op("dve", lambda e, j=j, pv=pv, t_=t_: e.tensor_tensor(
                            out=a32[:, j, 0:n32], in0=pv[:, NT - n32:NT], in1=t_[:, NT - n32:NT], op=ALU.mult),
                            reads=[pvB, tB_], writes=[a32B[j], repB])
                    if sample:
                        S.op("dve", lambda e, j=j, pv=pv, t_=t_: e.tensor_tensor(
                            out=a32[:, j, 0:NT].rearrange("p (t n) -> p n t", n=NS), in0=v3(pv[:, 0:NT]), in1=v3(t_[:, 0:NT]),
                            op=ALU.mult), reads=[pvB, tB_], writes=[a32B[j], repB])

                if "after_S2" in hooks:
                    hooks["after_S2"]()
                if prev_norm2[0] is not None:
                    for _ in prev_norm2[0]():
                        pass

                if tpe > 0:
                    for j in range(NCH):
                        pc, pcB = get_ps()
                        for k in range(tpe):
                            dg, dgB_ = get_dg()
                            S.op("dve", lambda e, j=j, k=k, dg=dg: e.tensor_scalar(
                                out=dg, in0=identb[:, :], scalar1=cw[:, j, k:k + 1], scalar2=None, op0=ALU.mult),
                                reads=[identbB, cwB], writes=[dgB_])
                            S.op("pe", lambda e, j=j, k=k, pc=pc, dg=dg: e.matmul(
                                pc[:, 0:NT], lhsT=dg, rhs=aext[:, j, k:k + L],
                                start=(k == 0), stop=(k == tpe - 1)), reads=[dgB_, aB[j]], writes=[pcB])
                        S.op("act", lambda e, j=j, pc=pc: e.activation(
                            out=cf[:, j, 0:NT], in_=pc[:, 0:NT], func=AF.Identity, bias=vecs[:, CONV_B, j:j + 1], scale=1.0),
                            reads=[pcB, vecB], writes=[cfB[j]])


                if "after_convPE" in hooks:
                    hooks["after_convPE"]()
                if sample:
                    stage_S3()

            def conv_gen():
                for k in range(tpe, CW):
                    for j in range(NCH):
                        if k == 0:
                            S.op("dve", lambda e, j=j: e.tensor_scalar(
                                out=v3(cf[:, j, 0:NT]), in0=aview(j)[:, :, 0:L], scalar1=cw[:, j, 0:1],
                                scalar2=vecs[:, CONV_B, j:j + 1], op0=ALU.mult, op1=ALU.add),
                                reads=[aB[j], cwB, vecB], writes=[cfB[j]])
                        else:
                            S.op("dve", lambda e, j=j, k=k: e.scalar_tensor_tensor(
                                out=v3(cf[:, j, 0:NT]), in0=aview(j)[:, :, k:k + L], scalar=cw[:, j, k:k + 1],
                                in1=v3(cf[:, j, 0:NT]), op0=ALU.mult, op1=ALU.add),
                                reads=[aB[j], cwB, cfB[j]], writes=[cfB[j]])
                    yield
                if not sample and not last_prompt:
                    for j in range(NCH):
                        S.op("pool", lambda e, j=j: e.tensor_copy(out=aext[:, j, 0:HC], in_=aext[:, j, L:L + HC]),
                             reads=[aB[j]], writes=[aB[j]])

            def phase_B():
                if sample:
                    for gi in range(2):
                        S.op("sp", lambda e, gi=gi: e.dma_start(out=gbc[0:64, gi, :], in_=gscr[gi]),
                             reads=[gscrB[gi]], writes=[gbcB[gi]], dma_key=("gsi", gi))
                for q in range(nsub):
                    src = xp[ti * 512 + q * 128: ti * 512 + (q + 1) * 128, :] if not sample else xs
                    S.op("act", lambda e, q=q, src=src: e.dma_start(out=xres[0:R, q, :], in_=src),
                         writes=[xresB[q]], dma_key=("x", q))
                if last_prompt:
                    fm_rows_out(a32, a32B, HC, ncp, ("o", "ncp"))
                def ln_copy(j):
                    S.op("pool", lambda e, j=j: e.tensor_copy(out=s_t[:, j, 0:NT], in_=cf[:, j, 0:NT]),
                         reads=[cfB[j]], writes=[sB[j]])
                    S.op("act", lambda e, j=j: e.activation(out=m_t[:, j, 0:NT], in_=cf[:, j, 0:NT], func=AF.Square),
                         reads=[cfB[j]], writes=[mB[j]])
                for j in range(NCH):
                    ln_copy(j)
                a1_ = None
                if next_A1[0] is not None:
                    a1_ = next_A1[0]()
                    next(a1_)
                if not sample:
                    stage_S3()
                if last_prompt:
                    fm_rows_out(u32, u32B, HP, npp, ("o", "npp"))
                if "after_S3" in hooks:
                    hooks["after_S3"]()

                for gi in range(2):
                    slotX = slotXB = None
                    for j in range(NCH):
                        if j % 4 == 0:
                            slotX, slotXB, _, _ = use_block()
                        jj = j % 4
                        pq, pqB = get_ps()
                        for k in range(NCH):
                            S.op("pe", lambda e, jj=jj, k=k, pq=pq, slotX=slotX: e.matmul(
                                pq[:, 0:NT], lhsT=slotX[:, k * 512 + jj * 128:k * 512 + (jj + 1) * 128], rhs=h[:, k, 0:NT],
                                start=(k == 0), stop=(k == NCH - 1)), reads=[slotXB, hB[k]], writes=[pqB])
                        ci = gi * 8 + j
                        S.op("act", lambda e, ci=ci, pq=pq: e.activation(
                            out=sg32[:, ci * 512:ci * 512 + NT], in_=pq[:, 0:NT], func=AF.Sigmoid),
                            reads=[pqB], writes=[sgB[ci]])

                if a1_ is not None:
                    for _ in a1_:
                        pass
                if "after_S5" in hooks:
                    hooks["after_S5"]()

                pmean, pmeanB = get_ps()
                pe2, pe2B = get_ps()
                for j in range(NCH):
                    S.op("pe", lambda e, j=j: e.matmul(pmean[:, 0:NT], lhsT=onesD[:, :], rhs=s_t[:, j, 0:NT],
                                                       start=(j == 0), stop=(j == NCH - 1)),
                         reads=[onesB, sB[j]], writes=[pmeanB])
                    S.op("pe", lambda e, j=j: e.matmul(pe2[:, 0:NT], lhsT=onesD[:, :], rhs=m_t[:, j, 0:NT],
                                                       start=(j == 0), stop=(j == NCH - 1)),
                         reads=[onesB, mB[j]], writes=[pe2B])

                slots7 = {}

                def emit_ob(j):
                    slotPO, slotPOB = slots7["po"]
                    jj = j % 4
                    pb_, pbB_ = get_ps()
                    for k in range(NCH):
                        S.op("pe", lambda e, jj=jj, k=k, pb_=pb_: e.matmul(
                            pb_[:, 0:NT], lhsT=slotPO[:, k * 512 + jj * 128:k * 512 + (jj + 1) * 128], rhs=pooled[:, k, 0:NT],
                            start=(k == 0), stop=(k == NCH - 1)), reads=[slotPOB, plB[k]], writes=[pbB_])
                    t2, t2B = get_tmp()
                    S.op("dve", lambda e, j=j, pb_=pb_, t2=t2: e.tensor_tensor(
                        out=t2[:, 0:NT], in0=pb_[:, 0:NT], in1=sg32[:, (8 + j) * 512:(8 + j) * 512 + NT], op=ALU.mult),
                        reads=[pbB_, sgB[8 + j]], writes=[t2B])
                    return t2, t2B

                msq, msqB = get_tmp()
                rstd, rstdB = lnst[:, 0, :], lnB[0]
                nmr, nmrB = lnst[:, 1, :], lnB[1]
                S.op("act", lambda e: e.activation(out=msq[:, 0:NT], in_=pmean[:, 0:NT], func=AF.Square),
                     reads=[pmeanB], writes=[msqB])
                S.op("dve", lambda e: e.tensor_tensor(out=msq[:, 0:NT], in0=pe2[:, 0:NT], in1=msq[:, 0:NT], op=ALU.subtract),
                     reads=[pe2B, msqB], writes=[msqB])
                S.op("act", lambda e: e.activation(out=rstd[:, 0:NT], in_=msq[:, 0:NT], func=AF.Sqrt, bias=epst[:, 0:1], scale=1.0),
                     reads=[msqB, epsB], writes=[rstdB])
                S.op("dve", lambda e: e.reciprocal(out=rstd[:, 0:NT], in_=rstd[:, 0:NT]), reads=[rstdB], writes=[rstdB])
                S.op("dve", lambda e: e.scalar_tensor_tensor(out=nmr[:, 0:NT], in0=pmean[:, 0:NT], scalar=-1.0, in1=rstd[:, 0:NT],
                                                             op0=ALU.mult, op1=ALU.mult),
                     reads=[pmeanB, rstdB], writes=[nmrB])
                for j in range(NCH):
                    z, zB = get_tmp()
                    S.op("dve", lambda e, j=j, z=z: e.tensor_tensor(out=z[:, 0:NT], in0=cf[:, j, 0:NT], in1=rstd[:, 0:NT], op=ALU.mult),
                         reads=[cfB[j], rstdB], writes=[zB])
                    S.op("dve", lambda e, z=z: e.tensor_tensor(out=z[:, 0:NT], in0=z[:, 0:NT], in1=nmr[:, 0:NT], op=ALU.add),
                         reads=[zB, nmrB], writes=[zB])
                    S.op("act", lambda e, j=j, z=z: e.activation(
                        out=s_t[:, j, 0:NT], in_=z[:, 0:NT], func=AF.Silu, bias=vecs[:, LN_B, j:j + 1], scale=vecs[:, LN_G, j:j + 1]),
                        reads=[zB, vecB], writes=[sB[j]])

                for g in range(4):
                    pps = []
                    for jj in range(2):
                        pp, ppB = get_ps()
                        pps.append((pp, ppB))
                        for kk in range(2):
                            S.op("pe", lambda e, g=g, jj=jj, kk=kk, pp=pp: e.matmul(
                                pp[:, 0:NT], lhsT=pmx[:, g, kk, jj * 128:(jj + 1) * 128], rhs=pooled[:, 2 * g + kk, 0:NT],
                                start=(kk == 0), stop=(kk == 1)), reads=[pmxB, plB[2 * g + kk]], writes=[ppB])
                    for jj in range(2):
                        pp, ppB = pps[jj]
                        j = 2 * g + jj
                        S.op("act", lambda e, j=j, pp=pp: e.activation(
                            out=pooled[:, j, 0:NT], in_=pp[:, 0:NT], func=AF.Copy, scale=vecs[:, PSC, j:j + 1]),
                            reads=[ppB, vecB], writes=[plB[j]])

                for j in range(NCH):
                    if j % 4 == 0:
                        a_, b_, _, _ = use_block(); slots7["po"] = (a_, b_)
                        a_, b_, _, _ = use_block(hold_prev=True); slots7["co"] = (a_, b_)
                    slotCO, slotCOB = slots7["co"]
                    jj = j % 4
                    t2, t2B = emit_ob(j)
                    pa_, paB_ = get_ps()
                    for k in range(NCH):
                        S.op("pe", lambda e, jj=jj, k=k, pa_=pa_, slotCO=slotCO: e.matmul(
                            pa_[:, 0:NT], lhsT=slotCO[:, k * 512 + jj * 128:k * 512 + (jj + 1) * 128], rhs=s_t[:, k, 0:NT],
                            start=(k == 0), stop=(k == NCH - 1)), reads=[slotCOB, sB[k]], writes=[paB_])
                    t1, t1B = get_tmp()
                    S.op("dve", lambda e, j=j, pa_=pa_, t1=t1: e.tensor_tensor(
                        out=t1[:, 0:NT], in0=pa_[:, 0:NT], in1=sg32[:, j * 512:j * 512 + NT], op=ALU.mult),
                        reads=[paB_, sgB[j]], writes=[t1B])
                    S.op("dve", lambda e, j=j, t1=t1, t2=t2: e.tensor_tensor(
                        out=m_t[:, j, 0:NT], in0=t1[:, 0:NT], in1=t2[:, 0:NT], op=ALU.add),
                        reads=[t1B, t2B], writes=[mB[j]])

                for f in range(2):
                    slotO, slotOB, _, _ = use_block()
                    for q in range(nsub):
                        po, poB = get_ps()
                        for k in range(NCH):
                            S.op("pe", lambda e, q=q, k=k, po=po, slotO=slotO: e.matmul(
                                po[0:R, :], lhsT=m_t[:, k, q * 128:q * 128 + R], rhs=slotO[:, k * 512:(k + 1) * 512],
                                start=(k == 0), stop=(k == NCH - 1)), reads=[slotOB, mB[k]], writes=[poB])
                        t_, tB_ = get_tmp()
                        S.op("dve", lambda e, f=f, po=po, t_=t_: e.tensor_tensor(
                            out=t_[0:R, 0:512], in0=po[0:R, :], in1=grow(0, f * 512, (f + 1) * 512), op=ALU.mult),
                            reads=[poB, growB[0]], writes=[tB_])
                        S.op("pool", lambda e, q=q, f=f, t_=t_: e.tensor_tensor(
                            out=xres[0:R, q, f * 512:(f + 1) * 512], in0=xres[0:R, q, f * 512:(f + 1) * 512], in1=t_[0:R, 0:512], op=ALU.add),
                            reads=[tB_, xresB[q]], writes=[xresB[q]])
                n2_ = norm2()
                if not last_prompt:
                    next(n2_)
                if defer_norm2[0] is None:
                    for _ in n2_:
                        pass
                else:
                    defer_norm2[0] = n2_


            next_A1 = [None]
            hooks = {}
            defer_norm2 = [None]
            prev_norm2 = [None]

            def norm2():
                return norm_gen([(lambda q=q: (xres[0:R, q, :], xresB[q])) for q in range(nsub)], R, s_t, sB, SC2, SH2, sample)

            def r2deps(j):
                return [sgB[j // 2]]

            def F_gen():
                for jb in range(8):
                    slotF, slotFB, _, _ = use_block()
                    for jj in range(4):
                        j = jb * 4 + jj
                        pf, pfB = get_ps()
                        for k in range(NCH):
                            S.op("pe", lambda e, jj=jj, k=k, pf=pf, slotF=slotF: e.matmul(
                                pf[:, 0:NT], lhsT=slotF[:, k * 512 + jj * 128:k * 512 + (jj + 1) * 128], rhs=s_t[:, k, 0:NT],
                                start=(k == 0), stop=(k == NCH - 1)), reads=[slotFB, sB[k]], writes=[pfB])
                        S.op("act", lambda e, pf=pf: e.activation(out=pf[:, 0:NT], in_=pf[:, 0:NT], func=AF.Relu),
                             reads=[pfB], writes=[pfB])
                        S.op("act", lambda e, j=j, pf=pf: e.activation(
                            out=r2f[:, j * 512:j * 512 + NT], in_=pf[:, 0:NT], func=AF.Square),
                            reads=[pfB], writes=r2deps(j))
                    yield

                for fq in range(4):
                    pws = [get_ps() for _ in range(nsub)]
                    for kh in range(2):
                        slotW, slotWB, _, _ = use_block()
                        for q in range(nsub):
                            pw, pwB = pws[q]
                            for kk in range(16):
                                k = kh * 16 + kk
                                S.op("pe", lambda e, q=q, k=k, kk=kk, pw=pw, slotW=slotW: e.matmul(
                                    pw[0:R, 0:256], lhsT=r2f[:, k * 512 + q * 128:k * 512 + q * 128 + R],
                                    rhs=slotW[:, kk * 256:(kk + 1) * 256],
                                    start=(k == 0), stop=(k == 31)), reads=[slotWB] + r2deps(k), writes=[pwB])
                    for q in range(nsub):
                        pw, pwB = pws[q]
                        t_, tB_ = get_tmp()
                        S.op("dve", lambda e, fq=fq, pw=pw, t_=t_: e.tensor_tensor(
                            out=t_[0:R, 0:256], in0=pw[0:R, 0:256], in1=grow(1, fq * 256, (fq + 1) * 256), op=ALU.mult),
                            reads=[pwB, growB[1]], writes=[tB_])
                        S.op("pool", lambda e, q=q, fq=fq, t_=t_: e.tensor_tensor(
                            out=xres[0:R, q, fq * 256:(fq + 1) * 256], in0=xres[0:R, q, fq * 256:(fq + 1) * 256],
                            in1=t_[0:R, 0:256], op=ALU.add), reads=[tB_, xresB[q]], writes=[xresB[q]])
                    yield

                for q in range(nsub):
                    yo, yoB = get_stg()
                    rstd_ = rms_rstd(xres[0:R, q, :], xresB[q], R, yo, yoB)
                    S.op("dve", lambda e, q=q, yo=yo, rstd_=rstd_: e.scalar_tensor_tensor(
                        out=yo[0:R, :], in0=xres[0:R, q, :], scalar=rstd_, in1=fgbc[0:R, :], op0=ALU.mult, op1=ALU.mult),
                        reads=[xresB[q], statB, fgB], writes=[yoB])
                    if not sample:
                        dst = yp[ti * 512 + q * 128: ti * 512 + (q + 1) * 128, :]
                    else:
                        dst = ys
                    S.op("sp", lambda e, yo=yo, dst=dst: e.dma_start(out=dst, in_=yo[0:R, :]), reads=[yoB],
                         dma_key=("y", ti, q), out_dma=True)


                if sample:
                    for t in range(LS):
                        fm_rows_out(a32, a32B, NS, ncs.rearrange("(n j) d -> n j d", j=HC)[:, HC - LS + t, :],
                                    ("o", "ncs1", t), col0=t * NS)
                        fm_rows_out(u32, u32B, NS, nps.rearrange("(n j) d -> n j d", j=HP)[:, HP - LS + t, :],
                                    ("o", "nps1", t), col0=t * NS)
                yield

            return phase_A1, phase_A2, conv_gen, phase_B, F_gen, next_A1, hooks, defer_norm2, prev_norm2, norm2

        tiles = [make_tile(ti) for ti in range(NTILES)]

        def drain(g):
            for _ in g:
                pass

        for ti in range(NTILES - 2):
            tiles[ti][5][0] = tiles[ti + 1][0]
        for ti in range(NTILES - 1):
            tiles[ti][7][0] = True
            tiles[ti + 1][8][0] = (lambda ti=ti: tiles[ti][7][0])
        tiles[0][6]["after_S2"] = lambda: mod_v(2)
        tiles[0][6]["after_convPE"] = lambda: mod_v(3)
        tiles[0][6]["after_S3"] = lambda: mod_v(4)
        tiles[0][6]["after_S5"] = lambda: mod_v(5)
        drain(tiles[0][0]())
        tiles[0][1]()
        drain(tiles[0][2]())
        tiles[0][3]()
        for ti in range(1, NTILES):
            if ti == NTILES - 1:
                drain(tiles[ti][0]())
            tiles[ti][1]()
            cg = tiles[ti][2]()
            fg_ = tiles[ti - 1][4]()
            step = 0
            n_rounds = CW - (T_PE if ti < 4 else 0)
            done_r = 0
            for _ in fg_:
                step += 1
                tgt = min(n_rounds, (n_rounds * step + 9) // 10)
                while done_r < tgt:
                    next(cg, None)
                    done_r += 1
            drain(cg)
            tiles[ti][3]()
        drain(tiles[NTILES - 1][4]())

        S.emit(st)
    return nc


_NC_CACHE = {}


def kernel(x_prompt, x_sample, state_conv, state_pool, c_prompt, c_sample, w_ada, b_ada, w_in,
           conv_w, conv_b, ln_g, ln_b, w_conv_out, pool_mix, pool_scale, w_pool_out, w_out,
           w_ff1, w_ff2, final_g):
    f = lambda a: np.ascontiguousarray(np.asarray(a, dtype=np.float32))
    x_prompt, x_sample, state_conv, state_pool = f(x_prompt), f(x_sample), f(state_conv), f(state_pool)
    c_prompt, c_sample = f(c_prompt), f(c_sample)
    if "nc" not in _NC_CACHE:
        _NC_CACHE["nc"] = build_nc()
    nc = _NC_CACHE["nc"]

    def fm(vec):
        return f(np.asarray(vec).reshape(NCH, 128).T)

    shared = {
        "w_ada": f(w_ada[0]), "b_ada": f(np.asarray(b_ada[0]).reshape(1, 6 * D)),
        "b_adaT": f(np.asarray(b_ada[0]).reshape(48, 128).T),
        "w_in": f(w_in[0]),
        "cwT": f(np.asarray(conv_w[0]).T.reshape(NCH, 128, CW).transpose(1, 0, 2).reshape(128, NCH * CW)),
        "vecT": f(np.concatenate([fm(conv_b[0]), fm(ln_g[0]), fm(ln_b[0]), fm(pool_scale[0])], axis=1)),
        "w_co": f(w_conv_out[0]), "pmix": f(pool_mix[0]), "w_po": f(w_pool_out[0]), "w_o": f(w_out[0]),
        "w_f1": f(w_ff1[0]), "w_f2": f(w_ff2[0]), "fg": f(np.asarray(final_g).reshape(1, D)),
    }
    in_maps = []
    for i in range(NCORES):
        m = dict(shared)
        m["xp"] = x_prompt[i]
        m["xs"] = x_sample[i * NS:(i + 1) * NS].reshape(NS * LS, D)
        m["sconv"] = state_conv[0, i * NS:(i + 1) * NS].reshape(NS * HC, D)
        m["spool"] = state_pool[0, i * NS:(i + 1) * NS].reshape(NS * HP, D)
        m["cvec"] = f(np.concatenate([c_prompt[i:i + 1], c_sample[i * NS:(i + 1) * NS]], axis=0))
        in_maps.append(m)
    res = run_bass_kernel_spmd(nc, in_maps, core_ids=list(range(NCORES)))
    rs = res.results
    y_prompt = np.stack([rs[i]["yp"] for i in range(NCORES)], axis=0)
    y_sample = np.concatenate([rs[i]["ys"].reshape(NS, LS, D) for i in range(NCORES)], axis=0)
    ncp_ = np.stack([rs[i]["ncp"] for i in range(NCORES)], axis=0)[None]
    npp_ = np.stack([rs[i]["npp"] for i in range(NCORES)], axis=0)[None]
    ncs_ = np.concatenate([rs[i]["ncs"].reshape(NS, HC, D) for i in range(NCORES)], axis=0)[None]
    nps_ = np.concatenate([rs[i]["nps"].reshape(NS, HP, D) for i in range(NCORES)], axis=0)[None]
    return (y_prompt.astype(np.float32), y_sample.astype(np.float32), ncp_.astype(np.float32),
            npp_.astype(np.float32), ncs_.astype(np.float32), nps_.astype(np.float32))
```

```python
import numpy as np
from contextlib import ExitStack
import concourse.bass as bass
import concourse.mybir as mybir
from concourse.bass_utils import run_bass_kernel_spmd

F32 = mybir.dt.float32
BF16 = mybir.dt.bfloat16
AF = mybir.ActivationFunctionType
ALU = mybir.AluOpType

D = 1024
NCH = 8
SEQ = 2048
NS = 16
LS = 4
HC = 30
HP = 15
CW = 31
EPS = 1e-6
NCORES = 8

T_PE = 12
RING = 5

ENGS = ("pe", "act", "dve", "pool", "sp")
EPOCH = 16000
SAFE_DIST = 4
STRICT_SAME_ENGINE = True


class Buf:
    __slots__ = ("name", "last_w", "readers")

    def __init__(self, name):
        self.name = name
        self.last_w = None
        self.readers = []


class Ins:
    __slots__ = ("eng", "fn", "deps", "sig", "cnt", "dma_key", "dma_val", "is_dma", "idx")


class Sched:
    def __init__(self, nc):
        self.nc = nc
        self.q = {e: [] for e in ENGS}
        self.dma_cnt = {}
        self.all_dma_out = []

    def op(self, eng, fn, reads=(), writes=(), dma_key=None, out_dma=False):
        ins = Ins()
        ins.eng = eng
        ins.fn = fn
        ins.sig = False
        ins.cnt = 0
        ins.is_dma = dma_key is not None
        ins.dma_key = dma_key
        ins.dma_val = 0
        ins.idx = len(self.q[eng])
        if ins.is_dma:
            v = self.dma_cnt.get(dma_key, 0) + 16
            self.dma_cnt[dma_key] = v
            ins.dma_val = v
        deps = {}
        for b in reads:
            if b.last_w is not None:
                deps[id(b.last_w)] = (b.last_w, True)
        for b in writes:
            if b.last_w is not None and id(b.last_w) not in deps:
                deps[id(b.last_w)] = (b.last_w, False)
            for r in b.readers:
                if id(r) not in deps:
                    deps[id(r)] = (r, False)
        final = []
        for d, raw in deps.values():
            if (not d.is_dma) and (not ins.is_dma) and d.eng == eng:
                if eng == "pe":
                    continue
                if not STRICT_SAME_ENGINE:
                    if not raw or ins.idx - d.idx >= SAFE_DIST:
                        continue
            final.append(d)
        ins.deps = final
        for b in reads:
            b.readers.append(ins)
        for b in writes:
            b.last_w = ins
            b.readers = []
        self.q[eng].append(ins)
        if out_dma:
            self.all_dma_out.append(ins)
        return ins

    def emit(self, stack):
        nc = self.nc
        fin = Ins()
        fin.eng = "sp"; fin.fn = None; fin.sig = False; fin.cnt = 0
        fin.is_dma = False; fin.dma_key = None; fin.dma_val = 0; fin.idx = len(self.q["sp"])
        fin.deps = list(self.all_dma_out)
        self.q["sp"].append(fin)
        for e in ENGS:
            for ins in self.q[e]:
                for d in ins.deps:
                    d.sig = True
        nsig = {}
        for e in ENGS:
            c = 0
            for ins in self.q[e]:
                if ins.sig and not ins.is_dma:
                    c += 1
                    ins.cnt = c
            nsig[e] = c
        esems = {}
        for e in ENGS:
            n = (nsig[e] + EPOCH - 1) // EPOCH
            esems[e] = [stack.enter_context(nc.semaphore(f"s_{e}_{i}")) for i in range(max(n, 1))]
        dsems = {}
        for k in self.dma_cnt:
            dsems[k] = stack.enter_context(nc.semaphore("d_" + "_".join(str(x) for x in k)))

        def signal_of(d):
            if d.is_dma:
                return ("d", d.dma_key), dsems[d.dma_key], d.dma_val
            ep = (d.cnt - 1) // EPOCH
            return ("e", d.eng, ep), esems[d.eng][ep], d.cnt - ep * EPOCH

        block = stack.enter_context(nc.Block())
        reg = {"pe": block.tensor, "act": block.scalar, "dve": block.vector,
               "pool": block.gpsimd, "sp": block.sync}
        for e in ENGS:
            qe = self.q[e]

            def body(eng, qe=qe, e=e):
                waited = {}
                maxep = {}
                for ins in qe:
                    for d in ins.deps:
                        key, sem, val = signal_of(d)
                        if key[0] == "e":
                            if maxep.get(key[1], -1) > key[2]:
                                continue
                        if waited.get(key, 0) < val:
                            eng.wait_ge(sem, val)
                            waited[key] = val
                            if key[0] == "e":
                                maxep[key[1]] = max(maxep.get(key[1], -1), key[2])
                    if ins.fn is None:
                        continue
                    bi = ins.fn(eng)
                    if ins.is_dma:
                        bi.then_inc(dsems[ins.dma_key], 16)
                    elif ins.sig:
                        ep = (ins.cnt - 1) // EPOCH
                        bi.then_inc(esems[e][ep], 1)

            reg[e](body)


def build_nc():
    nc = bass.Bass("TRN2", target_bir_lowering=False)

    def din(name, shape):
        return nc.dram_tensor(name, list(shape), F32, kind="ExternalInput").ap()

    def dout(name, shape):
        return nc.dram_tensor(name, list(shape), F32, kind="ExternalOutput").ap()

    xp = din("xp", [SEQ, D])
    xs = din("xs", [NS * LS, D])
    sconv = din("sconv", [NS * HC, D])
    spool = din("spool", [NS * HP, D])
    cvec = din("cvec", [NS + 1, D])
    w_ada = din("w_ada", [D, 6 * D])
    b_ada = din("b_ada", [1, 6 * D])
    b_adaT = din("b_adaT", [128, 48])
    w_in = din("w_in", [D, 5 * D])
    cwT = din("cwT", [128, NCH * CW])
    vecT = din("vecT", [128, 4 * NCH])
    w_co = din("w_co", [D, D])
    pmix = din("pmix", [4, 256, 256])
    w_po = din("w_po", [D, D])
    w_o = din("w_o", [D, D])
    w_f1 = din("w_f1", [D, 4 * D])
    w_f2 = din("w_f2", [4 * D, D])
    fg = din("fg", [1, D])

    yp = dout("yp", [SEQ, D])
    ys = dout("ys", [NS * LS, D])
    ncp = dout("ncp", [HC, D])
    npp = dout("npp", [HP, D])
    ncs = dout("ncs", [NS * HC, D])
    nps = dout("nps", [NS * HP, D])

    with ExitStack() as st:
        S = Sched(nc)

        def sb(name, shape, dt=F32):
            return st.enter_context(nc.sbuf_tensor(name, list(shape), dt))

        ring = [sb(f"ring{i}", [128, 4096], BF16) for i in range(RING)]
        ringB = [Buf(f"ring{i}") for i in range(RING)]
        pmx = sb("pmx", [128, 4, 2, 256], BF16); pmxB = Buf("pmx")
        xres = sb("xres", [128, 4, D]); xresB = [Buf(f"xres{q}") for q in range(4)]
        NSTG = 3
        stg = [sb(f"stg{i}", [128, D]) for i in range(NSTG)]; stgB = [Buf(f"stg{i}") for i in range(NSTG)]
        h = sb("h", [128, NCH, 512], BF16); hB = [Buf(f"h{c}") for c in range(NCH)]
        aext = sb("aext", [128, NCH, 544], BF16); aB = [Buf(f"a{c}") for c in range(NCH)]
        NTMP = 4
        tmp = [sb(f"tmp{i}", [128, 512]) for i in range(NTMP)]; tmpB = [Buf(f"tmp{i}") for i in range(NTMP)]
        NTMPP = 4
        tmpp = [sb(f"tmpp{i}", [128, 544]) for i in range(NTMPP)]; tmppB = [Buf(f"tmpp{i}") for i in range(NTMPP)]
        NTB = 0
        tmb = [sb(f"tmb{i}", [128, 512], BF16) for i in range(NTB)]; tmbB = [Buf(f"tmb{i}") for i in range(NTB)]
        utail = sb("utail", [128, NCH, HP]); utB = [Buf(f"ut{c}") for c in range(NCH)]
        pooled = sb("pooled", [128, NCH, 512], BF16); plB = [Buf(f"pl{c}") for c in range(NCH)]
        r2f = sb("r2f", [128, 32 * 512], BF16)
        sg32 = r2f[:].bitcast(F32)
        sgB = [Buf(f"sg{c}") for c in range(16)]
        cf = sb("cf", [128, NCH, 512]); cfB = [Buf(f"cf{c}") for c in range(NCH)]
        badd = cf[:].rearrange("p c n -> p (c n)")[:, 0:2 * D].rearrange("p (g d) -> p g d", d=D)
        lnst = sb("lnst", [128, 2, 512]); lnB = [Buf("ln1"), Buf("ln2")]
        sqj = sb("sqj", [128, D], BF16); sqjB = Buf("sqj")
        uhist = cf[:].rearrange("p c n -> p (c n)")[:, 2 * D:2 * D + NCH * NS * HP].rearrange("p (c n j) -> p c n j", n=NS, j=HP)
        uhB = [None] * NCH
        s_t = sb("s_t", [128, NCH, 512], BF16); sB = [Buf(f"s{c}") for c in range(NCH)]
        m_t = sb("m_t", [128, NCH, 512], BF16); mB = [Buf(f"m{c}") for c in range(NCH)]
        gbc = sb("gbc", [128, 2, D]); gbcB = [Buf("g1bc"), Buf("g2bc")]
        gsr = gbc; gsrB = gbcB
        fgbc = sb("fgbc", [128, D]); fgB = Buf("fgbc")
        a32 = sb("a32", [128, NCH, 64]); a32B = [Buf(f"a32_{c}") for c in range(NCH)]
        u32 = sb("u32", [128, NCH, 64]); u32B = [Buf(f"u32_{c}") for c in range(NCH)]
        ident = sb("ident", [128, 128]); identB = Buf("ident")
        identb = sb("identb", [128, 128], BF16); identbB = Buf("identb")
        onesD = sb("onesD", [128, 128], BF16); onesB = Buf("onesD")
        epst = sb("epst", [128, 1]); epsB = Buf("eps")
        cw = sb("cw", [128, NCH, CW]); vecs = sb("vecs", [128, 4, NCH]); cwB = Buf("cw"); vecB = Buf("vecs")
        bT = sb("bT", [128, 48]); bTB = Buf("bT")
        modT = sb("modT", [128, 48, NS + 1]); modB = Buf("modT")
        scT32 = sb("scT32", [128, NCH, NS + 1]); scT = sb("scT", [128, NCH, NS + 1], BF16); scB = Buf("scT")
        rep_p = a32[:].rearrange("p c n -> p (c n)").bitcast(BF16).rearrange("p (c n) -> p c n", n=128)
        rep_s = u32[:].rearrange("p c n -> p (c n)").bitcast(BF16)[:, 0:NCH * 64].rearrange("p (c n) -> p c n", n=64)
        repB = Buf("rep")
        invc = sb("invc", [128, 4, 16]); invB = Buf("invc")
        stat = sb("stat", [128, 16]); statB = Buf("stat")
        NDG = 7
        dgp = sb("dgp", [128, NDG, 128], BF16); dgB = [Buf(f"dg{i}") for i in range(NDG)]
        psum = [st.enter_context(nc.psum_tensor(f"ps{i}", [128, 512], F32)) for i in range(8)]
        psB = [Buf(f"ps{i}") for i in range(8)]

        rr = {"ps": 0, "tmp": 0, "tmb": 0, "stg": 0, "stat": 0, "dg": 0, "tmpp": 0}

        def get_tmpp():
            i = rr["tmpp"]; rr["tmpp"] = (i + 1) % NTMPP
            return tmpp[i], tmppB[i]

        def get_dg():
            i = rr["dg"]; rr["dg"] = (i + 1) % NDG
            return dgp[:, i, :], dgB[i]

        def get_ps():
            i = rr["ps"]; rr["ps"] = (i + 1) % 8
            return psum[i], psB[i]

        def get_tmp():
            i = rr["tmp"]; rr["tmp"] = (i + 1) % NTMP
            return tmp[i], tmpB[i]

        def get_tmb():
            i = rr["tmb"]; rr["tmb"] = (i + 1) % NTB
            return tmb[i], tmbB[i]

        def get_stg():
            i = rr["stg"]; rr["stg"] = (i + 1) % NSTG
            return stg[i], stgB[i]

        def get_stat():
            i = rr["stat"]; rr["stat"] = (i + 1) % 8
            return i * 2

        NBT = 32
        wscr = nc.dram_tensor("wscr", [NBT, 128, 4096], BF16).ap()
        scrB = [Buf(f"scr{i}") for i in range(NBT)]
        gscr = nc.dram_tensor("gscr", [2, 64, D], F32).ap()
        gscrB = [Buf("gscr0"), Buf("gscr1")]
        scr_written = set()
        scr_uses = {}
        pending_wo = {}

        def wblock(W, nk, r0, c0, ncols, sid=None):
            return (W, nk, r0, c0, ncols, sid)

        blk_state = {"n": 0, "issued": 0, "plan": []}

        def issue_block(bi):
            W, nk, r0, c0, ncols, sid = blk_state["plan"][bi]
            slot = bi % RING
            if sid is not None and sid in scr_written:
                S.op("sp", lambda e: e.dma_start(out=ring[slot][:, :], in_=wscr[sid]),
                     reads=[scrB[sid]], writes=[ringB[slot]], dma_key=("wh", slot))
                return
            src = W[r0:r0 + nk * 128, c0:c0 + ncols].rearrange("(k p) c -> p k c", p=128)
            dst = ring[slot][:, 0:nk * ncols].rearrange("p (k c) -> p k c", c=ncols)
            S.op("pool", lambda e: e.dma_start(out=dst, in_=src), writes=[ringB[slot]], dma_key=("w", slot))
            if sid is not None:
                if scr_uses.get(sid, 0) == sid % 3:
                    pending_wo[bi] = (sid, slot)
                scr_uses[sid] = scr_uses.get(sid, 0) + 1

        def use_block(hold_prev=False):
            bi = blk_state["n"]
            blk_state["n"] += 1
            if bi in pending_wo:
                sid_, slot_ = pending_wo.pop(bi)
                S.op("sp", lambda e: e.dma_start(out=wscr[sid_], in_=ring[slot_][:, :]),
                     reads=[ringB[slot_]], writes=[scrB[sid_]], dma_key=("wo", slot_))
                scr_written.add(sid_)
            depth = RING - 1 if hold_prev else RING
            while blk_state["issued"] < min(bi + depth, len(blk_state["plan"])):
                issue_block(blk_state["issued"])
                blk_state["issued"] += 1
            W, nk, r0, c0, ncols, sid = blk_state["plan"][bi]
            slot = bi % RING
            return ring[slot], ringB[slot], nk, ncols

        def halves(W, c0):
            return [wblock(W, 8, 0, c0, 512), wblock(W, 8, 0, c0 + 512, 512)]

        def tile_plan():
            p = []
            for hf in range(2):
                p.append(wblock(w_in, 8, 0, hf * 512, 512))
                p.append(wblock(w_in, 8, 0, 1024 + hf * 512, 512))
            p += halves(w_in, 2048)
            p += halves(w_in, 3072)
            p += halves(w_in, 4096)
            for hf in range(2):
                p.append(wblock(w_po, 8, 0, hf * 512, 512))
                p.append(wblock(w_co, 8, 0, hf * 512, 512))
            p += halves(w_o, 0)
            for b in range(8):
                p.append(wblock(w_f1, 8, 0, b * 512, 512))
            for b in range(4):
                p.append(wblock(w_f2, 16, 0, b * 256, 256))
                p.append(wblock(w_f2, 16, 2048, b * 256, 256))
            return p

        def gs_plan():
            return halves(w_ada, 2 * D) + halves(w_ada, 5 * D)

        def with_sid(blocks, sid0):
            return [wblock(b[0], b[1], b[2], b[3], b[4], sid0 + i) for i, b in enumerate(blocks)]

        def plan_A(sample_=False):
            p = []
            for hf in range(2):
                p.append(wblock(w_in, 8, 0, hf * 512, 512))
                p.append(wblock(w_in, 8, 0, 1024 + hf * 512, 512))
            if sample_:
                return with_sid(p, 0) + with_sid(halves(w_in, 2048), 4)
            return with_sid(p, 0)

        def plan_B(sample_):
            p = halves(w_in, 2048) + halves(w_in, 3072) + halves(w_in, 4096)
            nskip = 2 if sample_ else 0
            for hf in range(2):
                p.append(wblock(w_po, 8, 0, hf * 512, 512))
                p.append(wblock(w_co, 8, 0, hf * 512, 512))
            p = with_sid(p + halves(w_o, 0), 4)
            return p[nskip:]

        def plan_F():
            p = [wblock(w_f1, 8, 0, b * 512, 512) for b in range(8)]
            for b in range(4):
                p.append(wblock(w_f2, 16, 0, b * 256, 256))
                p.append(wblock(w_f2, 16, 2048, b * 256, 256))
            return with_sid(p, 16)

        plan = []
        for v in range(2):
            plan += halves(w_ada, v * 1024)
        NTILES = 5
        pa_, pb_ = plan_A(), plan_B(False)
        plan += pa_[0:4] + halves(w_ada, 2 * 1024) + halves(w_ada, 3 * 1024)
        plan += pb_[0:2] + halves(w_ada, 4 * 1024) + pb_[2:6] + halves(w_ada, 5 * 1024) + pb_[6:]
        for ti_ in range(1, NTILES):
            plan += plan_A(ti_ == 4) + plan_F() + plan_B(ti_ == 4)
        plan += plan_F()
        blk_state["plan"] = plan

        S.op("pool", lambda e: e.memset(ident[:], 0.0), writes=[identB])
        S.op("pool", lambda e: e.affine_select(out=ident[:], in_=ident[:], pattern=[[-1, 128]],
                                               compare_op=ALU.not_equal, fill=1.0, base=0, channel_multiplier=1),
             reads=[identB], writes=[identB])
        S.op("dve", lambda e: e.tensor_copy(out=identb[:], in_=ident[:]), reads=[identB], writes=[identbB])
        S.op("dve", lambda e: e.memset(onesD[:], 1.0 / D), writes=[onesB])
        S.op("dve", lambda e: e.memset(epst[:], EPS), writes=[epsB])
        S.op("dve", lambda e: e.memset(utail[:], 0.0), writes=utB)
        for c in range(NCH):
            S.op("dve", lambda e, c=c: e.memset(aext[:, c, 0:HC], 0.0), writes=[aB[c]])
        for g in range(4):
            w = 2 << g
            S.op("dve", lambda e, g=g, w=w: e.memset(invc[:, g, :], 1.0 / w), writes=[invB])
            for t in range(w - 1):
                S.op("dve", lambda e, g=g, t=t: e.memset(invc[:, g, t:t + 1], 1.0 / (t + 1)), writes=[invB])
        S.op("sp", lambda e: e.dma_start(out=cw[:].rearrange("p c k -> p (c k)"), in_=cwT), writes=[cwB], dma_key=("c", 0))
        S.op("sp", lambda e: e.dma_start(out=vecs[:].rearrange("p v c -> p (v c)"), in_=vecT), writes=[vecB], dma_key=("c", 1))
        S.op("sp", lambda e: e.dma_start(out=bT[:], in_=b_adaT), writes=[bTB], dma_key=("c", 2))
        c_full, cB = get_stg()
        c_sb = c_full[0:NS + 1, :]
        S.op("sp", lambda e: e.dma_start(out=c_sb, in_=cvec), writes=[cB], dma_key=("c", 3))
        S.op("sp", lambda e: e.dma_start(out=fgbc[:], in_=fg.partition_broadcast(128)), writes=[fgB], dma_key=("c", 4))
        CONV_B, LN_G, LN_B, PSC = 0, 1, 2, 3
        S.op("act", lambda e: e.activation(out=c_sb, in_=c_sb, func=AF.Silu), reads=[cB], writes=[cB])
        pt, ptB = get_ps()
        for k in range(NCH):
            S.op("pe", lambda e, k=k: e.transpose(out=pt[:, k * 17:(k + 1) * 17], in_=c_full[0:17, k * 128:(k + 1) * 128],
                                                  identity=ident[0:17, 0:17]),
                 reads=[cB, identB], writes=[ptB])
        S.op("dve", lambda e: e.tensor_copy(out=scT32[:].rearrange("p k n -> p (k n)"), in_=pt[:, 0:136]),
             reads=[ptB], writes=[scB])
        S.op("dve", lambda e: e.tensor_copy(out=scT[:].rearrange("p k n -> p (k n)"), in_=scT32[:].rearrange("p k n -> p (k n)")),
             reads=[scB], writes=[scB])
        for k in range(NCH):
            S.op("dve", lambda e, k=k: e.tensor_copy(out=rep_p[:, k, :], in_=scT32[:, k, 0:1].to_broadcast([128, 128])),
                 reads=[scB], writes=[repB])
            S.op("dve", lambda e, k=k: e.tensor_copy(out=rep_s[:, k, :].rearrange("p (n t) -> p n t", t=LS),
                                                     in_=scT32[:, k, 1:17].unsqueeze(2).to_broadcast([128, NS, LS])),
                 reads=[scB], writes=[repB])
        def g_rows(gi, f, slot, slotB, sample_rows, badd_ap=None, baddBufs=None, dst=None, dstBufs=None):
            pg, pgB = get_ps()
            R_ = 64 if sample_rows else 128
            lhs = rep_s if sample_rows else rep_p
            for k in range(NCH):
                S.op("pe", lambda e, k=k, pg=pg: e.matmul(
                    pg[0:R_, :], lhsT=lhs[:, k, :], rhs=slot[:, k * 512:(k + 1) * 512],
                    start=(k == 0), stop=(k == NCH - 1)), reads=[slotB, repB], writes=[pgB])
            S.op("dve", lambda e, pg=pg: e.tensor_tensor(
                out=(gbc[0:R_, gi, f * 512:(f + 1) * 512] if dst is None else dst), in0=pg[0:R_, :],
                in1=(badd[0:R_, gi, f * 512:(f + 1) * 512] if badd_ap is None else badd_ap), op=ALU.add),
                reads=[pgB] + (cfB[0:4] if baddBufs is None else baddBufs),
                writes=([gbcB[gi]] if dstBufs is None else dstBufs))

        def load_badd():
            S.op("sp", lambda e: e.dma_start(out=badd[:, 0, :], in_=b_ada[:, 2 * D:3 * D].partition_broadcast(128)),
                 writes=cfB[0:4], dma_key=("c", 5))
            S.op("sp", lambda e: e.dma_start(out=badd[:, 1, :], in_=b_ada[:, 5 * D:6 * D].partition_broadcast(128)),
                 writes=cfB[0:4], dma_key=("c", 6))

        def mod_v(v):
                pm_, pmB_ = get_ps()
                for hf in range(2):
                    slot, slotB, nk, ncols = use_block()
                    for cc in range(4):
                        c = hf * 4 + cc
                        for k in range(NCH):
                            S.op("pe", lambda e, c=c, cc=cc, k=k, slot=slot, pm_=pm_: e.matmul(
                                pm_[:, c * 17:(c + 1) * 17], lhsT=slot[:, k * 512 + cc * 128:k * 512 + (cc + 1) * 128],
                                rhs=scT[:, k, :], start=(k == 0), stop=(k == NCH - 1)),
                                reads=[slotB, scB], writes=[pmB_])
                    if v in (2, 5):
                        if hf == 0:
                            bs_, bsB_ = get_stg()
                            S.op("sp", lambda e, bs_=bs_, v=v: e.dma_start(
                                out=bs_[:, :], in_=b_ada[:, v * D:(v + 1) * D].partition_broadcast(128)),
                                writes=[bsB_], dma_key=("bs", 0 if v == 2 else 1))
                        gi_ = 0 if v == 2 else 1
                        g_rows(gi_, hf, slot, slotB, False, bs_[:, hf * 512:(hf + 1) * 512], [bsB_])
                        if hf == 0:
                            gs_, gsB_ = get_stg()
                        g_rows(gi_, hf, slot, slotB, True, bs_[0:64, hf * 512:(hf + 1) * 512], [bsB_],
                               dst=gs_[0:64, hf * 512:(hf + 1) * 512], dstBufs=[gsB_])
                        if hf == 1:
                            S.op("sp", lambda e, gs_=gs_, gi_=gi_: e.dma_start(out=gscr[gi_], in_=gs_[0:64, :]),
                                 reads=[gsB_], writes=[gscrB[gi_]], dma_key=("gso", gi_))
                S.op("dve", lambda e, v=v, pm_=pm_: e.tensor_tensor(
                    out=modT[:, v * 8:(v + 1) * 8, :], in0=pm_[:, 0:136].rearrange("p (c n) -> p c n", n=17),
                    in1=bT[:, v * 8:(v + 1) * 8].unsqueeze(2).to_broadcast([128, 8, 17]), op=ALU.add),
                    reads=[pmB_, bTB], writes=[modB])
                if v in (1, 4):
                    S.op("dve", lambda e, v=v: e.tensor_scalar(out=modT[:, v * 8:(v + 1) * 8, :], in0=modT[:, v * 8:(v + 1) * 8, :],
                                                              scalar1=1.0, scalar2=None, op0=ALU.add),
                         reads=[modB], writes=[modB])

        mod_v(0)
        mod_v(1)
        for g in range(4):
            S.op("pool", lambda e, g=g: e.dma_start(out=pmx[:, g, :, :],
                                                    in_=pmix[g].rearrange("(k p) c -> p k c", p=128)),
                 writes=[pmxB], dma_key=("pm", g))

        def mod_rest():
            for v in range(2, 6):
                mod_v(v)

        SH1, SC1, G1, SH2, SC2, G2 = range(6)

        def rms_rstd(src_ap, srcB, R, junk, junkB):
            c0 = get_stat()
            S.op("act", lambda e: e.activation(out=junk[0:R, :], in_=src_ap, func=AF.Square, scale=1.0 / 32.0,
                                               accum_out=stat[0:R, c0:c0 + 1]),
                 reads=[srcB], writes=[junkB, statB])
            S.op("act", lambda e: e.activation(out=stat[0:R, c0 + 1:c0 + 2], in_=stat[0:R, c0:c0 + 1], func=AF.Sqrt,
                                               bias=epst[0:R, 0:1], scale=1.0),
                 reads=[statB, epsB], writes=[statB])
            S.op("dve", lambda e: e.reciprocal(out=stat[0:R, c0 + 1:c0 + 2], in_=stat[0:R, c0 + 1:c0 + 2]),
                 reads=[statB], writes=[statB])
            return stat[0:R, c0 + 1:c0 + 2]

        def norm_stats(src_ap, srcB, R, in_place=None):
            if in_place is not None:
                xn, xnB = in_place
                rstd = rms_rstd(src_ap, srcB, R, sqj, sqjB)
            else:
                xn, xnB = get_stg()
                rstd = rms_rstd(src_ap, srcB, R, xn, xnB)
            S.op("act", lambda e: e.activation(out=xn[0:R, :], in_=src_ap, func=AF.Copy, scale=rstd),
                 reads=[srcB, statB], writes=[xnB])
            return xn, xnB

        def norm_gen(srcs, R, dst, dstB, vsc, vsh, sample, prefetch=None, n_early=2):
            nsub_ = len(srcs)
            staged = {}
            xns = {}
            if prefetch is not None:
                for q in range(min(NSTG - 1, nsub_)):
                    staged[q] = prefetch(q)

            def do_stats(q):
                if prefetch is not None:
                    xbuf, xB_ = staged.pop(q)
                    return norm_stats(xbuf[0:R, :], xB_, R, in_place=(xbuf, xB_))
                src_ap, srcB = srcs[q]()
                return norm_stats(src_ap, srcB, R)

            for q in range(min(n_early, nsub_)):
                xns[q] = do_stats(q)
            mark = rr["stg"]
            yield
            assert rr["stg"] == mark, "staging ring used between the two halves of a norm"
            banks = [get_ps() for _ in range(NCH)]
            for q in range(nsub_):
                if q not in xns:
                    xns[q] = do_stats(q)
                xn, xnB = xns.pop(q)
                for c in range(NCH):
                    pt_, ptB_ = banks[c]
                    S.op("pe", lambda e, c=c, q=q, pt_=pt_, xn=xn: e.transpose(
                        out=pt_[:, q * 128:q * 128 + R], in_=xn[0:R, c * 128:(c + 1) * 128], identity=ident[0:R, 0:R]),
                        reads=[xnB, identB], writes=[ptB_])
                if prefetch is not None and q + NSTG - 1 < nsub_:
                    staged[q + NSTG - 1] = prefetch(q + NSTG - 1)
            W_ = (nsub_ - 1) * 128 + R
            for c in range(NCH):
                pt_, ptB_ = banks[c]
                if not sample:
                    if c % 2 == 0:
                        S.op("act", lambda e, c=c, pt_=pt_: e.activation(
                            out=dst[:, c, 0:W_], in_=pt_[:, 0:W_], func=AF.Identity,
                            bias=modT[:, vsh * 8 + c, 0:1], scale=modT[:, vsc * 8 + c, 0:1]),
                            reads=[ptB_, modB], writes=[dstB[c]])
                    else:
                        S.op("dve", lambda e, c=c, pt_=pt_: e.tensor_scalar(
                            out=dst[:, c, 0:W_], in0=pt_[:, 0:W_], scalar1=modT[:, vsc * 8 + c, 0:1],
                            scalar2=modT[:, vsh * 8 + c, 0:1], op0=ALU.mult, op1=ALU.add),
                            reads=[ptB_, modB], writes=[dstB[c]])
                else:
                    t_, tB_ = get_tmp()
                    S.op("dve", lambda e, c=c, pt_=pt_, t_=t_: e.tensor_tensor(
                        out=t_[:, 0:R].rearrange("p (n t) -> p n t", t=LS),
                        in0=pt_[:, 0:R].rearrange("p (n t) -> p n t", t=LS),
                        in1=modT[:, vsc * 8 + c, 1:17].unsqueeze(2).to_broadcast([128, NS, LS]), op=ALU.mult),
                        reads=[ptB_, modB], writes=[tB_])
                    S.op("dve", lambda e, c=c, t_=t_: e.tensor_tensor(
                        out=dst[:, c, 0:R].rearrange("p (n t) -> p n t", t=LS),
                        in0=t_[:, 0:R].rearrange("p (n t) -> p n t", t=LS),
                        in1=modT[:, vsh * 8 + c, 1:17].unsqueeze(2).to_broadcast([128, NS, LS]), op=ALU.add),
                        reads=[tB_, modB], writes=[dstB[c]])

        def fm_rows_out(src, srcBs, ncols, dst_dram_ap, key, col0=0):
            so, soB = get_stg()
            for hb in range(2):
                pt_, ptB_ = get_ps()
                for cc in range(4):
                    c = hb * 4 + cc
                    S.op("pe", lambda e, c=c, cc=cc, pt_=pt_: e.transpose(
                        out=pt_[0:ncols, cc * 128:(cc + 1) * 128], in_=src[:, c, col0:col0 + ncols], identity=ident[:, :]),
                        reads=[srcBs[c], identB], writes=[ptB_])
                S.op("act", lambda e, hb=hb, pt_=pt_: e.activation(out=so[0:ncols, hb * 512:(hb + 1) * 512],
                                                                  in_=pt_[0:ncols, :], func=AF.Copy),
                     reads=[ptB_], writes=[soB])
            S.op("sp", lambda e: e.dma_start(out=dst_dram_ap, in_=so[0:ncols, :]), reads=[soB], dma_key=key, out_dma=True)

        def make_tile(ti):
            sample = (ti == 4)
            if not sample:
                NT, nseq, L, nsub, R = 512, 1, 512, 4, 128
            else:
                NT, nseq, L, nsub, R = 64, NS, LS, 1, 64
            first = (ti == 0)
            last_prompt = (ti == 3)
            AW = HC + L
            UW = HP + L
            mcol = 0 if not sample else 1

            def aview(c):
                return aext[:, c, 0:nseq * AW].rearrange("p (n w) -> p n w", w=AW)

            def v3(ap2d):
                return ap2d.rearrange("p (n l) -> p n l", l=L)

            tpe = (CW if first else T_PE) if not sample else 0
            n32 = HC if last_prompt else (NT if sample else 0)
            grow = (lambda gi, lo, hi: gbc[0:R, gi, lo:hi])
            growB = gbcB

            def phase_A1():
                if sample:
                    for qq in range(4):
                        hs, hsB = get_stg()
                        S.op("sp", lambda e, qq=qq, hs=hs: e.dma_start(out=hs[0:120, :], in_=sconv[qq * 120:(qq + 1) * 120, :]),
                             writes=[hsB], dma_key=("hs", qq))
                        for hb in range(2):
                            pt_, ptB_ = get_ps()
                            for cc in range(4):
                                c = hb * 4 + cc
                                S.op("pe", lambda e, c=c, cc=cc, pt_=pt_, hs=hs: e.transpose(
                                    out=pt_[:, cc * 128:cc * 128 + 120], in_=hs[0:120, c * 128:(c + 1) * 128],
                                    identity=ident[0:120, 0:120]), reads=[hsB, identB], writes=[ptB_])
                            for cc in range(4):
                                c = hb * 4 + cc
                                S.op("act", lambda e, c=c, cc=cc, pt_=pt_, qq=qq: e.activation(
                                    out=aview(c)[:, qq * 4:(qq + 1) * 4, 0:HC],
                                    in_=pt_[:, cc * 128:cc * 128 + 120].rearrange("p (n j) -> p n j", j=HC), func=AF.Copy),
                                    reads=[ptB_], writes=[aB[c]])
                    for qq in range(2):
                        hs, hsB = get_stg()
                        S.op("sp", lambda e, qq=qq, hs=hs: e.dma_start(out=hs[0:120, :], in_=spool[qq * 120:(qq + 1) * 120, :]),
                             writes=[hsB], dma_key=("hp", qq))
                        for hb in range(2):
                            pt_, ptB_ = get_ps()
                            for cc in range(4):
                                c = hb * 4 + cc
                                S.op("pe", lambda e, c=c, cc=cc, pt_=pt_, hs=hs: e.transpose(
                                    out=pt_[:, cc * 128:cc * 128 + 120], in_=hs[0:120, c * 128:(c + 1) * 128],
                                    identity=ident[0:120, 0:120]), reads=[hsB, identB], writes=[ptB_])
                            for cc in range(4):
                                c = hb * 4 + cc
                                S.op("act", lambda e, c=c, cc=cc, pt_=pt_, qq=qq: e.activation(
                                    out=uhist[:, c, qq * 8:(qq + 1) * 8, :],
                                    in_=pt_[:, cc * 128:cc * 128 + 120].rearrange("p (n j) -> p n j", j=HP), func=AF.Copy),
                                    reads=[ptB_], writes=cfB[4:8])
                    S.op("sp", lambda e: e.dma_start(
                        out=ncs.rearrange("(n j) d -> n j d", j=HC)[:, 0:HC - LS, :],
                        in_=sconv.rearrange("(n j) d -> n j d", j=HC)[:, LS:HC, :]), dma_key=("o", "ncs0"), out_dma=True)
                    S.op("sp", lambda e: e.dma_start(
                        out=nps.rearrange("(n j) d -> n j d", j=HP)[:, 0:HP - LS, :],
                        in_=spool.rearrange("(n j) d -> n j d", j=HP)[:, LS:HP, :]), dma_key=("o", "nps0"), out_dma=True)


                def load_x(q):
                    src = xp[ti * 512 + q * 128: ti * 512 + (q + 1) * 128, :] if not sample else xs
                    xs_, xsB_ = get_stg()
                    S.op("act", lambda e: e.dma_start(out=xs_[0:R, :], in_=src), writes=[xsB_], dma_key=("xa", q))
                    return xs_, xsB_
                g_ = norm_gen([None] * nsub, R, h, hB, SC1, SH1, sample, prefetch=load_x)
                next(g_)
                yield
                for _ in g_:
                    pass

            def stage_S3():
                slotU = slotUB = None
                for j in range(NCH):
                    if j % 4 == 0:
                        slotU, slotUB, _, _ = use_block()
                    jj = j % 4
                    g = j // 2
                    w = 2 << g
                    pu, puB = get_ps()
                    for k in range(NCH):
                        S.op("pe", lambda e, jj=jj, k=k, pu=pu, slotU=slotU: e.matmul(
                            pu[:, 0:NT], lhsT=slotU[:, k * 512 + jj * 128:k * 512 + (jj + 1) * 128], rhs=h[:, k, 0:NT],
                            start=(k == 0), stop=(k == NCH - 1)), reads=[slotUB, hB[k]], writes=[puB])
                    ue, ueB = get_tmpp()
                    uev = ue[:, 0:nseq * UW].rearrange("p (n w) -> p n w", w=UW)
                    S.op("act", lambda e, pu=pu, uev=uev: e.activation(out=uev[:, :, HP:HP + L], in_=v3(pu[:, 0:NT]), func=AF.Copy),
                         reads=[puB], writes=[ueB])
                    if not sample:
                        S.op("pool", lambda e, j=j, uev=uev: e.tensor_copy(out=uev[:, 0, 0:HP], in_=utail[:, j, :]),
                             reads=[utB[j]], writes=[ueB])
                        S.op("pool", lambda e, j=j, uev=uev: e.tensor_copy(out=utail[:, j, :], in_=uev[:, 0, L:L + HP]),
                             reads=[ueB], writes=[utB[j]])
                        if last_prompt:
                            S.op("pool", lambda e, j=j, uev=uev: e.tensor_copy(out=u32[:, j, 0:HP], in_=uev[:, 0, L:L + HP]),
                                 reads=[ueB], writes=[u32B[j], repB])
                    else:
                        S.op("pool", lambda e, j=j, uev=uev: e.tensor_copy(out=uev[:, :, 0:HP], in_=uhist[:, j, :, :]),
                             reads=cfB[4:8], writes=[ueB])
                        S.op("pool", lambda e, j=j, uev=uev: e.tensor_copy(
                            out=u32[:, j, 0:NT].rearrange("p (t n) -> p n t", n=NS), in_=uev[:, :, HP:HP + L]),
                            reads=[ueB], writes=[u32B[j], repB])
                    prev, prevB = uev, ueB
                    d = 1
                    pp_bufs = []
                    lvl = 0
                    while d < w:
                        lo = 2 * d - 1
                        if lvl < 2:
                            pp_bufs.append(get_tmpp())
                        nx, nxB = pp_bufs[lvl % 2]
                        lvl += 1
                        nxv = nx[:, 0:nseq * UW].rearrange("p (n w) -> p n w", w=UW)
                        S.op("pool" if sample else "dve", lambda e, prev=prev, nxv=nxv, lo=lo, d=d: e.tensor_tensor(
                            out=nxv[:, :, lo:UW], in0=prev[:, :, lo:UW], in1=prev[:, :, lo - d:UW - d], op=ALU.add),
                            reads=[prevB], writes=[nxB])
                        prev, prevB = nxv, nxB
                        d *= 2
                    S.op("dve", lambda e, j=j, prev=prev, uev=uev, w=w: e.scalar_tensor_tensor(
                        out=v3(pooled[:, j, 0:NT]), in0=prev[:, :, HP:HP + L], scalar=1.0 / w, in1=uev[:, :, HP:HP + L],
                        op0=ALU.mult, op1=ALU.subtract), reads=[prevB, ueB], writes=[plB[j]])
                    if first:
                        fx, fxB = get_tmp()
                        S.op("dve", lambda e, prev=prev, fx=fx, g=g: e.tensor_tensor(
                            out=fx[:, 0:HP], in0=prev[:, 0, HP:2 * HP], in1=invc[:, g, 0:HP], op=ALU.mult),
                            reads=[prevB, invB], writes=[fxB])
                        S.op("dve", lambda e, j=j, fx=fx, uev=uev: e.tensor_tensor(
                            out=pooled[:, j, 0:HP], in0=fx[:, 0:HP], in1=uev[:, 0, HP:2 * HP], op=ALU.subtract),
                            reads=[fxB, ueB], writes=[plB[j]])


            def phase_A2():
                slotV = slotVB = slotG = slotGB = None
                for j in range(NCH):
                    if j % 4 == 0:
                        slotV, slotVB, _, _ = use_block()
                        slotG, slotGB, _, _ = use_block(hold_prev=True)
                    jj = j % 4
                    pv, pvB = get_ps()
                    pg, pgB = get_ps()
                    for k in range(NCH):
                        S.op("pe", lambda e, jj=jj, k=k, pv=pv, slotV=slotV: e.matmul(
                            pv[:, 0:NT], lhsT=slotV[:, k * 512 + jj * 128:k * 512 + (jj + 1) * 128], rhs=h[:, k, 0:NT],
                            start=(k == 0), stop=(k == NCH - 1)), reads=[slotVB, hB[k]], writes=[pvB])
                    for k in range(NCH):
                        S.op("pe", lambda e, jj=jj, k=k, pg=pg, slotG=slotG: e.matmul(
                            pg[:, 0:NT], lhsT=slotG[:, k * 512 + jj * 128:k * 512 + (jj + 1) * 128], rhs=h[:, k, 0:NT],
                            start=(k == 0), stop=(k == NCH - 1)), reads=[slotGB, hB[k]], writes=[pgB])
                    t_, tB_ = get_tmp()
                    S.op("act", lambda e, pg=pg, t_=t_: e.activation(out=t_[:, 0:NT], in_=pg[:, 0:NT], func=AF.Sigmoid),
                         reads=[pgB], writes=[tB_])
                    S.op("dve", lambda e, j=j, pv=pv, t_=t_: e.tensor_tensor(
                        out=aview(j)[:, :, HC:HC + L], in0=v3(pv[:, 0:NT]), in1=v3(t_[:, 0:NT]), op=ALU.mult),
                        reads=[pvB, tB_], writes=[aB[j]])
                    if n32 and not sample:
                        S.op("dve", lambda e, j=j, pv=pv, t_=t_: e.tensor_tensor(
                            out=a32[:, j, 0:n32], in0=pv[:, NT - n32:NT], in1=t_[:, NT - n32:NT], op=ALU.mult),
                            reads=[pvB, tB_], writes=[a32B[j], repB])
                    if sample:
                        S.op("dve", lambda e, j=j, pv=pv, t_=t_: e.tensor_tensor(
                            out=a32[:, j, 0:NT].rearrange("p (t n) -> p n t", n=NS), in0=v3(pv[:, 0:NT]), in1=v3(t_[:, 0:NT]),
                            op=ALU.mult), reads=[pvB, tB_], writes=[a32B[j], repB])

                if "after_S2" in hooks:
                    hooks["after_S2"]()
                if prev_norm2[0] is not None:
                    for _ in prev_norm2[0]():
                        pass

                if tpe > 0:
                    for j in range(NCH):
                        pc, pcB = get_ps()
                        for k in range(tpe):
                            dg, dgB_ = get_dg()
                            S.op("dve", lambda e, j=j, k=k, dg=dg: e.tensor_scalar(
                                out=dg, in0=identb[:, :], scalar1=cw[:, j, k:k + 1], scalar2=None, op0=ALU.mult),
                                reads=[identbB, cwB], writes=[dgB_])
                            S.op("pe", lambda e, j=j, k=k, pc=pc, dg=dg: e.matmul(
                                pc[:, 0:NT], lhsT=dg, rhs=aext[:, j, k:k + L],
                                start=(k == 0), stop=(k == tpe - 1)), reads=[dgB_, aB[j]], writes=[pcB])
                        S.op("act", lambda e, j=j, pc=pc: e.activation(
                            out=cf[:, j, 0:NT], in_=pc[:, 0:NT], func=AF.Identity, bias=vecs[:, CONV_B, j:j + 1], scale=1.0),
                            reads=[pcB, vecB], writes=[cfB[j]])


                if "after_convPE" in hooks:
                    hooks["after_convPE"]()
                if sample:
                    stage_S3()

            def conv_gen():
                for k in range(tpe, CW):
                    for j in range(NCH):
                        if k == 0:
                            S.op("dve", lambda e, j=j: e.tensor_scalar(
                                out=v3(cf[:, j, 0:NT]), in0=aview(j)[:, :, 0:L], scalar1=cw[:, j, 0:1],
                                scalar2=vecs[:, CONV_B, j:j + 1], op0=ALU.mult, op1=ALU.add),
                                reads=[aB[j], cwB, vecB], writes=[cfB[j]])
                        else:
                            S.op("dve", lambda e, j=j, k=k: e.scalar_tensor_tensor(
                                out=v3(cf[:, j, 0:NT]), in0=aview(j)[:, :, k:k + L], scalar=cw[:, j, k:k + 1],
                                in1=v3(cf[:, j, 0:NT]), op0=ALU.mult, op1=ALU.add),
                                reads=[aB[j], cwB, cfB[j]], writes=[cfB[j]])
                    yield
                if not sample and not last_prompt:
                    for j in range(NCH):
                        S.op("pool", lambda e, j=j: e.tensor_copy(out=aext[:, j, 0:HC], in_=aext[:, j, L:L + HC]),
                             reads=[aB[j]], writes=[aB[j]])

            def phase_B():
                if sample:
                    for gi in range(2):
                        S.op("sp", lambda e, gi=gi: e.dma_start(out=gbc[0:64, gi, :], in_=gscr[gi]),
                             reads=[gscrB[gi]], writes=[gbcB[gi]], dma_key=("gsi", gi))
                for q in range(nsub):
                    src = xp[ti * 512 + q * 128: ti * 512 + (q + 1) * 128, :] if not sample else xs
                    S.op("act", lambda e, q=q, src=src: e.dma_start(out=xres[0:R, q, :], in_=src),
                         writes=[xresB[q]], dma_key=("x", q))
                if last_prompt:
                    fm_rows_out(a32, a32B, HC, ncp, ("o", "ncp"))
                def ln_copy(j):
                    S.op("pool", lambda e, j=j: e.tensor_copy(out=s_t[:, j, 0:NT], in_=cf[:, j, 0:NT]),
                         reads=[cfB[j]], writes=[sB[j]])
                    S.op("pool", lambda e, j=j: e.tensor_tensor(out=m_t[:, j, 0:NT], in0=cf[:, j, 0:NT], in1=cf[:, j, 0:NT],
                                                                op=ALU.mult),
                         reads=[cfB[j]], writes=[mB[j]])
                for j in range(NCH):
                    ln_copy(j)
                a1_ = None
                if next_A1[0] is not None:
                    a1_ = next_A1[0]()
                    next(a1_)
                if not sample:
                    stage_S3()
                if last_prompt:
                    fm_rows_out(u32, u32B, HP, npp, ("o", "npp"))
                if "after_S3" in hooks:
                    hooks["after_S3"]()

                for gi in range(2):
                    slotX = slotXB = None
                    for j in range(NCH):
                        if j % 4 == 0:
                            slotX, slotXB, _, _ = use_block()
                        jj = j % 4
                        pq, pqB = get_ps()
                        for k in range(NCH):
                            S.op("pe", lambda e, jj=jj, k=k, pq=pq, slotX=slotX: e.matmul(
                                pq[:, 0:NT], lhsT=slotX[:, k * 512 + jj * 128:k * 512 + (jj + 1) * 128], rhs=h[:, k, 0:NT],
                                start=(k == 0), stop=(k == NCH - 1)), reads=[slotXB, hB[k]], writes=[pqB])
                        ci = gi * 8 + j
                        S.op("act", lambda e, ci=ci, pq=pq: e.activation(
                            out=sg32[:, ci * 512:ci * 512 + NT], in_=pq[:, 0:NT], func=AF.Sigmoid),
                            reads=[pqB], writes=[sgB[ci]])

                if a1_ is not None:
                    for _ in a1_:
                        pass
                if "after_S5" in hooks:
                    hooks["after_S5"]()

                pmean, pmeanB = get_ps()
                pe2, pe2B = get_ps()
                for j in range(NCH):
                    S.op("pe", lambda e, j=j: e.matmul(pmean[:, 0:NT], lhsT=onesD[:, :], rhs=s_t[:, j, 0:NT],
                                                       start=(j == 0), stop=(j == NCH - 1)),
                         reads=[onesB, sB[j]], writes=[pmeanB])
                    S.op("pe", lambda e, j=j: e.matmul(pe2[:, 0:NT], lhsT=onesD[:, :], rhs=m_t[:, j, 0:NT],
                                                       start=(j == 0), stop=(j == NCH - 1)),
                         reads=[onesB, mB[j]], writes=[pe2B])

                slots7 = {}

                def emit_ob(j):
                    slotPO, slotPOB = slots7["po"]
                    jj = j % 4
                    pb_, pbB_ = get_ps()
                    for k in range(NCH):
                        S.op("pe", lambda e, jj=jj, k=k, pb_=pb_: e.matmul(
                            pb_[:, 0:NT], lhsT=slotPO[:, k * 512 + jj * 128:k * 512 + (jj + 1) * 128], rhs=pooled[:, k, 0:NT],
                            start=(k == 0), stop=(k == NCH - 1)), reads=[slotPOB, plB[k]], writes=[pbB_])
                    t2, t2B = get_tmp()
                    S.op("dve", lambda e, j=j, pb_=pb_, t2=t2: e.tensor_tensor(
                        out=t2[:, 0:NT], in0=pb_[:, 0:NT], in1=sg32[:, (8 + j) * 512:(8 + j) * 512 + NT], op=ALU.mult),
                        reads=[pbB_, sgB[8 + j]], writes=[t2B])
                    return t2, t2B

                msq, msqB = get_tmp()
                rstd, rstdB = lnst[:, 0, :], lnB[0]
                nmr, nmrB = lnst[:, 1, :], lnB[1]
                S.op("act", lambda e: e.activation(out=msq[:, 0:NT], in_=pmean[:, 0:NT], func=AF.Square),
                     reads=[pmeanB], writes=[msqB])
                S.op("dve", lambda e: e.tensor_tensor(out=msq[:, 0:NT], in0=pe2[:, 0:NT], in1=msq[:, 0:NT], op=ALU.subtract),
                     reads=[pe2B, msqB], writes=[msqB])
                S.op("act", lambda e: e.activation(out=rstd[:, 0:NT], in_=msq[:, 0:NT], func=AF.Sqrt, bias=epst[:, 0:1], scale=1.0),
                     reads=[msqB, epsB], writes=[rstdB])
                S.op("dve", lambda e: e.reciprocal(out=rstd[:, 0:NT], in_=rstd[:, 0:NT]), reads=[rstdB], writes=[rstdB])
                S.op("dve", lambda e: e.scalar_tensor_tensor(out=nmr[:, 0:NT], in0=pmean[:, 0:NT], scalar=-1.0, in1=rstd[:, 0:NT],
                                                             op0=ALU.mult, op1=ALU.mult),
                     reads=[pmeanB, rstdB], writes=[nmrB])
                for j in range(NCH):
                    z, zB = get_tmp()
                    S.op("dve", lambda e, j=j, z=z: e.tensor_tensor(out=z[:, 0:NT], in0=cf[:, j, 0:NT], in1=rstd[:, 0:NT], op=ALU.mult),
                         reads=[cfB[j], rstdB], writes=[zB])
                    S.op("dve", lambda e, z=z: e.tensor_tensor(out=z[:, 0:NT], in0=z[:, 0:NT], in1=nmr[:, 0:NT], op=ALU.add),
                         reads=[zB, nmrB], writes=[zB])
                    S.op("act", lambda e, j=j, z=z: e.activation(
                        out=s_t[:, j, 0:NT], in_=z[:, 0:NT], func=AF.Silu, bias=vecs[:, LN_B, j:j + 1], scale=vecs[:, LN_G, j:j + 1]),
                        reads=[zB, vecB], writes=[sB[j]])

                for g in range(4):
                    pps = []
                    for jj in range(2):
                        pp, ppB = get_ps()
                        pps.append((pp, ppB))
                        for kk in range(2):
                            S.op("pe", lambda e, g=g, jj=jj, kk=kk, pp=pp: e.matmul(
                                pp[:, 0:NT], lhsT=pmx[:, g, kk, jj * 128:(jj + 1) * 128], rhs=pooled[:, 2 * g + kk, 0:NT],
                                start=(kk == 0), stop=(kk == 1)), reads=[pmxB, plB[2 * g + kk]], writes=[ppB])
                    for jj in range(2):
                        pp, ppB = pps[jj]
                        j = 2 * g + jj
                        S.op("act", lambda e, j=j, pp=pp: e.activation(
                            out=pooled[:, j, 0:NT], in_=pp[:, 0:NT], func=AF.Copy, scale=vecs[:, PSC, j:j + 1]),
                            reads=[ppB, vecB], writes=[plB[j]])

                for j in range(NCH):
                    if j % 4 == 0:
                        a_, b_, _, _ = use_block(); slots7["po"] = (a_, b_)
                        a_, b_, _, _ = use_block(hold_prev=True); slots7["co"] = (a_, b_)
                    slotCO, slotCOB = slots7["co"]
                    jj = j % 4
                    t2, t2B = emit_ob(j)
                    pa_, paB_ = get_ps()
                    for k in range(NCH):
                        S.op("pe", lambda e, jj=jj, k=k, pa_=pa_, slotCO=slotCO: e.matmul(
                            pa_[:, 0:NT], lhsT=slotCO[:, k * 512 + jj * 128:k * 512 + (jj + 1) * 128], rhs=s_t[:, k, 0:NT],
                            start=(k == 0), stop=(k == NCH - 1)), reads=[slotCOB, sB[k]], writes=[paB_])
                    t1, t1B = get_tmp()
                    S.op("dve", lambda e, j=j, pa_=pa_, t1=t1: e.tensor_tensor(
                        out=t1[:, 0:NT], in0=pa_[:, 0:NT], in1=sg32[:, j * 512:j * 512 + NT], op=ALU.mult),
                        reads=[paB_, sgB[j]], writes=[t1B])
                    S.op("dve", lambda e, j=j, t1=t1, t2=t2: e.tensor_tensor(
                        out=m_t[:, j, 0:NT], in0=t1[:, 0:NT], in1=t2[:, 0:NT], op=ALU.add),
                        reads=[t1B, t2B], writes=[mB[j]])

                for f in range(2):
                    slotO, slotOB, _, _ = use_block()
                    for q in range(nsub):
                        po, poB = get_ps()
                        for k in range(NCH):
                            S.op("pe", lambda e, q=q, k=k, po=po, slotO=slotO: e.matmul(
                                po[0:R, :], lhsT=m_t[:, k, q * 128:q * 128 + R], rhs=slotO[:, k * 512:(k + 1) * 512],
                                start=(k == 0), stop=(k == NCH - 1)), reads=[slotOB, mB[k]], writes=[poB])
                        t_, tB_ = get_tmp()
                        S.op("dve", lambda e, f=f, po=po, t_=t_: e.tensor_tensor(
                            out=t_[0:R, 0:512], in0=po[0:R, :], in1=grow(0, f * 512, (f + 1) * 512), op=ALU.mult),
                            reads=[poB, growB[0]], writes=[tB_])
                        S.op("pool", lambda e, q=q, f=f, t_=t_: e.tensor_tensor(
                            out=xres[0:R, q, f * 512:(f + 1) * 512], in0=xres[0:R, q, f * 512:(f + 1) * 512], in1=t_[0:R, 0:512], op=ALU.add),
                            reads=[tB_, xresB[q]], writes=[xresB[q]])
                n2_ = norm2()
                if not last_prompt:
                    next(n2_)
                if defer_norm2[0] is None:
                    for _ in n2_:
                        pass
                else:
                    defer_norm2[0] = n2_


            next_A1 = [None]
            hooks = {}
            defer_norm2 = [None]
            prev_norm2 = [None]

            def norm2():
                return norm_gen([(lambda q=q: (xres[0:R, q, :], xresB[q])) for q in range(nsub)], R, s_t, sB, SC2, SH2, sample)

            def r2deps(j):
                return [sgB[j // 2]]

            def F_gen():
                for jb in range(8):
                    slotF, slotFB, _, _ = use_block()
                    for jj in range(4):
                        j = jb * 4 + jj
                        pf, pfB = get_ps()
                        for k in range(NCH):
                            S.op("pe", lambda e, jj=jj, k=k, pf=pf, slotF=slotF: e.matmul(
                                pf[:, 0:NT], lhsT=slotF[:, k * 512 + jj * 128:k * 512 + (jj + 1) * 128], rhs=s_t[:, k, 0:NT],
                                start=(k == 0), stop=(k == NCH - 1)), reads=[slotFB, sB[k]], writes=[pfB])
                        S.op("act", lambda e, pf=pf: e.activation(out=pf[:, 0:NT], in_=pf[:, 0:NT], func=AF.Relu),
                             reads=[pfB], writes=[pfB])
                        S.op("act", lambda e, j=j, pf=pf: e.activation(
                            out=r2f[:, j * 512:j * 512 + NT], in_=pf[:, 0:NT], func=AF.Square),
                            reads=[pfB], writes=r2deps(j))
                    yield

                for fq in range(4):
                    pws = [get_ps() for _ in range(nsub)]
                    for kh in range(2):
                        slotW, slotWB, _, _ = use_block()
                        for q in range(nsub):
                            pw, pwB = pws[q]
                            for kk in range(16):
                                k = kh * 16 + kk
                                S.op("pe", lambda e, q=q, k=k, kk=kk, pw=pw, slotW=slotW: e.matmul(
                                    pw[0:R, 0:256], lhsT=r2f[:, k * 512 + q * 128:k * 512 + q * 128 + R],
                                    rhs=slotW[:, kk * 256:(kk + 1) * 256],
                                    start=(k == 0), stop=(k == 31)), reads=[slotWB] + r2deps(k), writes=[pwB])
                    for q in range(nsub):
                        pw, pwB = pws[q]
                        t_, tB_ = get_tmp()
                        S.op("dve", lambda e, fq=fq, pw=pw, t_=t_: e.tensor_tensor(
                            out=t_[0:R, 0:256], in0=pw[0:R, 0:256], in1=grow(1, fq * 256, (fq + 1) * 256), op=ALU.mult),
                            reads=[pwB, growB[1]], writes=[tB_])
                        S.op("pool", lambda e, q=q, fq=fq, t_=t_: e.tensor_tensor(
                            out=xres[0:R, q, fq * 256:(fq + 1) * 256], in0=xres[0:R, q, fq * 256:(fq + 1) * 256],
                            in1=t_[0:R, 0:256], op=ALU.add), reads=[tB_, xresB[q]], writes=[xresB[q]])
                    yield

                for q in range(nsub):
                    yo, yoB = get_stg()
                    rstd_ = rms_rstd(xres[0:R, q, :], xresB[q], R, yo, yoB)
                    S.op("dve", lambda e, q=q, yo=yo, rstd_=rstd_: e.scalar_tensor_tensor(
                        out=yo[0:R, :], in0=xres[0:R, q, :], scalar=rstd_, in1=fgbc[0:R, :], op0=ALU.mult, op1=ALU.mult),
                        reads=[xresB[q], statB, fgB], writes=[yoB])
                    if not sample:
                        dst = yp[ti * 512 + q * 128: ti * 512 + (q + 1) * 128, :]
                    else:
                        dst = ys
                    S.op("sp", lambda e, yo=yo, dst=dst: e.dma_start(out=dst, in_=yo[0:R, :]), reads=[yoB],
                         dma_key=("y", ti, q), out_dma=True)


                if sample:
                    for t in range(LS):
                        fm_rows_out(a32, a32B, NS, ncs.rearrange("(n j) d -> n j d", j=HC)[:, HC - LS + t, :],
                                    ("o", "ncs1", t), col0=t * NS)
                        fm_rows_out(u32, u32B, NS, nps.rearrange("(n j) d -> n j d", j=HP)[:, HP - LS + t, :],
                                    ("o", "nps1", t), col0=t * NS)
                yield

            return phase_A1, phase_A2, conv_gen, phase_B, F_gen, next_A1, hooks, defer_norm2, prev_norm2, norm2

        tiles = [make_tile(ti) for ti in range(NTILES)]

        def drain(g):
            for _ in g:
                pass

        for ti in range(NTILES - 2):
            tiles[ti][5][0] = tiles[ti + 1][0]
        for ti in range(NTILES - 1):
            tiles[ti][7][0] = True
            tiles[ti + 1][8][0] = (lambda ti=ti: tiles[ti][7][0])
        tiles[0][6]["after_S2"] = lambda: mod_v(2)
        tiles[0][6]["after_convPE"] = lambda: mod_v(3)
        tiles[0][6]["after_S3"] = lambda: mod_v(4)
        tiles[0][6]["after_S5"] = lambda: mod_v(5)
        drain(tiles[0][0]())
        tiles[0][1]()
        drain(tiles[0][2]())
        tiles[0][3]()
        for ti in range(1, NTILES):
            if ti == NTILES - 1:
                drain(tiles[ti][0]())
            tiles[ti][1]()
            cg = tiles[ti][2]()
            fg_ = tiles[ti - 1][4]()
            step = 0
            n_rounds = CW - (T_PE if ti < 4 else 0)
            done_r = 0
            for _ in fg_:
                step += 1
                tgt = min(n_rounds, (n_rounds * step + 10) // 11)
                while done_r < tgt:
                    next(cg, None)
                    done_r += 1
            drain(cg)
            tiles[ti][3]()
        drain(tiles[NTILES - 1][4]())

        S.emit(st)
    return nc


_NC_CACHE = {}


def kernel(x_prompt, x_sample, state_conv, state_pool, c_prompt, c_sample, w_ada, b_ada, w_in,
           conv_w, conv_b, ln_g, ln_b, w_conv_out, pool_mix, pool_scale, w_pool_out, w_out,
           w_ff1, w_ff2, final_g):
    f = lambda a: np.ascontiguousarray(np.asarray(a, dtype=np.float32))
    x_prompt, x_sample, state_conv, state_pool = f(x_prompt), f(x_sample), f(state_conv), f(state_pool)
    c_prompt, c_sample = f(c_prompt), f(c_sample)
    if "nc" not in _NC_CACHE:
        _NC_CACHE["nc"] = build_nc()
    nc = _NC_CACHE["nc"]

    def fm(vec):
        return f(np.asarray(vec).reshape(NCH, 128).T)

    shared = {
        "w_ada": f(w_ada[0]), "b_ada": f(np.asarray(b_ada[0]).reshape(1, 6 * D)),
        "b_adaT": f(np.asarray(b_ada[0]).reshape(48, 128).T),
        "w_in": f(w_in[0]),
        "cwT": f(np.asarray(conv_w[0]).T.reshape(NCH, 128, CW).transpose(1, 0, 2).reshape(128, NCH * CW)),
        "vecT": f(np.concatenate([fm(conv_b[0]), fm(ln_g[0]), fm(ln_b[0]), fm(pool_scale[0])], axis=1)),
        "w_co": f(w_conv_out[0]), "pmix": f(pool_mix[0]), "w_po": f(w_pool_out[0]), "w_o": f(w_out[0]),
        "w_f1": f(w_ff1[0]), "w_f2": f(w_ff2[0]), "fg": f(np.asarray(final_g).reshape(1, D)),
    }
    in_maps = []
    for i in range(NCORES):
        m = dict(shared)
        m["xp"] = x_prompt[i]
        m["xs"] = x_sample[i * NS:(i + 1) * NS].reshape(NS * LS, D)
        m["sconv"] = state_conv[0, i * NS:(i + 1) * NS].reshape(NS * HC, D)
        m["spool"] = state_pool[0, i * NS:(i + 1) * NS].reshape(NS * HP, D)
        m["cvec"] = f(np.concatenate([c_prompt[i:i + 1], c_sample[i * NS:(i + 1) * NS]], axis=0))
        in_maps.append(m)
    res = run_bass_kernel_spmd(nc, in_maps, core_ids=list(range(NCORES)))
    rs = res.results
    y_prompt = np.stack([rs[i]["yp"] for i in range(NCORES)], axis=0)
    y_sample = np.concatenate([rs[i]["ys"].reshape(NS, LS, D) for i in range(NCORES)], axis=0)
    ncp_ = np.stack([rs[i]["ncp"] for i in range(NCORES)], axis=0)[None]
    npp_ = np.stack([rs[i]["npp"] for i in range(NCORES)], axis=0)[None]
    ncs_ = np.concatenate([rs[i]["ncs"].reshape(NS, HC, D) for i in range(NCORES)], axis=0)[None]
    nps_ = np.concatenate([rs[i]["nps"].reshape(NS, HP, D) for i in range(NCORES)], axis=0)[None]
    return (y_prompt.astype(np.float32), y_sample.astype(np.float32), ncp_.astype(np.float32),
            npp_.astype(np.float32), ncs_.astype(np.float32), nps_.astype(np.float32))
```
